# Optimizing a Trainium2 kernel written in Bass

```python
import math
import jax, jax.numpy as jnp
from jax import lax
import numpy as np

D_MODEL = 1024
BATCH = 8
SEQ = 8192
DEPTH = 4

N_MIXERS = 3
EPS = 1e-6
N_GLA = (DEPTH + 2) // 3
N_SGU = (DEPTH + 1) // 3
N_FOX = DEPTH // 3

GLA_HEADS = 4
GLA_DK = D_MODEL // (2 * GLA_HEADS)
GLA_DV = D_MODEL // GLA_HEADS
GLA_KD = GLA_HEADS * GLA_DK
GLA_VD = GLA_HEADS * GLA_DV
GLA_RANK = 16
GLA_NORMALIZER = 16.0
GLA_CHUNK = 64
GLA_IN = 2 * GLA_KD + 2 * GLA_VD + GLA_RANK

SGU_WIDTH = D_MODEL
SGU_GROUPS = 4
SGU_GDIM = SGU_WIDTH // SGU_GROUPS
SGU_CHUNK = 128
SGU_IN = 3 * SGU_WIDTH

FOX_HEADS = 16
FOX_DH = 64
FOX_AD = FOX_HEADS * FOX_DH
FOX_BLOCK = 128
FOX_IN = 4 * FOX_AD + FOX_HEADS
FORGET_BIAS_INIT = 2.0

kernel_name = "hybrid_gla_sgu_fox_trunk"


def rms_norm(x, g):
    xf = x.astype(jnp.float32)
    xf = xf * lax.rsqrt(jnp.mean(xf * xf, axis=-1, keepdims=True) + EPS)
    return xf.astype(x.dtype) * g


def layer_norm(x, g, b):
    xf = x.astype(jnp.float32)
    mu = jnp.mean(xf, axis=-1, keepdims=True)
    var = jnp.mean(jnp.square(xf - mu), axis=-1, keepdims=True)
    return ((xf - mu) * lax.rsqrt(var + EPS)).astype(x.dtype) * g + b


def gla_mixer(h, w_in, w_a2, b_a, g_head, w_out):
    B, S, _ = h.shape
    H, DK, DV, C = GLA_HEADS, GLA_DK, GLA_DV, GLA_CHUNK
    N = S // C
    q, k, v, z, a_low = jnp.split(h @ w_in, [GLA_KD, 2 * GLA_KD, 2 * GLA_KD + GLA_VD,
                                             2 * GLA_KD + 2 * GLA_VD], axis=-1)
    log_a = jax.nn.log_sigmoid((a_low @ w_a2 + b_a).astype(jnp.float32)) / GLA_NORMALIZER
    q = q.astype(jnp.float32) * (DK ** -0.5)

    def to_chunks(t, d):
        return t.astype(jnp.float32).reshape(B, N, C, H, d).transpose(1, 0, 3, 2, 4)

    qc, kc, vc, gc = to_chunks(q, DK), to_chunks(k, DK), to_chunks(v, DV), to_chunks(log_a, DK)
    causal = jnp.tril(jnp.ones((C, C), dtype=bool))

    def step(state, xs):
        qb, kb, vb, gb = xs
        b = jnp.cumsum(gb, axis=2)
        o_inter = jnp.einsum('bhik,bhkv->bhiv', qb * jnp.exp(b), state)
        diff = b[:, :, :, None, :] - b[:, :, None, :, :]
        decay = jnp.exp(jnp.where(causal[:, :, None], diff, -jnp.inf))
        attn = jnp.einsum('bhik,bhjk,bhijk->bhij', qb, kb, decay)
        o = o_inter + jnp.einsum('bhij,bhjv->bhiv', attn, vb)
        b_last = b[:, :, -1:, :]
        state = (jnp.exp(b_last[:, :, 0, :])[..., None] * state
                 + jnp.einsum('bhjk,bhjv->bhkv', kb * jnp.exp(b_last - b), vb))
        return state, o

    state0 = jnp.zeros((B, H, DK, DV), jnp.float32)
    _, o = lax.scan(step, state0, (qc, kc, vc, gc))
    o = o.transpose(1, 0, 3, 2, 4).reshape(B, S, H, DV)
    o = rms_norm(o, g_head).reshape(B, S, GLA_VD).astype(h.dtype)
    return (o * jax.nn.silu(z)) @ w_out


def sgu_mixer(h, w_in, ln_g, ln_b, w_s, b_s, w_out):
    B, S, _ = h.shape
    C, G, Dg = SGU_CHUNK, SGU_GROUPS, SGU_GDIM
    N = S // C
    u, v, z = jnp.split(h @ w_in, 3, axis=-1)
    u = jax.nn.gelu(u)
    v = layer_norm(jax.nn.gelu(v), ln_g, ln_b)
    w_causal = jnp.where(jnp.tril(jnp.ones((C, C), dtype=bool))[None], w_s, 0.0)
    vc = v.reshape(B, N, C, G, Dg)
    mixed = jnp.einsum('gts,bnsgd->bntgd', w_causal, vc) + b_s.T[None, None, :, :, None]
    sg = u * mixed.reshape(B, S, SGU_WIDTH)
    return (sg * jax.nn.silu(z)) @ w_out


def fox_mixer(h, w_in, b_f, g_q, g_k, w_out):
    B, S, _ = h.shape
    H, Dh, BLK = FOX_HEADS, FOX_DH, FOX_BLOCK
    q, k, v, z, f_logit = jnp.split(h @ w_in, [FOX_AD, 2 * FOX_AD, 3 * FOX_AD, 4 * FOX_AD], axis=-1)
    q = rms_norm(q.reshape(B, S, H, Dh), g_q).transpose(0, 2, 1, 3)
    k = rms_norm(k.reshape(B, S, H, Dh), g_k).transpose(0, 2, 1, 3)
    v = v.reshape(B, S, H, Dh).transpose(0, 2, 1, 3)
    log_f = jax.nn.log_sigmoid((f_logit + b_f).astype(jnp.float32))
    F = jnp.cumsum(log_f, axis=1).transpose(0, 2, 1)
    scale = Dh ** -0.5
    outs = []
    for i in range(S // BLK):
        lo, hi = i * BLK, (i + 1) * BLK
        s = jnp.einsum('bhqd,bhkd->bhqk', q[:, :, lo:hi], k[:, :, :hi]).astype(jnp.float32) * scale
        s = s + F[:, :, lo:hi, None] - F[:, :, None, :hi]
        causal = (lo + jnp.arange(BLK))[:, None] >= jnp.arange(hi)[None, :]
        p = jax.nn.softmax(jnp.where(causal, s, -jnp.inf), axis=-1).astype(v.dtype)
        outs.append(jnp.einsum('bhqk,bhkd->bhqd', p, v[:, :, :hi]))
    o = jnp.concatenate(outs, axis=2).transpose(0, 2, 1, 3).reshape(B, S, FOX_AD)
    return (o * jax.nn.silu(z)) @ w_out


def setup_inputs(seed: int = 0) -> dict:
    key = jax.random.key(seed)
    ks = jax.random.split(key, 32)
    D = D_MODEL
    nrm = lambda k, shape, s: jax.random.normal(k, shape, jnp.float32) * s
    return {
        "x": nrm(ks[0], (BATCH, SEQ, D), 1.0),
        "c": nrm(ks[1], (BATCH, D), 1.0),
        "norm_pre_g": 1.0 + nrm(ks[2], (DEPTH, D), 0.1),
        "norm_post_g": 1.0 + nrm(ks[3], (DEPTH, D), 0.1),
        "w_mod": nrm(ks[4], (DEPTH, D, 3 * D), D ** -0.5),
        "b_mod": nrm(ks[5], (DEPTH, 3 * D), 0.01),
        "gla_w_in": nrm(ks[6], (N_GLA, D, GLA_IN), D ** -0.5),
        "gla_w_a2": nrm(ks[7], (N_GLA, GLA_RANK, GLA_KD), GLA_RANK ** -0.5),
        "gla_b_a": nrm(ks[8], (N_GLA, GLA_KD), 0.1),
        "gla_g_head": 1.0 + nrm(ks[9], (N_GLA, GLA_HEADS, GLA_DV), 0.1),
        "gla_w_out": nrm(ks[10], (N_GLA, GLA_VD, D), GLA_VD ** -0.5),
        "sgu_w_in": nrm(ks[11], (N_SGU, D, SGU_IN), D ** -0.5),
        "sgu_ln_g": 1.0 + nrm(ks[12], (N_SGU, SGU_WIDTH), 0.1),
        "sgu_ln_b": nrm(ks[13], (N_SGU, SGU_WIDTH), 0.1),
        "sgu_w_s": nrm(ks[14], (N_SGU, SGU_GROUPS, SGU_CHUNK, SGU_CHUNK), SGU_CHUNK ** -0.5),
        "sgu_b_s": 1.0 + nrm(ks[15], (N_SGU, SGU_GROUPS, SGU_CHUNK), 0.1),
        "sgu_w_out": nrm(ks[16], (N_SGU, SGU_WIDTH, D), SGU_WIDTH ** -0.5),
        "fox_w_in": nrm(ks[17], (N_FOX, D, FOX_IN), D ** -0.5),
        "fox_b_f": FORGET_BIAS_INIT + nrm(ks[18], (N_FOX, FOX_HEADS), 0.1),
        "fox_g_q": 1.0 + nrm(ks[19], (N_FOX, FOX_DH), 0.1),
        "fox_g_k": 1.0 + nrm(ks[20], (N_FOX, FOX_DH), 0.1),
        "fox_w_out": nrm(ks[21], (N_FOX, FOX_AD, D), FOX_AD ** -0.5),
    }


def reference(x, c, norm_pre_g, norm_post_g, w_mod, b_mod,
              gla_w_in, gla_w_a2, gla_b_a, gla_g_head, gla_w_out,
              sgu_w_in, sgu_ln_g, sgu_ln_b, sgu_w_s, sgu_b_s, sgu_w_out,
              fox_w_in, fox_b_f, fox_g_q, fox_g_k, fox_w_out):
    cond = jax.nn.silu(c)
    for i in range(DEPTH):
        mod = (cond @ w_mod[i] + b_mod[i])[:, None, :]
        shift, scale, gate = jnp.split(mod, 3, axis=-1)
        h = rms_norm(x, norm_pre_g[i]) * (1.0 + scale) + shift
        kind, j = i % N_MIXERS, i // N_MIXERS
        if kind == 0:
            o = gla_mixer(h, gla_w_in[j], gla_w_a2[j], gla_b_a[j], gla_g_head[j], gla_w_out[j])
        elif kind == 1:
            o = sgu_mixer(h, sgu_w_in[j], sgu_ln_g[j], sgu_ln_b[j], sgu_w_s[j], sgu_b_s[j], sgu_w_out[j])
        else:
            o = fox_mixer(h, fox_w_in[j], fox_b_f[j], fox_g_q[j], fox_g_k[j], fox_w_out[j])
        x = x + gate * rms_norm(o, norm_post_g[i])
    return x
```

```python
import numpy as np
from contextlib import ExitStack
import concourse.bass as bass
import concourse.mybir as mybir
from concourse.bass_utils import run_bass_kernel_spmd

F32 = mybir.dt.float32
BF16 = mybir.dt.bfloat16
AF = mybir.ActivationFunctionType
ALU = mybir.AluOpType

D = 1024
KC = 8
ST = 512
EPS = 1e-6
N_IN = {0: 3088, 1: 3072, 2: 4112}
N_CORES = 8


class Buf:
    __slots__ = ("w", "r", "excl")

    def __init__(self, excl=False):
        self.w = {}
        self.r = {}
        self.excl = excl


class T:
    __slots__ = ("ap", "buf")

    def __init__(self, ap, buf=None):
        self.ap = ap
        self.buf = buf if buf is not None else Buf()

    def __getitem__(self, k):
        return self.ap[k]


def _b(x):
    return x.buf if isinstance(x, T) else x


class _Rec:
    def __init__(self):
        self.call = None

    def __getattr__(self, name):
        def f(*a, **kw):
            assert self.call is None
            self.call = (name, a, kw)
            return None
        return f


def _capture(fn):
    r = _Rec()
    fn(r)
    name, a, kw = r.call
    line = fn.__code__.co_firstlineno

    def replay(eng):
        return getattr(eng, name)(*a, **kw)
    replay.line = line
    return replay


class Sched:
    ENG = ("pe", "act", "dve", "pool", "sp")

    def __init__(self, nc, es):
        self.nc = nc
        self.es = es
        self.q = {e: [] for e in self.ENG}
        self.cnt = {e: 0 for e in self.ENG}
        self.seen = {e: {} for e in self.ENG}
        self.sems = {}
        self.names = {}
        self.dcnt = {}
        for e in self.ENG:
            self.sems[e] = es.enter_context(nc.semaphore("s_" + e))

    def slot(self, name):
        k = "d_" + name
        self.sems[k] = self.es.enter_context(self.nc.semaphore(k))
        self.dcnt[k] = 0
        return k

    @staticmethod
    def _split(reads, writes):
        r2 = [b for b in reads if not _b(b).excl]
        w2 = list(writes) + [b for b in reads if _b(b).excl]
        return r2, w2

    def _waits(self, eng, reads, writes):
        reads, writes = self._split(reads, writes)
        need = {}
        for b in reads:
            for k, v in _b(b).w.items():
                if need.get(k, 0) < v:
                    need[k] = v
        for b in writes:
            bb = _b(b)
            for dct in (bb.w, bb.r):
                for k, v in dct.items():
                    if need.get(k, 0) < v:
                        need[k] = v
        waits = []
        seen = self.seen[eng]
        for k, v in need.items():
            if k in self.dcnt:
                v = self.dcnt[k]
            if k == eng and eng == "pe":
                continue
            if seen.get(k, 0) >= v:
                continue
            seen[k] = v
            waits.append((k, v))
        return waits

    def _record(self, ev, reads, writes):
        reads, writes = self._split(reads, writes)
        k, v = ev
        for b in reads:
            bb = _b(b)
            if bb.r.get(k, 0) < v:
                bb.r[k] = v
        for b in writes:
            bb = _b(b)
            bb.r = {}
            if bb.w.get(k, 0) < v:
                bb.w[k] = v

    def op(self, eng, fn, reads=(), writes=()):
        waits = self._waits(eng, reads, writes)
        self.cnt[eng] += 1
        ev = (eng, self.cnt[eng])
        self.q[eng].append((waits, _capture(fn), (eng, 1)))
        self._record(ev, reads, writes)

    def pe_group(self, fns, reads=(), writes=()):
        waits = self._waits("pe", reads, writes)
        self.cnt["pe"] += 1
        ev = ("pe", self.cnt["pe"])
        n = len(fns)
        for i, fn in enumerate(fns):
            self.q["pe"].append((waits if i == 0 else [], _capture(fn), ("pe", 1) if i == n - 1 else None))
        self._record(ev, reads, writes)

    def dma(self, q, slot, fn, reads=(), writes=()):
        waits = self._waits(q, reads, writes)
        self.dcnt[slot] += 16
        ev = (slot, self.dcnt[slot])
        self.q[q].append((waits, _capture(fn), (slot, 16)))
        self._record(ev, reads, writes)

    def barrier(self):
        for e in self.ENG:
            waits = []
            for k in list(self.ENG) + list(self.dcnt.keys()):
                if k == e:
                    continue
                v = self.cnt[k] if k in self.cnt else self.dcnt[k]
                if v > 0 and self.seen[e].get(k, 0) < v:
                    self.seen[e][k] = v
                    waits.append((k, v))
            if waits:
                self.q[e].append((waits, None, None))

    def emit(self):
        nc = self.nc

        def replay(name, eng):
            for waits, fn, inc in self.q[name]:
                for k, v in waits:
                    eng.wait_ge(self.sems[k], v)
                if fn is None:
                    continue
                ins = fn(eng)
                try:
                    self.names[ins.ins.name] = (name, fn.line)
                except Exception:
                    pass
                if inc is not None:
                    ins.then_inc(self.sems[inc[0]], inc[1])

        with nc.Block() as block:
            @block.sync
            def _(e):
                replay("sp", e)

            @block.tensor
            def _(e):
                replay("pe", e)

            @block.scalar
            def _(e):
                replay("act", e)

            @block.vector
            def _(e):
                replay("dve", e)

            @block.gpsimd
            def _(e):
                replay("pool", e)


class Arena:
    def __init__(self, ap, size):
        self.ap = ap
        self.size = size
        self.off = 0

    def mark(self):
        return self.off

    def reset(self, m):
        self.off = m

    def alloc(self, shape, dtype=F32):
        n = int(np.prod(shape))
        n32 = n if dtype == F32 else (n + 1) // 2
        assert self.off + n32 <= self.size, ("SBUF arena overflow", self.off, n32, self.size)
        v = self.ap[:, self.off:self.off + n32]
        self.off += n32
        if dtype == BF16:
            v = v.bitcast(BF16)
        if len(shape) == 2:
            v = v.rearrange("p (a b) -> p a b", a=shape[0])
        elif len(shape) == 3:
            v = v.rearrange("p (a b c) -> p a b c", a=shape[0], b=shape[1])
        return T(v)


class _Stop(Exception):
    pass


STOP_AT = [None]
STQ = "pool"
DEBUG = [False]


class Prog:
    def dbg(self, name, t):
        if not DEBUG[0]:
            return
        nm = "dbg_%s_%d" % (name, len(self.dbg_names))
        self.dbg_names.append(nm)
        shp = list(t.ap.shape)
        d = self.nc.dram_tensor(nm, shp, t.ap.dtype, kind="ExternalOutput").ap()
        sl = self.sc.slot(nm)
        self.sc.dma("sp", sl, lambda e: e.dma_start(out=d, in_=t.ap), reads=[t])

    def ck(self, n):
        if STOP_AT[0] is not None and n >= STOP_AT[0]:
            raise _Stop()

    def __init__(self, S, layer_ids):
        self.S = S
        self.layer_ids = list(layer_ids)
        self.nST = S // ST
        self.in_names = []
        self.dbg_names = []
        nc = self.nc = bass.Bass("TRN2", target_bir_lowering=False)
        self.dram = {}
        self._din("x", [S, D])
        self._din("ccol", [128, 8])
        self._din("ident", [128, 128])
        self._din("tri", [128, 128])
        self._din("uneg", [128, 128])
        self._din("negmask", [128, 128])
        self._din("blockones", [128, 128])
        self._din("sel", [65, 64])
        for L in self.layer_ids:
            kind = L % 3
            p = "L%d_" % L
            self._din(p + "wmod", [D, 3 * D])
            self._din(p + "bmodc", [128, 24])
            self._din(p + "bmodg", [128, D])
            self._din(p + "gprec", [128, 8])
            self._din(p + "gpost", [128, D])
            self._din(p + "win", [D, N_IN[kind]])
            self._din(p + "wout", [D, D])
            if kind == 0:
                self._din(p + "wa2", [17, 512])
                self._din(p + "ghc", [128, 8])
            if kind == 2:
                self._din(p + "gqk", [128, 2])
                self._din(p + "bf", [16, 1])
            if kind == 1:
                self._din(p + "lng", [128, D])
                self._din(p + "lnb", [128, D])
                self._din(p + "ws", [128, 4, 128])
                self._din(p + "bs", [128, 4, 128])
        self.out = nc.dram_tensor("out", [S, D], F32, kind="ExternalOutput").ap()
        self.xs = [nc.dram_tensor("xs%d" % i, [S, D], F32, kind="Internal").ap() for i in range(2)]
        if any(L % 3 == 2 for L in self.layer_ids):
            self.QA = nc.dram_tensor("fox_qa", [16, 70, S], BF16, kind="Internal").ap()
            self.KA = nc.dram_tensor("fox_ka", [16, 70, S], BF16, kind="Internal").ap()
            self.VS = nc.dram_tensor("fox_v", [S, D], BF16, kind="Internal").ap()
            self.SZ = nc.dram_tensor("fox_sz", [D, S], BF16, kind="Internal").ap()
            self.OG = nc.dram_tensor("fox_og", [D, S], BF16, kind="Internal").ap()

    def _din(self, name, shape, dtype=F32):
        self.in_names.append(name)
        self.dram[name] = self.nc.dram_tensor(name, shape, dtype, kind="ExternalInput").ap()

    def rstd_from_ss(self, ss, out, n_inv, tmp):
        sc = self.sc
        sc.op("act", lambda e: e.activation(out=tmp.ap, in_=ss.ap, func=AF.Ln, scale=n_inv, bias=EPS),
              reads=[ss], writes=[tmp])
        sc.op("act", lambda e: e.activation(out=out.ap, in_=tmp.ap, func=AF.Exp, scale=-0.5),
              reads=[tmp], writes=[out])

    def build(self):
        nc = self.nc
        with ExitStack() as es:
            arena_t = es.enter_context(nc.sbuf_tensor("arena", [128, 53000], F32))
            ps_t = es.enter_context(nc.psum_tensor("ps", [128, 8, 512], F32))
            self.sc = sc = Sched(nc, es)
            self.A = A = Arena(arena_t[:, :], 53000)
            self.ps = ps_t
            self.psbuf = [Buf(excl=True) for _ in range(8)]
            self.slots = {}
            for nm in ["c0", "stg0", "stg1", "x0", "x1", "x2", "x3", "xo0", "xo1", "xo2", "xo3", "xs0", "xs1", "xs2", "xs3", "sm0", "sm1", "sm2", "sm3",
                       "fq0", "fq1", "fv0", "fv1", "fz0", "ff0", "fk0", "fk1", "fvh0", "fvh1", "fqq0", "fqq1", "fqq2",
                       "fsz0", "fsz1", "fog0", "fog1", "fo0", "fo1"]:
                self.slots[nm] = sc.slot(nm)
            self.global_consts()
            x_in = self.dram["x"]
            try:
                for li, L in enumerate(self.layer_ids):
                    x_out = self.out if li == len(self.layer_ids) - 1 else self.xs[li % 2]
                    m = A.mark()
                    self.layer(L, x_in, x_out)
                    sc.barrier()
                    A.reset(m)
                    x_in = x_out
            except _Stop:
                pass
            sc.barrier()
            sc.emit()
        return nc

    def pst(self, bank, n=1):
        if n == 1:
            return self.ps[:, bank, :]
        return self.ps[:, bank:bank + n, :].rearrange("p a b -> p (a b)")

    def global_consts(self):
        sc, A, dr = self.sc, self.A, self.dram
        sl = self.slots["c0"]
        self.identf = A.alloc([128])
        self.identb = A.alloc([128], BF16)
        self.trif = A.alloc([128])
        self.unegf = A.alloc([128])
        self.onesb = A.alloc([128], BF16)
        self.ones = A.alloc([128])
        self.cond = A.alloc([8])
        self.cond_rep = A.alloc([8, 128])
        cc = A.alloc([8])
        sc.dma("sp", sl, lambda e: e.dma_start(out=self.identf.ap, in_=dr["ident"]), writes=[self.identf])
        sc.dma("sp", sl, lambda e: e.dma_start(out=self.trif.ap, in_=dr["tri"]), writes=[self.trif])
        sc.dma("sp", sl, lambda e: e.dma_start(out=self.unegf.ap, in_=dr["uneg"]), writes=[self.unegf])
        sc.dma("sp", sl, lambda e: e.dma_start(out=cc.ap, in_=dr["ccol"]), writes=[cc])
        sc.barrier()
        sc.op("dve", lambda e: e.tensor_copy(out=self.identb.ap, in_=self.identf.ap), reads=[self.identf], writes=[self.identb])
        sc.op("dve", lambda e: e.memset(self.ones.ap, 1.0), writes=[self.ones])
        sc.op("dve", lambda e: e.memset(self.onesb.ap, 1.0), writes=[self.onesb])
        sc.op("act", lambda e: e.activation(out=self.cond.ap, in_=cc.ap, func=AF.Silu), reads=[cc], writes=[self.cond])
        for kc in range(KC):
            sc.op("dve", lambda e, kc=kc: e.tensor_scalar(out=self.cond_rep[:, kc, :], in0=self.ones.ap,
                                                         scalar1=self.cond[:, kc:kc + 1], scalar2=None, op0=ALU.mult),
                  reads=[self.ones, self.cond], writes=[self.cond_rep])

    def prep(self, L, tokmajor_ranges, nocol_ranges=None, wout_rowscale=None):
        sc, A, dr = self.sc, self.A, self.dram
        kind = L % 3
        p = "L%d_" % L
        nin = N_IN[kind]
        BW = 256
        if nocol_ranges is None:
            nocol_ranges = tokmajor_ranges
        self.Win = A.alloc([KC, nin], BF16)
        self.Wout = A.alloc([KC, D], BF16)
        self.Gbc = A.alloc([D])
        nch = (nin + 127) // 128
        self.biascol = A.alloc([nch])
        self.biasrow = {r: A.alloc([r[1] - r[0]]) for r in tokmajor_ranges}
        tmp_mark = A.mark()
        stg = [A.alloc([KC, BW]) for _ in range(2)]
        stg_slot = [self.slots["stg0"], self.slots["stg1"]]
        modc = A.alloc([16])
        acol = A.alloc([8])
        shift_rep = A.alloc([KC, 128])
        small = A.alloc([24 + 8])
        bmodc, gprec = T(small[:, 0:24], small.buf), T(small[:, 24:32], small.buf)
        gtmp = A.alloc([D])
        gpost = A.alloc([D])
        sl = self.slots["c0"]
        sc.dma("sp", sl, lambda e: e.dma_start(out=bmodc.ap, in_=dr[p + "bmodc"]), writes=[small])
        sc.dma("sp", sl, lambda e: e.dma_start(out=gprec.ap, in_=dr[p + "gprec"]), writes=[small])
        sc.dma("sp", sl, lambda e: e.dma_start(out=gtmp.ap, in_=dr[p + "bmodg"]), writes=[gtmp])
        sc.dma("sp", sl, lambda e: e.dma_start(out=gpost.ap, in_=dr[p + "gpost"]), writes=[gpost])
        sc.barrier()
        blk = [0]

        def load_block(src, c0, w):
            i = blk[0] % 2
            blk[0] += 1
            s = stg[i]
            sc.dma("sp", stg_slot[i],
                   lambda e: e.dma_start(out=s[:, :, 0:w], in_=src[:, c0:c0 + w].rearrange("(kc p) n -> p kc n", p=128)),
                   writes=[s])
            return s

        PB_MOD, PB_G, PB_BC, PB_BR = 0, 1, 3, 4
        psmod = T(self.pst(PB_MOD), self.psbuf[PB_MOD])
        psG = [T(self.pst(PB_G + i), self.psbuf[PB_G + i]) for i in range(2)]
        wmod = dr[p + "wmod"]
        for j in range(12):
            s = load_block(wmod, j * BW, BW)
            if j < 8:
                fns = []
                for h in range(2):
                    ch = j * 2 + h
                    for kc in range(KC):
                        fns.append(lambda e, ch=ch, h=h, kc=kc, s=s: e.matmul(
                            psmod[:, ch:ch + 1], lhsT=s[:, kc, h * 128:(h + 1) * 128], rhs=self.cond[:, kc:kc + 1],
                            start=(kc == 0), stop=(kc == KC - 1)))
                sc.pe_group(fns, reads=[s, self.cond], writes=[psmod])
            else:
                g = j - 8
                fns = [lambda e, kc=kc, s=s, g=g: e.matmul(
                    psG[g // 2][:, (g % 2) * BW:(g % 2 + 1) * BW], lhsT=self.cond_rep[:, kc, :], rhs=s[:, kc, :],
                    start=(kc == 0), stop=(kc == KC - 1)) for kc in range(KC)]
                sc.pe_group(fns, reads=[s, self.cond_rep], writes=[psG[g // 2]])
        self.ck(1)
        sc.op("dve", lambda e: e.tensor_tensor(out=modc.ap, in0=psmod[:, 0:16], in1=bmodc[:, 0:16], op=ALU.add),
              reads=[psmod, small], writes=[modc])
        sc.op("dve", lambda e: e.scalar_tensor_tensor(out=acol.ap, in0=modc[:, 8:16], scalar=1.0, in1=gprec.ap,
                                                      op0=ALU.add, op1=ALU.mult),
              reads=[modc, small], writes=[acol])
        for kc in range(KC):
            sc.op("dve", lambda e, kc=kc: e.tensor_scalar(out=shift_rep[:, kc, :], in0=self.ones.ap,
                                                         scalar1=modc[:, kc:kc + 1], scalar2=None, op0=ALU.mult),
                  reads=[self.ones, modc], writes=[shift_rep])
        for i in range(2):
            sc.op("dve", lambda e, i=i: e.tensor_tensor(out=gtmp[:, i * 512:(i + 1) * 512], in0=psG[i].ap,
                                                       in1=gtmp[:, i * 512:(i + 1) * 512], op=ALU.add),
                  reads=[psG[i], gtmp], writes=[gtmp])
        sc.op("dve", lambda e: e.tensor_tensor(out=self.Gbc.ap, in0=gtmp.ap, in1=gpost.ap, op=ALU.mult),
              reads=[gtmp, gpost], writes=[self.Gbc])
        self.dbg("modc", modc); self.dbg("acol", acol); self.dbg("Gbc", self.Gbc)
        self.ck(2)
        win = dr[p + "win"]
        psbc = T(self.pst(PB_BC), self.psbuf[PB_BC])
        psbr = [T(self.pst(PB_BR + i), self.psbuf[PB_BR + i]) for i in range(2)]
        written = []
        c0 = 0
        tog = 0
        while c0 < nin:
            w = min(BW, nin - c0)
            s = load_block(win, c0, w)
            rng = None
            for r in tokmajor_ranges:
                if r[0] <= c0 < r[1]:
                    rng = r
            if rng is not None:
                pb = psbr[tog % 2]
                tog += 1
                fns = [lambda e, kc=kc, s=s, pb=pb, w=w: e.matmul(pb[:, 0:w], lhsT=shift_rep[:, kc, :], rhs=s[:, kc, 0:w],
                                                                   start=(kc == 0), stop=(kc == KC - 1)) for kc in range(KC)]
                sc.pe_group(fns, reads=[s, shift_rep], writes=[pb])
                br = self.biasrow[rng]
                o = c0 - rng[0]
                sc.op("act", lambda e, br=br, o=o, w=w, pb=pb: e.copy(out=br[:, o:o + w], in_=pb[:, 0:w]),
                      reads=[pb], writes=[br])
            if not any(r[0] <= c0 < r[1] for r in nocol_ranges):
                fns = []
                for h in range((w + 127) // 128):
                    ch = c0 // 128 + h
                    hw = min(128, w - h * 128)
                    written.append((ch, hw))
                    for kc in range(KC):
                        fns.append(lambda e, ch=ch, h=h, hw=hw, kc=kc, s=s: e.matmul(
                            psbc[0:hw, ch:ch + 1], lhsT=s[:, kc, h * 128:h * 128 + hw], rhs=modc[:, kc:kc + 1],
                            start=(kc == 0), stop=(kc == KC - 1)))
                sc.pe_group(fns, reads=[s, modc], writes=[psbc])
            for kc in range(KC):
                eng = "dve" if kc % 2 == 0 else "pool"
                if eng == "dve":
                    sc.op(eng, lambda e, kc=kc, s=s, c0=c0, w=w: e.tensor_scalar(
                        out=self.Win[:, kc, c0:c0 + w], in0=s[:, kc, 0:w], scalar1=acol[:, kc:kc + 1], scalar2=None,
                        op0=ALU.mult), reads=[s, acol], writes=[self.Win])
                else:
                    sc.op(eng, lambda e, kc=kc, s=s, c0=c0, w=w: e.tensor_scalar(
                        out=self.Win[:, kc, c0:c0 + w], in0=s[:, kc, 0:w], scalar1=acol[:, kc:kc + 1], scalar2=1.0,
                        op0=ALU.mult, op1=ALU.mult), reads=[s, acol], writes=[self.Win])
            c0 += w
        for ch, hw in written:
            sc.op("dve", lambda e, ch=ch, hw=hw: e.tensor_copy(out=self.biascol[0:hw, ch:ch + 1], in_=psbc[0:hw, ch:ch + 1]),
                  reads=[psbc], writes=[self.biascol])
        self.dbg("Win", self.Win); self.dbg("biascol", T(self.biascol[:, 0:8], self.biascol.buf))
        for r_, t_ in self.biasrow.items():
            self.dbg("biasrow", t_)
        self.ck(3)
        wout = dr[p + "wout"]
        for j in range(D // BW):
            s = load_block(wout, j * BW, BW)
            if wout_rowscale is None:
                sc.op("dve", lambda e, s=s, j=j: e.tensor_copy(out=self.Wout[:, 0:4, j * BW:(j + 1) * BW], in_=s[:, 0:4, :]),
                      reads=[s], writes=[self.Wout])
                sc.op("act", lambda e, s=s, j=j: e.copy(out=self.Wout[:, 4:8, j * BW:(j + 1) * BW], in_=s[:, 4:8, :]),
                      reads=[s], writes=[self.Wout])
            else:
                for kc in range(KC):
                    sc.op("dve", lambda e, s=s, j=j, kc=kc: e.tensor_scalar(
                        out=self.Wout[:, kc, j * BW:(j + 1) * BW], in0=s[:, kc, :], scalar1=wout_rowscale[:, kc:kc + 1],
                        scalar2=None, op0=ALU.mult), reads=[s, wout_rowscale], writes=[self.Wout])
        sc.barrier()
        A.reset(tmp_mark)

    def alloc_stageA(self, nxt=4, nxnT=2):
        A = self.A
        self.xt = [A.alloc([D]) for _ in range(nxt)]
        self.xn = [A.alloc([D], BF16) for _ in range(2)]
        self.xnT = [A.alloc([KC, ST], BF16) for _ in range(nxnT)]
        self.junk = A.alloc([D], BF16)
        self.ssA = [A.alloc([4]) for _ in range(2)]
        self.lnA = [A.alloc([4]) for _ in range(2)]
        self.rsA = [A.alloc([4]) for _ in range(2)]
        self.psT = T(self.ps[:, 7, :].bitcast(BF16).rearrange("p (a b) -> p a b", a=8), self.psbuf[7])

    def stageA(self, j, x_in):
        sc = self.sc
        ss, ln, rs, xnT = self.ssA[j % 2], self.lnA[j % 2], self.rsA[j % 2], self.xnT[j % len(self.xnT)]
        nxt = len(self.xt)
        for tt in range(4):
            tok = j * ST + tt * 128
            xt = self.xt[tt % nxt]
            xn = self.xn[tt % 2]
            sc.dma("sp", self.slots["x%d" % (tt % nxt)], lambda e, xt=xt, tok=tok: e.dma_start(out=xt.ap, in_=x_in[tok:tok + 128, :]),
                   writes=[xt])
            sc.op("act", lambda e, xt=xt, tt=tt, ss=ss: e.activation(out=self.junk.ap, in_=xt.ap, func=AF.Square,
                                                                     accum_out=ss[:, tt:tt + 1]),
                  reads=[xt], writes=[self.junk, ss])
            sc.op("act", lambda e, tt=tt: e.activation(out=ln[:, tt:tt + 1], in_=ss[:, tt:tt + 1], func=AF.Ln, scale=1.0 / D, bias=EPS),
                  reads=[ss], writes=[ln])
            sc.op("act", lambda e, tt=tt: e.activation(out=rs[:, tt:tt + 1], in_=ln[:, tt:tt + 1], func=AF.Exp, scale=-0.5),
                  reads=[ln], writes=[rs])
            sc.op("act", lambda e, xt=xt, xn=xn, tt=tt, rs=rs: e.activation(out=xn.ap, in_=xt.ap, func=AF.Copy,
                                                                            scale=rs[:, tt:tt + 1]),
                  reads=[xt, rs], writes=[xn])
            fns = [lambda e, c=c, xn=xn: e.transpose(out=self.psT[:, c, :], in_=xn[:, c * 128:(c + 1) * 128],
                                                     identity=self.identb.ap) for c in range(KC)]
            sc.pe_group(fns, reads=[xn, self.identb], writes=[self.psT])
            sc.op("dve", lambda e, tt=tt, xnT=xnT: e.tensor_copy(out=xnT[:, :, tt * 128:(tt + 1) * 128], in_=self.psT.ap),
                  reads=[self.psT], writes=[xnT])
        return xnT

    def alloc_stageO(self, nxo=4, nt1=2):
        A = self.A
        self.xo = [A.alloc([D]) for _ in range(nxo)]
        self.t1 = [A.alloc([D]) for _ in range(nt1)]
        self.ssO = [A.alloc([4]) for _ in range(2)]
        self.lnO = [A.alloc([4]) for _ in range(2)]
        self.rsO = [A.alloc([4]) for _ in range(2)]

    def stageO(self, j, ogT, x_in, x_out, psY):
        sc = self.sc
        ss, ln, rs = self.ssO[j % 2], self.lnO[j % 2], self.rsO[j % 2]
        for tt in range(4):
            tok = j * ST + tt * 128
            py = psY[tt % 2]
            xo = self.xo[tt % 2]
            t1 = self.t1[tt % 2]
            sc.dma("sp", self.slots["xo%d" % (tt % 2)], lambda e, xo=xo, tok=tok: e.dma_start(out=xo.ap, in_=x_in[tok:tok + 128, :]),
                   writes=[xo])
            fns = []
            for nb in range(2):
                for c in range(KC):
                    fns.append(lambda e, nb=nb, c=c, py=py, tt=tt: e.matmul(
                        py[:, nb * 512:(nb + 1) * 512], lhsT=ogT[:, c, tt * 128:(tt + 1) * 128],
                        rhs=self.Wout[:, c, nb * 512:(nb + 1) * 512], start=(c == 0), stop=(c == KC - 1)))
            sc.pe_group(fns, reads=[ogT, self.Wout], writes=[py])
            sc.op("act", lambda e, py=py, tt=tt, ss=ss: e.activation(out=self.junk.ap, in_=py.ap, func=AF.Square,
                                                                     accum_out=ss[:, tt:tt + 1]),
                  reads=[py], writes=[self.junk, ss])
            sc.op("dve", lambda e, py=py, t1=t1: e.tensor_tensor(out=t1.ap, in0=py.ap, in1=self.Gbc.ap, op=ALU.mult),
                  reads=[py, self.Gbc], writes=[t1])
        self.rstd_from_ss(ss, rs, 1.0 / D, ln)
        return ss, rs

    def stageO_full(self, j, ogT, x_in, x_out, psY):
        sc = self.sc
        for tt in range(4):
            tok = j * ST + tt * 128
            k = (j * 4 + tt) % 2
            py = psY[k]
            xi = tt % len(self.xo)
            xo = self.xo[xi]
            t1 = self.t1[k % len(self.t1)]
            ss, ln, rs = self.ssO[k], self.lnO[k], self.rsO[k]
            sc.dma("sp", self.slots["xo%d" % xi], lambda e, xo=xo, tok=tok: e.dma_start(out=xo.ap, in_=x_in[tok:tok + 128, :]),
                   writes=[xo])
            fns = []
            for nb in range(2):
                for c in range(KC):
                    fns.append(lambda e, nb=nb, c=c, py=py, tt=tt: e.matmul(
                        py[:, nb * 512:(nb + 1) * 512], lhsT=ogT[:, c, tt * 128:(tt + 1) * 128],
                        rhs=self.Wout[:, c, nb * 512:(nb + 1) * 512], start=(c == 0), stop=(c == KC - 1)))
            self.ck(12)
            sc.pe_group(fns, reads=[ogT, self.Wout], writes=[py])
            self.ck(13)
            for nb in range(2):
                sc.op("act", lambda e, py=py, ss=ss, nb=nb: e.activation(out=self.junk[:, nb * 512:(nb + 1) * 512], in_=py[:, nb * 512:(nb + 1) * 512],
                                                                         func=AF.Square, accum_out=ss[:, 1 + nb:2 + nb]),
                      reads=[py], writes=[self.junk, ss])
                sc.op("dve", lambda e, py=py, t1=t1, nb=nb: e.tensor_tensor(out=t1[:, nb * 512:(nb + 1) * 512], in0=py[:, nb * 512:(nb + 1) * 512],
                                                                            in1=self.Gbc[:, nb * 512:(nb + 1) * 512], op=ALU.mult),
                      reads=[py, self.Gbc], writes=[t1])
            self.ck(14)
            sc.op("dve", lambda e, ss=ss: e.tensor_tensor(out=ss[:, 0:1], in0=ss[:, 1:2], in1=ss[:, 2:3], op=ALU.add),
                  reads=[ss], writes=[ss])
            sc.op("act", lambda e, ss=ss, ln=ln: e.activation(out=ln[:, 0:1], in_=ss[:, 0:1], func=AF.Ln, scale=1.0 / D, bias=EPS),
                  reads=[ss], writes=[ln])
            sc.op("act", lambda e, rs=rs, ln=ln: e.activation(out=rs[:, 0:1], in_=ln[:, 0:1], func=AF.Exp, scale=-0.5),
                  reads=[ln], writes=[rs])
            self.ck(15)
            sc.op("dve", lambda e, t1=t1, xo=xo, rs=rs: e.scalar_tensor_tensor(out=xo.ap, in0=t1.ap, scalar=rs[:, 0:1], in1=xo.ap,
                                                                               op0=ALU.mult, op1=ALU.add),
                  reads=[t1, xo, rs], writes=[xo])
            self.ck(16)
            sc.dma(STQ, self.slots["xs%d" % xi], lambda e, xo=xo, tok=tok: e.dma_start(out=x_out[tok:tok + 128, :], in_=xo.ap),
                   reads=[xo])

    def layer(self, L, x_in, x_out):
        kind = L % 3
        if kind == 1:
            self.layer_sgu(L, x_in, x_out)
        elif kind == 0:
            self.layer_gla(L, x_in, x_out)
        else:
            self.layer_fox(L, x_in, x_out)


    def layer_gla(self, L, x_in, x_out):
        sc, A, dr = self.sc, self.A, self.dram
        p = "L%d_" % L
        ghc = A.alloc([8])
        sl = self.slots["c0"]
        sc.dma("sp", sl, lambda e: e.dma_start(out=ghc.ap, in_=dr[p + "ghc"]), writes=[ghc])
        sc.barrier()
        self.prep(L, [(512, 2048)], nocol_ranges=[(1024, 2048)], wout_rowscale=ghc)
        self.alloc_stageA(nxt=2, nxnT=1)
        self.alloc_stageO(nxo=2, nt1=1)
        wa2 = A.alloc([512])
        sc.dma("sp", sl, lambda e: e.dma_start(out=wa2[0:17, :], in_=dr[p + "wa2"]), writes=[wa2])
        sc.barrier()
        alT = A.alloc([ST])
        sc.op("dve", lambda e: e.memset(alT[0:17, :], 1.0), writes=[alT])
        f512 = A.alloc([512])
        spt = [A.alloc([512]) for _ in range(2)]
        enb = [A.alloc([512]) for _ in range(2)]
        ebT = A.alloc([4, ST])
        enbT = A.alloc([4, ST])
        elast = [A.alloc([4]) for _ in range(4)]
        qT = A.alloc([4, ST], BF16)
        kT = A.alloc([4, ST], BF16)
        ktok = [A.alloc([512], BF16) for _ in range(4)]
        vtok = [A.alloc([D], BF16) for _ in range(4)]
        szT = A.alloc([KC, ST], BF16)
        ogT = A.alloc([KC, ST], BF16)
        ATs = A.alloc([4, 128], BF16)
        osq = A.alloc([8, 128], BF16)
        rstd = A.alloc([4, 128])
        otmp = A.alloc([8, 128], BF16)
        Sst = A.alloc([4, 256])
        Sbf = A.alloc([4, 256], BF16)
        sc.op("dve", lambda e: e.memset(Sst.ap, 0.0), writes=[Sst])
        sc.op("dve", lambda e: e.memset(Sbf.ap, 0.0), writes=[Sbf])
        psW = [T(self.pst(0, 2), self.psbuf[0]), T(self.pst(2, 2), self.psbuf[2])]
        psA = [T(self.pst(4 + i), self.psbuf[4 + i]) for i in range(3)]
        brow = self.biasrow[(512, 2048)]
        LNS = -0.5 * float(np.log(128.0))
        ia = [0]
        iw = [0]

        def nextA():
            t = psA[ia[0] % 3]
            ia[0] += 1
            return t

        def nextW():
            t = psW[iw[0] % 2]
            iw[0] += 1
            return t

        for j in range(self.nST):
            xnT = self.stageA(j, x_in)
            pa = nextA()
            fns = [lambda e, kc=kc, pa=pa: e.matmul(pa[0:16, :], lhsT=self.Win[:, kc, 3072:3088], rhs=xnT[:, kc, :],
                                                    start=(kc == 0), stop=(kc == KC - 1)) for kc in range(KC)]
            sc.pe_group(fns, reads=[xnT, self.Win], writes=[pa])
            sc.op("act", lambda e, pa=pa: e.activation(out=alT[0:16, :], in_=pa[0:16, :], func=AF.Identity,
                                                       bias=self.biascol[0:16, 24:25]),
                  reads=[pa, self.biascol], writes=[alT])
            for tt in range(4):
                ts_ = slice(tt * 128, (tt + 1) * 128)
                pa = nextA()
                sc.pe_group([lambda e, pa=pa, ts_=ts_: e.matmul(pa.ap, lhsT=alT[0:17, ts_], rhs=wa2[0:17, :], start=True, stop=True)],
                            reads=[alT, wa2], writes=[pa])
                sp_ = spt[tt % 2]
                sc.op("act", lambda e, pa=pa: e.activation(out=f512.ap, in_=pa.ap, func=AF.Exp, scale=-1.0),
                      reads=[pa], writes=[f512])
                sc.op("act", lambda e, sp_=sp_: e.activation(out=sp_.ap, in_=f512.ap, func=AF.Ln, bias=1.0),
                      reads=[f512], writes=[sp_])
                pb = nextA()
                sc.pe_group([lambda e, pb=pb, sp_=sp_: e.matmul(pb.ap, lhsT=self.unegf.ap, rhs=sp_.ap, start=True, stop=True)],
                            reads=[self.unegf, sp_], writes=[pb])
                en_ = enb[tt % 2]
                sc.op("act", lambda e, pb=pb, en_=en_: e.activation(out=en_.ap, in_=pb.ap, func=AF.Exp, scale=-1.0),
                      reads=[pb], writes=[en_])
                pc = nextA()
                fns = [lambda e, h=h, pc=pc, sp_=sp_: e.matmul(pc[:, h * 128:(h + 1) * 128], lhsT=sp_[:, h * 128:(h + 1) * 128],
                                                               rhs=self.unegf.ap, start=True, stop=True) for h in range(4)]
                sc.pe_group(fns, reads=[self.unegf, sp_], writes=[pc])
                pc3 = pc.ap.rearrange("p (h t) -> p h t", h=4)
                sc.op("act", lambda e, pc3=pc3, ts_=ts_: e.activation(out=ebT[:, :, ts_], in_=pc3, func=AF.Exp, bias=LNS),
                      reads=[pc], writes=[ebT])
                sc.op("act", lambda e, pc3=pc3, ts_=ts_: e.activation(out=enbT[:, :, ts_], in_=pc3, func=AF.Exp, scale=-1.0),
                      reads=[pc], writes=[enbT])
                el = elast[tt]
                sc.op("act", lambda e, pc3=pc3, el=el: e.activation(out=el.ap, in_=pc3[:, :, 127], func=AF.Exp),
                      reads=[pc], writes=[el])
                pk = nextA()
                fns = [lambda e, kc=kc, pk=pk, ts_=ts_: e.matmul(pk.ap, lhsT=xnT[:, kc, ts_], rhs=self.Win[:, kc, 512:1024],
                                                                 start=(kc == 0), stop=(kc == KC - 1)) for kc in range(KC)]
                sc.pe_group(fns, reads=[xnT, self.Win], writes=[pk])
                sc.op("dve", lambda e, pk=pk: e.tensor_tensor(out=f512.ap, in0=pk.ap, in1=brow[:, 0:512], op=ALU.add),
                      reads=[pk, brow], writes=[f512])
                sc.op("pool", lambda e, tt=tt, en_=en_: e.tensor_tensor(out=ktok[tt].ap, in0=f512.ap, in1=en_.ap, op=ALU.mult),
                      reads=[f512, en_], writes=[ktok[tt]])
                pv = nextW()
                fns = []
                for nb in range(2):
                    for kc in range(KC):
                        fns.append(lambda e, nb=nb, kc=kc, pv=pv, ts_=ts_: e.matmul(
                            pv[:, nb * 512:(nb + 1) * 512], lhsT=xnT[:, kc, ts_],
                            rhs=self.Win[:, kc, 1024 + nb * 512:1024 + (nb + 1) * 512], start=(kc == 0), stop=(kc == KC - 1)))
                sc.pe_group(fns, reads=[xnT, self.Win], writes=[pv])
                sc.op("dve", lambda e, pv=pv, tt=tt: e.tensor_tensor(out=vtok[tt].ap, in0=pv.ap, in1=brow[:, 512:1536], op=ALU.add),
                      reads=[pv, brow], writes=[vtok[tt]])
            for h in range(4):
                pa = nextA()
                fns = [lambda e, kc=kc, pa=pa, h=h: e.matmul(pa.ap, lhsT=self.Win[:, kc, h * 128:(h + 1) * 128], rhs=xnT[:, kc, :],
                                                             start=(kc == 0), stop=(kc == KC - 1)) for kc in range(KC)]
                sc.pe_group(fns, reads=[xnT, self.Win], writes=[pa])
                sc.op("dve", lambda e, pa=pa, h=h: e.scalar_tensor_tensor(out=qT[:, h, :], in0=pa.ap, scalar=self.biascol[:, h:h + 1],
                                                                          in1=ebT[:, h, :], op0=ALU.add, op1=ALU.mult),
                      reads=[pa, self.biascol, ebT], writes=[qT])
                pa = nextA()
                fns = [lambda e, kc=kc, pa=pa, h=h: e.matmul(pa.ap, lhsT=self.Win[:, kc, 512 + h * 128:512 + (h + 1) * 128],
                                                             rhs=xnT[:, kc, :], start=(kc == 0), stop=(kc == KC - 1))
                       for kc in range(KC)]
                sc.pe_group(fns, reads=[xnT, self.Win], writes=[pa])
                sc.op("dve", lambda e, pa=pa, h=h: e.scalar_tensor_tensor(out=kT[:, h, :], in0=pa.ap, scalar=self.biascol[:, 4 + h:5 + h],
                                                                          in1=enbT[:, h, :], op0=ALU.add, op1=ALU.mult),
                      reads=[pa, self.biascol, enbT], writes=[kT])
            for c in range(KC):
                pa = nextA()
                fns = [lambda e, kc=kc, c=c, pa=pa: e.matmul(pa.ap, lhsT=self.Win[:, kc, 2048 + c * 128:2048 + (c + 1) * 128],
                                                             rhs=xnT[:, kc, :], start=(kc == 0), stop=(kc == KC - 1))
                       for kc in range(KC)]
                sc.pe_group(fns, reads=[xnT, self.Win], writes=[pa])
                sc.op("act", lambda e, c=c, pa=pa: e.activation(out=szT[:, c, :], in_=pa.ap, func=AF.Silu,
                                                                bias=self.biascol[:, 16 + c:17 + c]),
                      reads=[pa, self.biascol], writes=[szT])
            for tt in range(4):
                ts_ = slice(tt * 128, (tt + 1) * 128)
                pa = nextA()
                fns = [lambda e, h=h, pa=pa, ts_=ts_: e.matmul(pa[:, h * 128:(h + 1) * 128], lhsT=kT[:, h, ts_], rhs=qT[:, h, ts_],
                                                               start=True, stop=True) for h in range(4)]
                sc.pe_group(fns, reads=[kT, qT], writes=[pa])
                for h in range(4):
                    sc.op("dve", lambda e, h=h, pa=pa: e.tensor_tensor(out=ATs[:, h, :], in0=pa[:, h * 128:(h + 1) * 128],
                                                                       in1=self.trif.ap, op=ALU.mult),
                          reads=[pa, self.trif], writes=[ATs])
                po = nextW()
                fns = []
                for h in range(4):
                    for half in range(2):
                        c = 2 * h + half
                        fns.append(lambda e, h=h, c=c, half=half, po=po, tt=tt: e.matmul(
                            po[:, c * 128:(c + 1) * 128], lhsT=vtok[tt][:, c * 128:(c + 1) * 128], rhs=ATs[:, h, :],
                            start=True, stop=False))
                        fns.append(lambda e, h=h, c=c, half=half, po=po, ts_=ts_: e.matmul(
                            po[:, c * 128:(c + 1) * 128], lhsT=Sbf[:, h, half * 128:(half + 1) * 128], rhs=qT[:, h, ts_],
                            start=False, stop=True))
                sc.pe_group(fns, reads=[vtok[tt], ATs, Sbf, qT], writes=[po])
                for nb in range(2):
                    sc.op("act", lambda e, nb=nb, po=po: e.activation(
                        out=osq[:, nb * 4:(nb + 1) * 4, :], in_=po[:, nb * 512:(nb + 1) * 512].rearrange("p (c t) -> p c t", c=4),
                        func=AF.Square), reads=[po], writes=[osq])
                ps_ = nextA()
                fns = []
                for h in range(4):
                    for half in range(2):
                        fns.append(lambda e, h=h, half=half, ps_=ps_: e.matmul(
                            ps_[:, h * 128:(h + 1) * 128], lhsT=self.onesb.ap, rhs=osq[:, 2 * h + half, :],
                            start=(half == 0), stop=(half == 1)))
                sc.pe_group(fns, reads=[self.onesb, osq], writes=[ps_])
                sc.op("act", lambda e, ps_=ps_: e.activation(out=rstd.ap, in_=ps_.ap.rearrange("p (h t) -> p h t", h=4),
                                                             func=AF.Ln, scale=1.0 / 256, bias=EPS),
                      reads=[ps_], writes=[rstd])
                sc.op("act", lambda e: e.activation(out=rstd.ap, in_=rstd.ap, func=AF.Exp, scale=-0.5),
                      reads=[rstd], writes=[rstd])
                for h in range(4):
                    for half in range(2):
                        c = 2 * h + half
                        sc.op("dve", lambda e, h=h, c=c, po=po: e.tensor_tensor(out=otmp[:, c, :], in0=po[:, c * 128:(c + 1) * 128],
                                                                                in1=rstd[:, h, :], op=ALU.mult),
                              reads=[po, rstd], writes=[otmp])
                sc.op("pool", lambda e, ts_=ts_: e.tensor_tensor(out=ogT[:, :, ts_], in0=otmp.ap, in1=szT[:, :, ts_], op=ALU.mult),
                      reads=[otmp, szT], writes=[ogT])
                pP = nextW()
                fns = [lambda e, h=h, pP=pP, tt=tt: e.matmul(pP[:, h * 256:(h + 1) * 256], lhsT=ktok[tt][:, h * 128:(h + 1) * 128],
                                                             rhs=vtok[tt][:, h * 256:(h + 1) * 256], start=True, stop=True)
                       for h in range(4)]
                sc.pe_group(fns, reads=[ktok[tt], vtok[tt]], writes=[pP])
                S2 = Sst.ap.rearrange("p h v -> p (h v)")
                sc.op("dve", lambda e, pP=pP, S2=S2: e.tensor_tensor(out=S2, in0=pP.ap, in1=S2, op=ALU.add),
                      reads=[pP, Sst], writes=[Sst])
                el = elast[tt]
                for h in range(4):
                    sc.op("pool", lambda e, h=h, el=el: e.tensor_scalar(out=Sst[:, h, :], in0=Sst[:, h, :], scalar1=el[:, h:h + 1],
                                                                        scalar2=1.0, op0=ALU.mult, op1=ALU.mult),
                          reads=[Sst, el], writes=[Sst])
                sc.op("act", lambda e: e.copy(out=Sbf.ap, in_=Sst.ap), reads=[Sst], writes=[Sbf])
            self.stageO_full(j, ogT, x_in, x_out, psW)

    def layer_fox(self, L, x_in, x_out):
        sc, A, dr, S = self.sc, self.A, self.dram, self.S
        p = "L%d_" % L
        QA, KA, VS, SZ, OG = self.QA, self.KA, self.VS, self.SZ, self.OG
        self.prep(L, [(2048, 3072)])
        sl = self.slots["c0"]
        gqk = A.alloc([2])
        bfc = A.alloc([1])
        negm_f = A.alloc([128])
        negm = A.alloc([128], BF16)
        bo_f = A.alloc([128])
        bones = A.alloc([128], BF16)
        self_sel = A.alloc([64])
        ones3 = A.alloc([3, ST], BF16)
        sc.dma("sp", sl, lambda e: e.dma_start(out=gqk.ap, in_=dr[p + "gqk"]), writes=[gqk])
        sc.dma("sp", sl, lambda e: e.dma_start(out=bfc[0:16, :], in_=dr[p + "bf"]), writes=[bfc])
        sc.dma("sp", sl, lambda e: e.dma_start(out=negm_f.ap, in_=dr["negmask"]), writes=[negm_f])
        sc.dma("sp", sl, lambda e: e.dma_start(out=bo_f.ap, in_=dr["blockones"]), writes=[bo_f])
        sc.dma("sp", sl, lambda e: e.dma_start(out=self_sel[0:65, :], in_=dr["sel"]), writes=[self_sel])
        sc.barrier()
        sc.op("dve", lambda e: e.tensor_copy(out=negm.ap, in_=negm_f.ap), reads=[negm_f], writes=[negm])
        sc.op("dve", lambda e: e.tensor_copy(out=bones.ap, in_=bo_f.ap), reads=[bo_f], writes=[bones])
        sc.op("dve", lambda e: e.memset(ones3.ap, 1.0), writes=[ones3])
        sc.op("dve", lambda e: e.tensor_scalar(out=gqk[:, 0:1], in0=gqk[:, 0:1], scalar1=0.125, scalar2=None, op0=ALU.mult),
              reads=[gqk], writes=[gqk])
        nfb = A.alloc([1])
        sc.op("dve", lambda e: e.tensor_tensor(out=nfb[0:16, :], in0=self.biascol[0:16, 32:33], in1=bfc[0:16, :], op=ALU.add),
              reads=[self.biascol, bfc], writes=[nfb])
        sc.op("dve", lambda e: e.tensor_scalar(out=nfb[0:16, :], in0=nfb[0:16, :], scalar1=-1.0, scalar2=None, op0=ALU.mult),
              reads=[nfb], writes=[nfb])
        ph_mark = A.mark()
        self.alloc_stageA(nxt=2, nxnT=1)
        sq = A.alloc([ST], BF16)
        rstd = A.alloc([ST])
        tmpf = A.alloc([ST])
        qn = [A.alloc([ST], BF16) for _ in range(2)]
        vtok = [A.alloc([D], BF16) for _ in range(2)]
        szT = A.alloc([KC, ST], BF16)
        Ff = [A.alloc([ST]) for _ in range(2)]
        fe = A.alloc([ST])
        fsp = A.alloc([ST])
        fr = A.alloc([ST])
        pcs = [A.alloc([ST], BF16) for _ in range(6)]
        sc.op("dve", lambda e: e.memset(fr.ap, 1.0), writes=[fr])
        psW = [T(self.pst(0, 2), self.psbuf[0]), T(self.pst(2, 2), self.psbuf[2])]
        psA = [T(self.pst(4 + i), self.psbuf[4 + i]) for i in range(3)]
        brow = self.biasrow[(2048, 3072)]
        ia = [0]

        def nextA():
            t = psA[ia[0] % 3]
            ia[0] += 1
            return t

        iq = 0
        for j in range(self.nST):
            tok0 = j * ST
            xnT = self.stageA(j, x_in)
            for which in range(2):
                DST = QA if which == 0 else KA
                for c in range(KC):
                    col0 = which * 1024 + c * 128
                    bcol = self.biascol[:, which * 8 + c:which * 8 + c + 1]
                    pa = nextA()
                    fns = [lambda e, kc=kc, pa=pa, col0=col0: e.matmul(pa.ap, lhsT=self.Win[:, kc, col0:col0 + 128], rhs=xnT[:, kc, :],
                                                                       start=(kc == 0), stop=(kc == KC - 1)) for kc in range(KC)]
                    sc.pe_group(fns, reads=[xnT, self.Win], writes=[pa])
                    sc.op("act", lambda e, pa=pa, bcol=bcol: e.activation(out=sq.ap, in_=pa.ap, func=AF.Square, bias=bcol),
                          reads=[pa, self.biascol], writes=[sq])
                    pb = nextA()
                    sc.pe_group([lambda e, pb=pb: e.matmul(pb.ap, lhsT=bones.ap, rhs=sq.ap, start=True, stop=True)],
                                reads=[bones, sq], writes=[pb])
                    sc.op("act", lambda e, pb=pb: e.activation(out=rstd.ap, in_=pb.ap, func=AF.Ln, scale=1.0 / 64, bias=EPS),
                          reads=[pb], writes=[rstd])
                    sc.op("act", lambda e: e.activation(out=rstd.ap, in_=rstd.ap, func=AF.Exp, scale=-0.5),
                          reads=[rstd], writes=[rstd])
                    sc.op("dve", lambda e, pa=pa, bcol=bcol: e.scalar_tensor_tensor(out=tmpf.ap, in0=pa.ap, scalar=bcol, in1=rstd.ap,
                                                                                    op0=ALU.add, op1=ALU.mult),
                          reads=[pa, self.biascol, rstd], writes=[tmpf])
                    q_ = qn[iq % 2]
                    slq = self.slots["fq%d" % (iq % 2)]
                    iq += 1
                    sc.op("pool", lambda e, q_=q_, which=which: e.tensor_scalar(out=q_.ap, in0=tmpf.ap, scalar1=gqk[:, which:which + 1],
                                                                               scalar2=1.0, op0=ALU.mult, op1=ALU.mult),
                          reads=[tmpf, gqk], writes=[q_])
                    for hh in range(2):
                        sc.dma("sp", slq, lambda e, q_=q_, hh=hh, c=c, DST=DST, tok0=tok0: e.dma_start(
                            out=DST[2 * c + hh, 0:64, tok0:tok0 + ST], in_=q_[hh * 64:(hh + 1) * 64, :]), reads=[q_])
            for tt in range(4):
                ts_ = slice(tt * 128, (tt + 1) * 128)
                pv = psW[tt % 2]
                fns = []
                for nb in range(2):
                    for kc in range(KC):
                        fns.append(lambda e, nb=nb, kc=kc, pv=pv, ts_=ts_: e.matmul(
                            pv[:, nb * 512:(nb + 1) * 512], lhsT=xnT[:, kc, ts_],
                            rhs=self.Win[:, kc, 2048 + nb * 512:2048 + (nb + 1) * 512], start=(kc == 0), stop=(kc == KC - 1)))
                sc.pe_group(fns, reads=[xnT, self.Win], writes=[pv])
                v_ = vtok[tt % 2]
                sc.op("dve", lambda e, pv=pv, v_=v_: e.tensor_tensor(out=v_.ap, in0=pv.ap, in1=brow.ap, op=ALU.add),
                      reads=[pv, brow], writes=[v_])
                sc.dma("sp", self.slots["fv%d" % (tt % 2)], lambda e, v_=v_, tt=tt, tok0=tok0: e.dma_start(
                    out=VS[tok0 + tt * 128:tok0 + (tt + 1) * 128, :], in_=v_.ap), reads=[v_])
            for c in range(KC):
                pa = nextA()
                fns = [lambda e, kc=kc, c=c, pa=pa: e.matmul(pa.ap, lhsT=self.Win[:, kc, 3072 + c * 128:3072 + (c + 1) * 128],
                                                             rhs=xnT[:, kc, :], start=(kc == 0), stop=(kc == KC - 1))
                       for kc in range(KC)]
                sc.pe_group(fns, reads=[xnT, self.Win], writes=[pa])
                sc.op("act", lambda e, c=c, pa=pa: e.activation(out=szT[:, c, :], in_=pa.ap, func=AF.Silu,
                                                                bias=self.biascol[:, 24 + c:25 + c]),
                      reads=[pa, self.biascol], writes=[szT])
            sc.dma("sp", self.slots["fz0"], lambda e, tok0=tok0: e.dma_start(
                out=SZ[:, tok0:tok0 + ST].rearrange("(c p) t -> p c t", p=128), in_=szT.ap), reads=[szT])
            pa = nextA()
            fns = [lambda e, kc=kc, pa=pa: e.matmul(pa[0:16, :], lhsT=self.Win[:, kc, 4096:4112], rhs=xnT[:, kc, :],
                                                    start=(kc == 0), stop=(kc == KC - 1)) for kc in range(KC)]
            sc.pe_group(fns, reads=[xnT, self.Win], writes=[pa])
            sc.op("act", lambda e, pa=pa: e.activation(out=fe[0:16, :], in_=pa[0:16, :], func=AF.Exp, scale=-1.0, bias=nfb[0:16, :]),
                  reads=[pa, nfb], writes=[fe])
            sc.op("act", lambda e: e.activation(out=fsp[0:16, :], in_=fe[0:16, :], func=AF.Ln, bias=1.0),
                  reads=[fe], writes=[fsp])
            F_ = Ff[j % 2]
            Fp = Ff[(j + 1) % 2]
            init = 0.0 if j == 0 else Fp[0:16, ST - 1:ST]
            sc.op("dve", lambda e, F_=F_, init=init: e.tensor_tensor_scan(out=F_[0:16, :], data0=fr[0:16, :],
                                                                          data1=fsp[0:16, :], initial=init, op0=ALU.mult, op1=ALU.subtract),
                  reads=[fsp, fr] + ([Fp] if j > 0 else []), writes=[F_])
            sc.op("dve", lambda e, F_=F_: e.tensor_copy(out=pcs[0][0:16, :], in_=F_[0:16, :]), reads=[F_], writes=[pcs[0]])
            sc.op("dve", lambda e, F_=F_: e.tensor_tensor(out=fe[0:16, :], in0=F_[0:16, :], in1=pcs[0][0:16, :], op=ALU.subtract),
                  reads=[F_, pcs[0]], writes=[fe])
            sc.op("dve", lambda e: e.tensor_copy(out=pcs[1][0:16, :], in_=fe[0:16, :]), reads=[fe], writes=[pcs[1]])
            sc.op("dve", lambda e: e.tensor_tensor(out=fe[0:16, :], in0=fe[0:16, :], in1=pcs[1][0:16, :], op=ALU.subtract),
                  reads=[fe, pcs[1]], writes=[fe])
            sc.op("dve", lambda e: e.tensor_copy(out=pcs[2][0:16, :], in_=fe[0:16, :]), reads=[fe], writes=[pcs[2]])
            for i in range(3):
                sc.op("pool", lambda e, i=i: e.tensor_scalar(out=pcs[3 + i][0:16, :], in0=pcs[i][0:16, :], scalar1=-1.0, scalar2=1.0,
                                                             op0=ALU.mult, op1=ALU.mult), reads=[pcs[i]], writes=[pcs[3 + i]])
            for i in range(3):
                sc.dma("sp", self.slots["ff0"], lambda e, i=i, tok0=tok0: e.dma_start(out=QA[:, 64 + i, tok0:tok0 + ST], in_=pcs[i][0:16, :]),
                       reads=[pcs[i]])
                sc.dma("sp", self.slots["ff0"], lambda e, i=i, tok0=tok0: e.dma_start(out=KA[:, 67 + i, tok0:tok0 + ST], in_=pcs[3 + i][0:16, :]),
                       reads=[pcs[3 + i]])
            sc.dma("sp", self.slots["ff0"], lambda e, tok0=tok0: e.dma_start(out=QA[:, 67:70, tok0:tok0 + ST], in_=ones3[0:16, :, :]),
                   reads=[ones3])
            sc.dma("sp", self.slots["ff0"], lambda e, tok0=tok0: e.dma_start(out=KA[:, 64:67, tok0:tok0 + ST], in_=ones3[0:16, :, :]),
                   reads=[ones3])
        sc.barrier()
        A.reset(ph_mark)
        NKT = S // 128
        NQB = S // ST
        KAh = [A.alloc([S], BF16) for _ in range(2)]
        Vh = [A.alloc([NKT, 65], BF16) for _ in range(2)]
        QAq = [A.alloc([ST], BF16) for _ in range(3)]
        szq = [A.alloc([ST], BF16) for _ in range(2)]
        PT = [A.alloc([2, ST], BF16) for _ in range(3)]
        Osb = [A.alloc([ST]) for _ in range(2)]
        tn = A.alloc([ST])
        ogq = [A.alloc([ST], BF16) for _ in range(2)]
        for v_ in Vh:
            sc.op("dve", lambda e, v_=v_: e.memset(v_.ap, 1.0), writes=[v_])
        psS = [T(self.ps[:, 0:2, :], self.psbuf[0]), T(self.ps[:, 2:4, :], self.psbuf[2])]
        psO = [T(self.pst(4), self.psbuf[4]), T(self.pst(5), self.psbuf[5])]
        psB = T(self.pst(6), self.psbuf[6])
        iS = 0
        iP = 0
        iQ = 0
        for h in range(16):
            ka = KAh[h % 2]
            vh = Vh[h % 2]
            sc.dma("sp", self.slots["fk%d" % (h % 2)], lambda e, ka=ka, h=h: e.dma_start(out=ka[0:70, :], in_=KA[h, :, :]), writes=[ka])
            sc.dma("sp", self.slots["fvh%d" % (h % 2)], lambda e, vh=vh, h=h: e.dma_start(
                out=vh[:, :, 0:64], in_=VS[:, h * 64:(h + 1) * 64].rearrange("(kt p) d -> p kt d", p=128)), writes=[vh])
            for qb in range(NQB):
                q0t = qb * ST
                qa = QAq[iQ % 3]
                zq = szq[iQ % 2]
                sc.dma("sp", self.slots["fqq%d" % (iQ % 3)], lambda e, qa=qa, h=h, q0t=q0t: e.dma_start(out=qa[0:70, :], in_=QA[h, :, q0t:q0t + ST]),
                       writes=[qa])
                sc.dma("sp", self.slots["fsz%d" % (iQ % 2)], lambda e, zq=zq, h=h, q0t=q0t: e.dma_start(out=zq[0:64, :], in_=SZ[h * 64:(h + 1) * 64, q0t:q0t + ST]),
                       writes=[zq])
                po = psO[iQ % 2]
                nk = 4 * qb + 4
                groups = [[kt, kt + 1] for kt in range(0, 4 * qb, 2)] + [[kt] for kt in range(4 * qb, nk)]
                for g in groups:
                    pS = psS[iS % 2]
                    iS += 1
                    pt = PT[iP % 3]
                    iP += 1
                    fns = []
                    q0 = 0
                    for i, kt in enumerate(g):
                        r = kt - 4 * qb
                        q0 = max(r, 0) * 128
                        fns.append(lambda e, i=i, kt=kt, q0=q0, pS=pS, ka=ka, qa=qa, r=r: e.matmul(
                            pS[:, i, q0:ST], lhsT=ka[0:70, kt * 128:(kt + 1) * 128], rhs=qa[0:70, q0:ST], start=True, stop=(r < 0)))
                        if r >= 0:
                            fns.append(lambda e, i=i, q0=q0, pS=pS: e.matmul(pS[:, i, q0:q0 + 128], lhsT=self.identb.ap, rhs=negm.ap,
                                                                             start=False, stop=True))
                    sc.pe_group(fns, reads=[ka, qa, self.identb, negm], writes=[pS])
                    if len(g) == 2:
                        sc.op("act", lambda e, pS=pS, pt=pt: e.activation(out=pt.ap, in_=pS.ap, func=AF.Exp), reads=[pS], writes=[pt])
                    else:
                        sc.op("act", lambda e, pS=pS, pt=pt, q0=q0: e.activation(out=pt[:, 0, q0:ST], in_=pS[:, 0, q0:ST], func=AF.Exp),
                              reads=[pS], writes=[pt])
                    fns = []
                    for i, kt in enumerate(g):
                        r = kt - 4 * qb
                        q0 = max(r, 0) * 128
                        fns.append(lambda e, i=i, kt=kt, q0=q0, po=po, vh=vh, pt=pt, nk=nk: e.matmul(
                            po[0:65, q0:ST], lhsT=vh[:, kt, :], rhs=pt[:, i, q0:ST], start=(kt == 0), stop=(kt == nk - 1)))
                    sc.pe_group(fns, reads=[vh, pt], writes=[po])
                ob = Osb[iQ % 2]
                og_ = ogq[iQ % 2]
                sc.op("dve", lambda e, ob=ob, po=po: e.tensor_copy(out=ob[0:65, :], in_=po[0:65, :]), reads=[po], writes=[ob])
                sc.op("dve", lambda e, ob=ob: e.reciprocal(out=ob[64:65, :], in_=ob[64:65, :]), reads=[ob], writes=[ob])
                sc.pe_group([lambda e, ob=ob: e.matmul(psB[0:64, :], lhsT=self_sel[0:65, :], rhs=ob[0:65, :], start=True, stop=True)],
                            reads=[self_sel, ob], writes=[psB])
                sc.op("dve", lambda e, ob=ob: e.tensor_tensor(out=tn[0:64, :], in0=ob[0:64, :], in1=psB[0:64, :], op=ALU.mult),
                      reads=[ob, psB], writes=[tn])
                sc.op("pool", lambda e, og_=og_, zq=zq: e.tensor_tensor(out=og_[0:64, :], in0=tn[0:64, :], in1=zq[0:64, :], op=ALU.mult),
                      reads=[tn, zq], writes=[og_])
                sc.dma("sp", self.slots["fog%d" % (iQ % 2)], lambda e, og_=og_, h=h, q0t=q0t: e.dma_start(
                    out=OG[h * 64:(h + 1) * 64, q0t:q0t + ST], in_=og_[0:64, :]), reads=[og_])
                iQ += 1
        sc.barrier()
        A.reset(ph_mark)
        self.alloc_stageA(nxt=1, nxnT=1)
        self.alloc_stageO()
        ogT = [A.alloc([KC, ST], BF16) for _ in range(2)]
        psW = [T(self.pst(0, 2), self.psbuf[0]), T(self.pst(2, 2), self.psbuf[2])]
        for j in range(self.nST):
            tok0 = j * ST
            og = ogT[j % 2]
            sc.dma("sp", self.slots["fo%d" % (j % 2)], lambda e, og=og, tok0=tok0: e.dma_start(
                out=og.ap, in_=OG[:, tok0:tok0 + ST].rearrange("(c p) t -> p c t", p=128)), writes=[og])
            self.stageO_full(j, og, x_in, x_out, psW)

    def layer_sgu(self, L, x_in, x_out):
        sc, A, dr = self.sc, self.A, self.dram
        p = "L%d_" % L
        self.prep(L, [(1024, 2048)])
        sc.barrier()
        self.ck(4)
        self.alloc_stageA()
        self.alloc_stageO()
        lng = A.alloc([D])
        lnb = A.alloc([D])
        bsb = A.alloc([4, 128])
        wsf = A.alloc([4, 128])
        WcT = A.alloc([4, 128], BF16)
        sl = self.slots["c0"]
        sc.dma("sp", sl, lambda e: e.dma_start(out=lng.ap, in_=dr[p + "lng"]), writes=[lng])
        sc.dma("sp", sl, lambda e: e.dma_start(out=lnb.ap, in_=dr[p + "lnb"]), writes=[lnb])
        sc.dma("sp", sl, lambda e: e.dma_start(out=bsb.ap, in_=dr[p + "bs"]), writes=[bsb])
        sc.dma("sp", sl, lambda e: e.dma_start(out=wsf.ap, in_=dr[p + "ws"]), writes=[wsf])
        sc.barrier()
        pw = T(self.pst(0), self.psbuf[0])
        fns = [lambda e, g=g: e.transpose(out=pw[:, g * 128:(g + 1) * 128], in_=wsf[:, g, :], identity=self.identf.ap)
               for g in range(4)]
        sc.pe_group(fns, reads=[wsf, self.identf], writes=[pw])
        for g in range(4):
            sc.op("dve", lambda e, g=g: e.tensor_tensor(out=WcT[:, g, :], in0=pw[:, g * 128:(g + 1) * 128], in1=self.trif.ap,
                                                       op=ALU.mult), reads=[pw, self.trif], writes=[WcT])
        self.dbg("Wout", self.Wout); self.dbg("WcT", WcT)
        self.ck(5)
        gl = [A.alloc([D]) for _ in range(4)]
        vln = [A.alloc([D], BF16) for _ in range(4)]
        uT = A.alloc([KC, ST], BF16)
        szT = A.alloc([KC, ST], BF16)
        ogT = A.alloc([KC, ST], BF16)
        mt = [A.alloc([ST]) for _ in range(2)]
        s1 = A.alloc([4])
        nm = A.alloc([4])
        s2 = A.alloc([4])
        lv = A.alloc([4])
        rv = A.alloc([4])
        psW = [T(self.pst(0, 2), self.psbuf[0]), T(self.pst(2, 2), self.psbuf[2])]
        psA = [T(self.pst(4 + i), self.psbuf[4 + i]) for i in range(3)]
        brow = self.biasrow[(1024, 2048)]
        ia = 0
        for j in range(self.nST):
            xnT = self.stageA(j, x_in)
            self.dbg("xnT", xnT); self.dbg("rsA", self.rsA[j % 2])
            self.ck(6)
            for tt in range(4):
                pv = psW[tt % 2]
                fns = []
                for nb in range(2):
                    for kc in range(KC):
                        fns.append(lambda e, nb=nb, kc=kc, pv=pv, tt=tt: e.matmul(
                            pv[:, nb * 512:(nb + 1) * 512], lhsT=xnT[:, kc, tt * 128:(tt + 1) * 128],
                            rhs=self.Win[:, kc, 1024 + nb * 512:1024 + (nb + 1) * 512], start=(kc == 0), stop=(kc == KC - 1)))
                sc.pe_group(fns, reads=[xnT, self.Win], writes=[pv])
                g_ = gl[tt]
                sc.op("dve", lambda e, pv=pv, g_=g_: e.tensor_tensor(out=g_.ap, in0=pv.ap, in1=brow.ap, op=ALU.add),
                      reads=[pv, brow], writes=[g_])
                sc.op("act", lambda e, g_=g_, tt=tt: e.activation(out=g_.ap, in_=g_.ap, func=AF.Gelu_apprx_tanh,
                                                                  accum_out=s1[:, tt:tt + 1]),
                      reads=[g_], writes=[g_, s1])
            self.ck(7)
            for c in range(KC):
                pa = psA[ia % 3]
                ia += 1
                fns = [lambda e, kc=kc, c=c, pa=pa: e.matmul(pa.ap, lhsT=self.Win[:, kc, c * 128:(c + 1) * 128], rhs=xnT[:, kc, :],
                                                             start=(kc == 0), stop=(kc == KC - 1)) for kc in range(KC)]
                sc.pe_group(fns, reads=[xnT, self.Win], writes=[pa])
                sc.op("act", lambda e, c=c, pa=pa: e.activation(out=uT[:, c, :], in_=pa.ap, func=AF.Gelu_apprx_tanh,
                                                                bias=self.biascol[:, c:c + 1]),
                      reads=[pa, self.biascol], writes=[uT])
            self.dbg("gl0", gl[0]); self.dbg("uT", uT); self.dbg("s1", s1)
            self.ck(8)
            sc.op("dve", lambda e: e.tensor_scalar(out=nm.ap, in0=s1.ap, scalar1=-1.0 / D, scalar2=None, op0=ALU.mult),
                  reads=[s1], writes=[nm])
            for tt in range(4):
                g_ = gl[tt]
                sc.op("act", lambda e, g_=g_, tt=tt: e.activation(out=self.junk.ap, in_=g_.ap, func=AF.Square,
                                                                  bias=nm[:, tt:tt + 1], accum_out=s2[:, tt:tt + 1]),
                      reads=[g_, nm], writes=[self.junk, s2])
            self.rstd_from_ss(s2, rv, 1.0 / D, lv)
            for tt in range(4):
                g_ = gl[tt]
                sc.op("dve", lambda e, g_=g_, tt=tt: e.tensor_scalar(out=g_.ap, in0=g_.ap, scalar1=nm[:, tt:tt + 1],
                                                                    scalar2=rv[:, tt:tt + 1], op0=ALU.add, op1=ALU.mult),
                      reads=[g_, nm, rv], writes=[g_])
                sc.op("pool", lambda e, g_=g_: e.tensor_tensor(out=g_.ap, in0=g_.ap, in1=lng.ap, op=ALU.mult),
                      reads=[g_, lng], writes=[g_])
                sc.op("pool", lambda e, g_=g_, tt=tt: e.tensor_tensor(out=vln[tt].ap, in0=g_.ap, in1=lnb.ap, op=ALU.add),
                      reads=[g_, lnb], writes=[vln[tt]])
            self.dbg("vln0", vln[0]); self.dbg("rv", rv)
            self.ck(9)
            for c in range(KC):
                pa = psA[ia % 3]
                ia += 1
                fns = [lambda e, kc=kc, c=c, pa=pa: e.matmul(pa.ap, lhsT=self.Win[:, kc, 2048 + c * 128:2048 + (c + 1) * 128],
                                                             rhs=xnT[:, kc, :], start=(kc == 0), stop=(kc == KC - 1))
                       for kc in range(KC)]
                sc.pe_group(fns, reads=[xnT, self.Win], writes=[pa])
                sc.op("act", lambda e, c=c, pa=pa: e.activation(out=szT[:, c, :], in_=pa.ap, func=AF.Silu,
                                                                bias=self.biascol[:, 16 + c:17 + c]),
                      reads=[pa, self.biascol], writes=[szT])
            self.ck(10)
            for c in range(KC):
                g = c // 2
                pa = psA[ia % 3]
                ia += 1
                fns = [lambda e, c=c, g=g, tt=tt, pa=pa: e.matmul(pa[:, tt * 128:(tt + 1) * 128], lhsT=vln[tt][:, c * 128:(c + 1) * 128],
                                                                  rhs=WcT[:, g, :], start=True, stop=True) for tt in range(4)]
                sc.pe_group(fns, reads=vln + [WcT], writes=[pa])
                m_ = mt[c % 2]
                for tt in range(4):
                    sc.op("dve", lambda e, pa=pa, m_=m_, g=g, tt=tt: e.tensor_tensor(
                        out=m_[:, tt * 128:(tt + 1) * 128], in0=pa[:, tt * 128:(tt + 1) * 128], in1=bsb[:, g, :], op=ALU.add),
                        reads=[pa, bsb], writes=[m_])
                sc.op("dve", lambda e, m_=m_, c=c: e.tensor_tensor(out=m_.ap, in0=m_.ap, in1=uT[:, c, :], op=ALU.mult),
                      reads=[m_, uT], writes=[m_])
                sc.op("pool", lambda e, m_=m_, c=c: e.tensor_tensor(out=ogT[:, c, :], in0=m_.ap, in1=szT[:, c, :], op=ALU.mult),
                      reads=[m_, szT], writes=[ogT])
            self.dbg("szT", szT); self.dbg("ogT", ogT)
            self.ck(11)
            self.stageO_full(j, ogT, x_in, x_out, psW)


def col8(v):
    return np.ascontiguousarray(v.reshape(-1, 128).T)


def rep(v, n=128):
    return np.ascontiguousarray(np.broadcast_to(v.reshape(1, -1), (n, v.size)))


def make_in_maps(inputs, layer_ids, S, n_cores):
    f = lambda a: np.ascontiguousarray(np.asarray(a, dtype=np.float32))
    x = f(inputs["x"])
    c = f(inputs["c"])
    shared = {"ident": np.eye(128, dtype=np.float32),
              "tri": np.triu(np.ones((128, 128), dtype=np.float32)),
              "uneg": np.triu(np.full((128, 128), -1.0 / 16, dtype=np.float32)),
              "negmask": np.tril(np.full((128, 128), -30000.0, dtype=np.float32), -1),
              "blockones": np.kron(np.eye(2, dtype=np.float32), np.ones((64, 64), dtype=np.float32)),
              "sel": np.concatenate([np.zeros((64, 64), np.float32), np.ones((1, 64), np.float32)], 0)}
    for L in layer_ids:
        kind, jj = L % 3, L // 3
        p = "L%d_" % L
        bm = f(inputs["b_mod"][L])
        shared[p + "wmod"] = f(inputs["w_mod"][L])
        shared[p + "bmodc"] = col8(bm)
        shared[p + "bmodg"] = rep(bm[2048:3072])
        shared[p + "gprec"] = col8(f(inputs["norm_pre_g"][L]))
        shared[p + "gpost"] = rep(f(inputs["norm_post_g"][L]))
        if kind == 0:
            shared[p + "win"] = f(inputs["gla_w_in"][jj])
            shared[p + "wout"] = f(inputs["gla_w_out"][jj])
            shared[p + "wa2"] = np.ascontiguousarray(np.concatenate([f(inputs["gla_w_a2"][jj]), f(inputs["gla_b_a"][jj])[None]], 0))
            shared[p + "ghc"] = col8(f(inputs["gla_g_head"][jj]).reshape(-1))
        if kind == 2:
            shared[p + "win"] = f(inputs["fox_w_in"][jj])
            shared[p + "wout"] = f(inputs["fox_w_out"][jj])
            shared[p + "gqk"] = np.ascontiguousarray(np.stack([np.tile(f(inputs["fox_g_q"][jj]), 2), np.tile(f(inputs["fox_g_k"][jj]), 2)], 1))
            shared[p + "bf"] = np.ascontiguousarray(f(inputs["fox_b_f"][jj])[:, None])
        if kind == 1:
            shared[p + "win"] = f(inputs["sgu_w_in"][jj])
            shared[p + "wout"] = f(inputs["sgu_w_out"][jj])
            shared[p + "lng"] = rep(f(inputs["sgu_ln_g"][jj]))
            shared[p + "lnb"] = rep(f(inputs["sgu_ln_b"][jj]))
            shared[p + "ws"] = np.ascontiguousarray(f(inputs["sgu_w_s"][jj]).transpose(1, 0, 2))
            shared[p + "bs"] = np.ascontiguousarray(np.broadcast_to(f(inputs["sgu_b_s"][jj])[None], (128, 4, 128)))
    maps = []
    for b in range(n_cores):
        m = dict(shared)
        m["x"] = np.ascontiguousarray(x[b, :S])
        m["ccol"] = col8(c[b])
        maps.append(m)
    return maps


_PROG_CACHE = {}


def run(inputs, layer_ids=(0, 1, 2, 3), S=8192, n_cores=N_CORES):
    key = (S, tuple(layer_ids))
    if key not in _PROG_CACHE:
        pr = Prog(S, layer_ids)
        pr.build()
        _PROG_CACHE[key] = pr
    pr = _PROG_CACHE[key]
    maps = make_in_maps(inputs, layer_ids, S, n_cores)
    res = run_bass_kernel_spmd(pr.nc, maps, core_ids=list(range(n_cores)))
    return np.stack([np.asarray(r["out"]) for r in res.results], axis=0)


def kernel(**inputs):
    return run(inputs).astype(np.float32)
```

```python
import numpy as np
from contextlib import ExitStack
import concourse.bass as bass
import concourse.mybir as mybir
from concourse.bass_utils import run_bass_kernel_spmd

F32 = mybir.dt.float32
BF16 = mybir.dt.bfloat16
AF = mybir.ActivationFunctionType
ALU = mybir.AluOpType

D = 1024
KC = 8
ST = 512
EPS = 1e-6
N_IN = {0: 3088, 1: 3072, 2: 4112}
N_CORES = 8


class Buf:
    __slots__ = ("w", "r", "excl")

    def __init__(self, excl=False):
        self.w = {}
        self.r = {}
        self.excl = excl


class T:
    __slots__ = ("ap", "buf")

    def __init__(self, ap, buf=None):
        self.ap = ap
        self.buf = buf if buf is not None else Buf()

    def __getitem__(self, k):
        return self.ap[k]


def _b(x):
    return x.buf if isinstance(x, T) else x


class _Rec:
    def __init__(self):
        self.call = None

    def __getattr__(self, name):
        def f(*a, **kw):
            assert self.call is None
            self.call = (name, a, kw)
            return None
        return f


def _capture(fn):
    r = _Rec()
    fn(r)
    name, a, kw = r.call
    line = fn.__code__.co_firstlineno

    def replay(eng):
        return getattr(eng, name)(*a, **kw)
    replay.line = line
    return replay


class Sched:
    ENG = ("pe", "act", "dve", "pool", "sp")

    def __init__(self, nc, es):
        self.nc = nc
        self.es = es
        self.q = {e: [] for e in self.ENG}
        self.cnt = {e: 0 for e in self.ENG}
        self.seen = {e: {} for e in self.ENG}
        self.sems = {}
        self.names = {}
        self.dcnt = {}
        for e in self.ENG:
            self.sems[e] = es.enter_context(nc.semaphore("s_" + e))

    def slot(self, name):
        k = "d_" + name
        self.sems[k] = self.es.enter_context(self.nc.semaphore(k))
        self.dcnt[k] = 0
        return k

    @staticmethod
    def _split(reads, writes):
        r2 = [b for b in reads if not _b(b).excl]
        w2 = list(writes) + [b for b in reads if _b(b).excl]
        return r2, w2

    def _waits(self, eng, reads, writes):
        reads, writes = self._split(reads, writes)
        need = {}
        for b in reads:
            for k, v in _b(b).w.items():
                if need.get(k, 0) < v:
                    need[k] = v
        for b in writes:
            bb = _b(b)
            for dct in (bb.w, bb.r):
                for k, v in dct.items():
                    if need.get(k, 0) < v:
                        need[k] = v
        waits = []
        seen = self.seen[eng]
        for k, v in need.items():
            if k in self.dcnt:
                v = self.dcnt[k]
            if k == eng and eng == "pe":
                continue
            if seen.get(k, 0) >= v:
                continue
            seen[k] = v
            waits.append((k, v))
        return waits

    def _record(self, ev, reads, writes):
        reads, writes = self._split(reads, writes)
        k, v = ev
        for b in reads:
            bb = _b(b)
            if bb.r.get(k, 0) < v:
                bb.r[k] = v
        for b in writes:
            bb = _b(b)
            bb.r = {}
            if bb.w.get(k, 0) < v:
                bb.w[k] = v

    def op(self, eng, fn, reads=(), writes=()):
        waits = self._waits(eng, reads, writes)
        self.cnt[eng] += 1
        ev = (eng, self.cnt[eng])
        self.q[eng].append((waits, _capture(fn), (eng, 1)))
        self._record(ev, reads, writes)

    def pe_group(self, fns, reads=(), writes=()):
        waits = self._waits("pe", reads, writes)
        self.cnt["pe"] += 1
        ev = ("pe", self.cnt["pe"])
        n = len(fns)
        for i, fn in enumerate(fns):
            self.q["pe"].append((waits if i == 0 else [], _capture(fn), ("pe", 1) if i == n - 1 else None))
        self._record(ev, reads, writes)

    def dma(self, q, slot, fn, reads=(), writes=()):
        waits = self._waits(q, reads, writes)
        self.dcnt[slot] += 16
        ev = (slot, self.dcnt[slot])
        self.q[q].append((waits, _capture(fn), (slot, 16)))
        self._record(ev, reads, writes)

    def barrier(self):
        for e in self.ENG:
            waits = []
            for k in list(self.ENG) + list(self.dcnt.keys()):
                if k == e:
                    continue
                v = self.cnt[k] if k in self.cnt else self.dcnt[k]
                if v > 0 and self.seen[e].get(k, 0) < v:
                    self.seen[e][k] = v
                    waits.append((k, v))
            if waits:
                self.q[e].append((waits, None, None))

    def emit(self):
        nc = self.nc

        def replay(name, eng):
            for waits, fn, inc in self.q[name]:
                for k, v in waits:
                    eng.wait_ge(self.sems[k], v)
                if fn is None:
                    continue
                ins = fn(eng)
                try:
                    self.names[ins.ins.name] = (name, fn.line)
                except Exception:
                    pass
                if inc is not None:
                    ins.then_inc(self.sems[inc[0]], inc[1])

        with nc.Block() as block:
            @block.sync
            def _(e):
                replay("sp", e)

            @block.tensor
            def _(e):
                replay("pe", e)

            @block.scalar
            def _(e):
                replay("act", e)

            @block.vector
            def _(e):
                replay("dve", e)

            @block.gpsimd
            def _(e):
                replay("pool", e)


class Arena:
    def __init__(self, ap, size):
        self.ap = ap
        self.size = size
        self.off = 0

    def mark(self):
        return self.off

    def reset(self, m):
        self.off = m

    def alloc(self, shape, dtype=F32):
        n = int(np.prod(shape))
        n32 = n if dtype == F32 else (n + 1) // 2
        assert self.off + n32 <= self.size, ("SBUF arena overflow", self.off, n32, self.size)
        v = self.ap[:, self.off:self.off + n32]
        self.off += n32
        if dtype == BF16:
            v = v.bitcast(BF16)
        if len(shape) == 2:
            v = v.rearrange("p (a b) -> p a b", a=shape[0])
        elif len(shape) == 3:
            v = v.rearrange("p (a b c) -> p a b c", a=shape[0], b=shape[1])
        return T(v)


class _Stop(Exception):
    pass


STOP_AT = [None]
STQ = "pool"
DEBUG = [False]


class Prog:
    def dbg(self, name, t):
        if not DEBUG[0]:
            return
        nm = "dbg_%s_%d" % (name, len(self.dbg_names))
        self.dbg_names.append(nm)
        shp = list(t.ap.shape)
        d = self.nc.dram_tensor(nm, shp, t.ap.dtype, kind="ExternalOutput").ap()
        sl = self.sc.slot(nm)
        self.sc.dma("sp", sl, lambda e: e.dma_start(out=d, in_=t.ap), reads=[t])

    def ck(self, n):
        if STOP_AT[0] is not None and n >= STOP_AT[0]:
            raise _Stop()

    def __init__(self, S, layer_ids):
        self.S = S
        self.layer_ids = list(layer_ids)
        self.nST = S // ST
        self.in_names = []
        self.dbg_names = []
        nc = self.nc = bass.Bass("TRN2", target_bir_lowering=False)
        self.dram = {}
        self._din("x", [S, D])
        self._din("ccol", [128, 8])
        self._din("ident", [128, 128])
        self._din("tri", [128, 128])
        self._din("uneg", [128, 128])
        self._din("negmask", [128, 128])
        self._din("blockones", [128, 128])
        self._din("sel", [65, 64])
        for L in self.layer_ids:
            kind = L % 3
            p = "L%d_" % L
            self._din(p + "wmod", [D, 3 * D])
            self._din(p + "bmodc", [128, 24])
            self._din(p + "bmodg", [128, D])
            self._din(p + "gprec", [128, 8])
            self._din(p + "gpost", [128, D])
            self._din(p + "win", [D, N_IN[kind]])
            self._din(p + "wout", [D, D])
            if kind == 0:
                self._din(p + "wa2", [17, 512])
                self._din(p + "ghc", [128, 8])
            if kind == 2:
                self._din(p + "gqk", [128, 2])
                self._din(p + "bf", [16, 1])
            if kind == 1:
                self._din(p + "lng", [128, D])
                self._din(p + "lnb", [128, D])
                self._din(p + "ws", [128, 4, 128])
                self._din(p + "bs", [128, 4, 128])
        self.out = nc.dram_tensor("out", [S, D], F32, kind="ExternalOutput").ap()
        self.xs = [nc.dram_tensor("xs%d" % i, [S, D], F32, kind="Internal").ap() for i in range(2)]
        if any(L % 3 == 2 for L in self.layer_ids):
            self.QA = nc.dram_tensor("fox_qa", [16, 70, S], BF16, kind="Internal").ap()
            self.KA = nc.dram_tensor("fox_ka", [16, 70, S], BF16, kind="Internal").ap()
            self.VS = nc.dram_tensor("fox_v", [S, D], BF16, kind="Internal").ap()
            self.SZ = nc.dram_tensor("fox_sz", [D, S], BF16, kind="Internal").ap()
            self.OG = nc.dram_tensor("fox_og", [D, S], BF16, kind="Internal").ap()

    def _din(self, name, shape, dtype=F32):
        self.in_names.append(name)
        self.dram[name] = self.nc.dram_tensor(name, shape, dtype, kind="ExternalInput").ap()

    def rstd_from_ss(self, ss, out, n_inv, tmp):
        sc = self.sc
        sc.op("act", lambda e: e.activation(out=tmp.ap, in_=ss.ap, func=AF.Ln, scale=n_inv, bias=EPS),
              reads=[ss], writes=[tmp])
        sc.op("act", lambda e: e.activation(out=out.ap, in_=tmp.ap, func=AF.Exp, scale=-0.5),
              reads=[tmp], writes=[out])

    def build(self):
        nc = self.nc
        with ExitStack() as es:
            arena_t = es.enter_context(nc.sbuf_tensor("arena", [128, 53000], F32))
            ps_t = es.enter_context(nc.psum_tensor("ps", [128, 8, 512], F32))
            self.sc = sc = Sched(nc, es)
            self.A = A = Arena(arena_t[:, :], 53000)
            self.ps = ps_t
            self.psbuf = [Buf(excl=True) for _ in range(8)]
            self.slots = {}
            for nm in ["c0", "stg0", "stg1", "x0", "x1", "x2", "x3", "xo0", "xo1", "xo2", "xo3", "xs0", "xs1", "xs2", "xs3", "sm0", "sm1", "sm2", "sm3",
                       "fq0", "fq1", "fv0", "fv1", "fz0", "ff0", "fk0", "fk1", "fvh0", "fvh1", "fqq0", "fqq1", "fqq2",
                       "fsz0", "fsz1", "fog0", "fog1", "fo0", "fo1"]:
                self.slots[nm] = sc.slot(nm)
            self.global_consts()
            x_in = self.dram["x"]
            try:
                for li, L in enumerate(self.layer_ids):
                    x_out = self.out if li == len(self.layer_ids) - 1 else self.xs[li % 2]
                    m = A.mark()
                    self.layer(L, x_in, x_out)
                    sc.barrier()
                    A.reset(m)
                    x_in = x_out
            except _Stop:
                pass
            sc.barrier()
            sc.emit()
        return nc

    def pst(self, bank, n=1):
        if n == 1:
            return self.ps[:, bank, :]
        return self.ps[:, bank:bank + n, :].rearrange("p a b -> p (a b)")

    def global_consts(self):
        sc, A, dr = self.sc, self.A, self.dram
        sl = self.slots["c0"]
        self.identf = A.alloc([128])
        self.identb = A.alloc([128], BF16)
        self.trif = A.alloc([128])
        self.unegf = A.alloc([128])
        self.onesb = A.alloc([128], BF16)
        self.ones = A.alloc([128])
        self.cond = A.alloc([8])
        self.cond_rep = A.alloc([8, 128])
        cc = A.alloc([8])
        sc.dma("sp", sl, lambda e: e.dma_start(out=self.identf.ap, in_=dr["ident"]), writes=[self.identf])
        sc.dma("sp", sl, lambda e: e.dma_start(out=self.trif.ap, in_=dr["tri"]), writes=[self.trif])
        sc.dma("sp", sl, lambda e: e.dma_start(out=self.unegf.ap, in_=dr["uneg"]), writes=[self.unegf])
        sc.dma("sp", sl, lambda e: e.dma_start(out=cc.ap, in_=dr["ccol"]), writes=[cc])
        sc.barrier()
        sc.op("dve", lambda e: e.tensor_copy(out=self.identb.ap, in_=self.identf.ap), reads=[self.identf], writes=[self.identb])
        sc.op("dve", lambda e: e.memset(self.ones.ap, 1.0), writes=[self.ones])
        sc.op("dve", lambda e: e.memset(self.onesb.ap, 1.0), writes=[self.onesb])
        sc.op("act", lambda e: e.activation(out=self.cond.ap, in_=cc.ap, func=AF.Silu), reads=[cc], writes=[self.cond])
        for kc in range(KC):
            sc.op("dve", lambda e, kc=kc: e.tensor_scalar(out=self.cond_rep[:, kc, :], in0=self.ones.ap,
                                                         scalar1=self.cond[:, kc:kc + 1], scalar2=None, op0=ALU.mult),
                  reads=[self.ones, self.cond], writes=[self.cond_rep])

    def prep(self, L, tokmajor_ranges, nocol_ranges=None, wout_rowscale=None):
        sc, A, dr = self.sc, self.A, self.dram
        kind = L % 3
        p = "L%d_" % L
        nin = N_IN[kind]
        BW = 256
        if nocol_ranges is None:
            nocol_ranges = tokmajor_ranges
        self.Win = A.alloc([KC, nin], BF16)
        self.Wout = A.alloc([KC, D], BF16)
        self.Gbc = A.alloc([D])
        nch = (nin + 127) // 128
        self.biascol = A.alloc([nch])
        self.biasrow = {r: A.alloc([r[1] - r[0]]) for r in tokmajor_ranges}
        tmp_mark = A.mark()
        stg = [A.alloc([KC, BW]) for _ in range(2)]
        stg_slot = [self.slots["stg0"], self.slots["stg1"]]
        modc = A.alloc([16])
        acol = A.alloc([8])
        shift_rep = A.alloc([KC, 128])
        small = A.alloc([24 + 8])
        bmodc, gprec = T(small[:, 0:24], small.buf), T(small[:, 24:32], small.buf)
        gtmp = A.alloc([D])
        gpost = A.alloc([D])
        sl = self.slots["c0"]
        sc.dma("sp", sl, lambda e: e.dma_start(out=bmodc.ap, in_=dr[p + "bmodc"]), writes=[small])
        sc.dma("sp", sl, lambda e: e.dma_start(out=gprec.ap, in_=dr[p + "gprec"]), writes=[small])
        sc.dma("sp", sl, lambda e: e.dma_start(out=gtmp.ap, in_=dr[p + "bmodg"]), writes=[gtmp])
        sc.dma("sp", sl, lambda e: e.dma_start(out=gpost.ap, in_=dr[p + "gpost"]), writes=[gpost])
        sc.barrier()
        blk = [0]

        def load_block(src, c0, w):
            i = blk[0] % 2
            blk[0] += 1
            s = stg[i]
            sc.dma("sp", stg_slot[i],
                   lambda e: e.dma_start(out=s[:, :, 0:w], in_=src[:, c0:c0 + w].rearrange("(kc p) n -> p kc n", p=128)),
                   writes=[s])
            return s

        PB_MOD, PB_G, PB_BC, PB_BR = 0, 1, 3, 4
        psmod = T(self.pst(PB_MOD), self.psbuf[PB_MOD])
        psG = [T(self.pst(PB_G + i), self.psbuf[PB_G + i]) for i in range(2)]
        wmod = dr[p + "wmod"]
        for j in range(12):
            s = load_block(wmod, j * BW, BW)
            if j < 8:
                fns = []
                for h in range(2):
                    ch = j * 2 + h
                    for kc in range(KC):
                        fns.append(lambda e, ch=ch, h=h, kc=kc, s=s: e.matmul(
                            psmod[:, ch:ch + 1], lhsT=s[:, kc, h * 128:(h + 1) * 128], rhs=self.cond[:, kc:kc + 1],
                            start=(kc == 0), stop=(kc == KC - 1)))
                sc.pe_group(fns, reads=[s, self.cond], writes=[psmod])
            else:
                g = j - 8
                fns = [lambda e, kc=kc, s=s, g=g: e.matmul(
                    psG[g // 2][:, (g % 2) * BW:(g % 2 + 1) * BW], lhsT=self.cond_rep[:, kc, :], rhs=s[:, kc, :],
                    start=(kc == 0), stop=(kc == KC - 1)) for kc in range(KC)]
                sc.pe_group(fns, reads=[s, self.cond_rep], writes=[psG[g // 2]])
        self.ck(1)
        sc.op("dve", lambda e: e.tensor_tensor(out=modc.ap, in0=psmod[:, 0:16], in1=bmodc[:, 0:16], op=ALU.add),
              reads=[psmod, small], writes=[modc])
        sc.op("dve", lambda e: e.scalar_tensor_tensor(out=acol.ap, in0=modc[:, 8:16], scalar=1.0, in1=gprec.ap,
                                                      op0=ALU.add, op1=ALU.mult),
              reads=[modc, small], writes=[acol])
        for kc in range(KC):
            sc.op("dve", lambda e, kc=kc: e.tensor_scalar(out=shift_rep[:, kc, :], in0=self.ones.ap,
                                                         scalar1=modc[:, kc:kc + 1], scalar2=None, op0=ALU.mult),
                  reads=[self.ones, modc], writes=[shift_rep])
        for i in range(2):
            sc.op("dve", lambda e, i=i: e.tensor_tensor(out=gtmp[:, i * 512:(i + 1) * 512], in0=psG[i].ap,
                                                       in1=gtmp[:, i * 512:(i + 1) * 512], op=ALU.add),
                  reads=[psG[i], gtmp], writes=[gtmp])
        sc.op("dve", lambda e: e.tensor_tensor(out=self.Gbc.ap, in0=gtmp.ap, in1=gpost.ap, op=ALU.mult),
              reads=[gtmp, gpost], writes=[self.Gbc])
        self.dbg("modc", modc); self.dbg("acol", acol); self.dbg("Gbc", self.Gbc)
        self.ck(2)
        win = dr[p + "win"]
        psbc = T(self.pst(PB_BC), self.psbuf[PB_BC])
        psbr = [T(self.pst(PB_BR + i), self.psbuf[PB_BR + i]) for i in range(2)]
        written = []
        c0 = 0
        tog = 0
        while c0 < nin:
            w = min(BW, nin - c0)
            s = load_block(win, c0, w)
            rng = None
            for r in tokmajor_ranges:
                if r[0] <= c0 < r[1]:
                    rng = r
            if rng is not None:
                pb = psbr[tog % 2]
                tog += 1
                fns = [lambda e, kc=kc, s=s, pb=pb, w=w: e.matmul(pb[:, 0:w], lhsT=shift_rep[:, kc, :], rhs=s[:, kc, 0:w],
                                                                   start=(kc == 0), stop=(kc == KC - 1)) for kc in range(KC)]
                sc.pe_group(fns, reads=[s, shift_rep], writes=[pb])
                br = self.biasrow[rng]
                o = c0 - rng[0]
                sc.op("act", lambda e, br=br, o=o, w=w, pb=pb: e.copy(out=br[:, o:o + w], in_=pb[:, 0:w]),
                      reads=[pb], writes=[br])
            if not any(r[0] <= c0 < r[1] for r in nocol_ranges):
                fns = []
                for h in range((w + 127) // 128):
                    ch = c0 // 128 + h
                    hw = min(128, w - h * 128)
                    written.append((ch, hw))
                    for kc in range(KC):
                        fns.append(lambda e, ch=ch, h=h, hw=hw, kc=kc, s=s: e.matmul(
                            psbc[0:hw, ch:ch + 1], lhsT=s[:, kc, h * 128:h * 128 + hw], rhs=modc[:, kc:kc + 1],
                            start=(kc == 0), stop=(kc == KC - 1)))
                sc.pe_group(fns, reads=[s, modc], writes=[psbc])
            for kc in range(KC):
                eng = "dve" if kc % 2 == 0 else "pool"
                if eng == "dve":
                    sc.op(eng, lambda e, kc=kc, s=s, c0=c0, w=w: e.tensor_scalar(
                        out=self.Win[:, kc, c0:c0 + w], in0=s[:, kc, 0:w], scalar1=acol[:, kc:kc + 1], scalar2=None,
                        op0=ALU.mult), reads=[s, acol], writes=[self.Win])
                else:
                    sc.op(eng, lambda e, kc=kc, s=s, c0=c0, w=w: e.tensor_scalar(
                        out=self.Win[:, kc, c0:c0 + w], in0=s[:, kc, 0:w], scalar1=acol[:, kc:kc + 1], scalar2=1.0,
                        op0=ALU.mult, op1=ALU.mult), reads=[s, acol], writes=[self.Win])
            c0 += w
        for ch, hw in written:
            sc.op("dve", lambda e, ch=ch, hw=hw: e.tensor_copy(out=self.biascol[0:hw, ch:ch + 1], in_=psbc[0:hw, ch:ch + 1]),
                  reads=[psbc], writes=[self.biascol])
        self.dbg("Win", self.Win); self.dbg("biascol", T(self.biascol[:, 0:8], self.biascol.buf))
        for r_, t_ in self.biasrow.items():
            self.dbg("biasrow", t_)
        self.ck(3)
        wout = dr[p + "wout"]
        for j in range(D // BW):
            s = load_block(wout, j * BW, BW)
            if wout_rowscale is None:
                sc.op("dve", lambda e, s=s, j=j: e.tensor_copy(out=self.Wout[:, 0:4, j * BW:(j + 1) * BW], in_=s[:, 0:4, :]),
                      reads=[s], writes=[self.Wout])
                sc.op("act", lambda e, s=s, j=j: e.copy(out=self.Wout[:, 4:8, j * BW:(j + 1) * BW], in_=s[:, 4:8, :]),
                      reads=[s], writes=[self.Wout])
            else:
                for kc in range(KC):
                    sc.op("dve", lambda e, s=s, j=j, kc=kc: e.tensor_scalar(
                        out=self.Wout[:, kc, j * BW:(j + 1) * BW], in0=s[:, kc, :], scalar1=wout_rowscale[:, kc:kc + 1],
                        scalar2=None, op0=ALU.mult), reads=[s, wout_rowscale], writes=[self.Wout])
        sc.barrier()
        A.reset(tmp_mark)

    def alloc_stageA(self, nxt=4, nxnT=2):
        A = self.A
        self.xt = [A.alloc([D]) for _ in range(nxt)]
        self.xn = [A.alloc([D], BF16) for _ in range(2)]
        self.xnT = [A.alloc([KC, ST], BF16) for _ in range(nxnT)]
        self.junk = A.alloc([D], BF16)
        self.ssA = [A.alloc([4]) for _ in range(2)]
        self.lnA = [A.alloc([4]) for _ in range(2)]
        self.rsA = [A.alloc([4]) for _ in range(2)]
        self.psT = T(self.ps[:, 7, :].bitcast(BF16).rearrange("p (a b) -> p a b", a=8), self.psbuf[7])

    def stageA(self, j, x_in):
        sc = self.sc
        ss, ln, rs, xnT = self.ssA[j % 2], self.lnA[j % 2], self.rsA[j % 2], self.xnT[j % len(self.xnT)]
        nxt = len(self.xt)
        for tt in range(4):
            tok = j * ST + tt * 128
            xt = self.xt[tt % nxt]
            xn = self.xn[tt % 2]
            sc.dma("sp", self.slots["x%d" % (tt % nxt)], lambda e, xt=xt, tok=tok: e.dma_start(out=xt.ap, in_=x_in[tok:tok + 128, :]),
                   writes=[xt])
            sc.op("act", lambda e, xt=xt, tt=tt, ss=ss: e.activation(out=self.junk.ap, in_=xt.ap, func=AF.Square,
                                                                     accum_out=ss[:, tt:tt + 1]),
                  reads=[xt], writes=[self.junk, ss])
            sc.op("act", lambda e, tt=tt: e.activation(out=ln[:, tt:tt + 1], in_=ss[:, tt:tt + 1], func=AF.Ln, scale=1.0 / D, bias=EPS),
                  reads=[ss], writes=[ln])
            sc.op("act", lambda e, tt=tt: e.activation(out=rs[:, tt:tt + 1], in_=ln[:, tt:tt + 1], func=AF.Exp, scale=-0.5),
                  reads=[ln], writes=[rs])
            sc.op("act", lambda e, xt=xt, xn=xn, tt=tt, rs=rs: e.activation(out=xn.ap, in_=xt.ap, func=AF.Copy,
                                                                            scale=rs[:, tt:tt + 1]),
                  reads=[xt, rs], writes=[xn])
            fns = [lambda e, c=c, xn=xn: e.transpose(out=self.psT[:, c, :], in_=xn[:, c * 128:(c + 1) * 128],
                                                     identity=self.identb.ap) for c in range(KC)]
            sc.pe_group(fns, reads=[xn, self.identb], writes=[self.psT])
            sc.op("dve", lambda e, tt=tt, xnT=xnT: e.tensor_copy(out=xnT[:, :, tt * 128:(tt + 1) * 128], in_=self.psT.ap),
                  reads=[self.psT], writes=[xnT])
        return xnT

    def alloc_stageO(self, nxo=4, nt1=2):
        A = self.A
        self.xo = [A.alloc([D]) for _ in range(nxo)]
        self.t1 = [A.alloc([D]) for _ in range(nt1)]
        self.ssO = [A.alloc([4]) for _ in range(2)]
        self.lnO = [A.alloc([4]) for _ in range(2)]
        self.rsO = [A.alloc([4]) for _ in range(2)]

    def stageO(self, j, ogT, x_in, x_out, psY):
        sc = self.sc
        ss, ln, rs = self.ssO[j % 2], self.lnO[j % 2], self.rsO[j % 2]
        for tt in range(4):
            tok = j * ST + tt * 128
            py = psY[tt % 2]
            xo = self.xo[tt % 2]
            t1 = self.t1[tt % 2]
            sc.dma("sp", self.slots["xo%d" % (tt % 2)], lambda e, xo=xo, tok=tok: e.dma_start(out=xo.ap, in_=x_in[tok:tok + 128, :]),
                   writes=[xo])
            fns = []
            for nb in range(2):
                for c in range(KC):
                    fns.append(lambda e, nb=nb, c=c, py=py, tt=tt: e.matmul(
                        py[:, nb * 512:(nb + 1) * 512], lhsT=ogT[:, c, tt * 128:(tt + 1) * 128],
                        rhs=self.Wout[:, c, nb * 512:(nb + 1) * 512], start=(c == 0), stop=(c == KC - 1)))
            sc.pe_group(fns, reads=[ogT, self.Wout], writes=[py])
            sc.op("act", lambda e, py=py, tt=tt, ss=ss: e.activation(out=self.junk.ap, in_=py.ap, func=AF.Square,
                                                                     accum_out=ss[:, tt:tt + 1]),
                  reads=[py], writes=[self.junk, ss])
            sc.op("dve", lambda e, py=py, t1=t1: e.tensor_tensor(out=t1.ap, in0=py.ap, in1=self.Gbc.ap, op=ALU.mult),
                  reads=[py, self.Gbc], writes=[t1])
        self.rstd_from_ss(ss, rs, 1.0 / D, ln)
        return ss, rs

    def stageO_full(self, j, ogT, x_in, x_out, psY):
        sc = self.sc
        for tt in range(4):
            tok = j * ST + tt * 128
            k = (j * 4 + tt) % 2
            py = psY[k]
            xi = tt % len(self.xo)
            xo = self.xo[xi]
            t1 = self.t1[k % len(self.t1)]
            ss, ln, rs = self.ssO[k], self.lnO[k], self.rsO[k]
            sc.dma("sp", self.slots["xo%d" % xi], lambda e, xo=xo, tok=tok: e.dma_start(out=xo.ap, in_=x_in[tok:tok + 128, :]),
                   writes=[xo])
            fns = []
            for nb in range(2):
                for c in range(KC):
                    fns.append(lambda e, nb=nb, c=c, py=py, tt=tt: e.matmul(
                        py[:, nb * 512:(nb + 1) * 512], lhsT=ogT[:, c, tt * 128:(tt + 1) * 128],
                        rhs=self.Wout[:, c, nb * 512:(nb + 1) * 512], start=(c == 0), stop=(c == KC - 1)))
            self.ck(12)
            sc.pe_group(fns, reads=[ogT, self.Wout], writes=[py])
            self.ck(13)
            for nb in range(2):
                sc.op("act", lambda e, py=py, ss=ss, nb=nb: e.activation(out=self.junk[:, nb * 512:(nb + 1) * 512], in_=py[:, nb * 512:(nb + 1) * 512],
                                                                         func=AF.Square, accum_out=ss[:, 1 + nb:2 + nb]),
                      reads=[py], writes=[self.junk, ss])
                sc.op("dve", lambda e, py=py, t1=t1, nb=nb: e.tensor_tensor(out=t1[:, nb * 512:(nb + 1) * 512], in0=py[:, nb * 512:(nb + 1) * 512],
                                                                            in1=self.Gbc[:, nb * 512:(nb + 1) * 512], op=ALU.mult),
                      reads=[py, self.Gbc], writes=[t1])
            self.ck(14)
            sc.op("dve", lambda e, ss=ss: e.tensor_tensor(out=ss[:, 0:1], in0=ss[:, 1:2], in1=ss[:, 2:3], op=ALU.add),
                  reads=[ss], writes=[ss])
            sc.op("act", lambda e, ss=ss, ln=ln: e.activation(out=ln[:, 0:1], in_=ss[:, 0:1], func=AF.Ln, scale=1.0 / D, bias=EPS),
                  reads=[ss], writes=[ln])
            sc.op("act", lambda e, rs=rs, ln=ln: e.activation(out=rs[:, 0:1], in_=ln[:, 0:1], func=AF.Exp, scale=-0.5),
                  reads=[ln], writes=[rs])
            self.ck(15)
            sc.op("dve", lambda e, t1=t1, xo=xo, rs=rs: e.scalar_tensor_tensor(out=xo.ap, in0=t1.ap, scalar=rs[:, 0:1], in1=xo.ap,
                                                                               op0=ALU.mult, op1=ALU.add),
                  reads=[t1, xo, rs], writes=[xo])
            self.ck(16)
            sc.dma(STQ, self.slots["xs%d" % xi], lambda e, xo=xo, tok=tok: e.dma_start(out=x_out[tok:tok + 128, :], in_=xo.ap),
                   reads=[xo])

    def layer(self, L, x_in, x_out):
        kind = L % 3
        if kind == 1:
            self.layer_sgu(L, x_in, x_out)
        elif kind == 0:
            self.layer_gla(L, x_in, x_out)
        else:
            self.layer_fox(L, x_in, x_out)


    def layer_gla(self, L, x_in, x_out):
        sc, A, dr = self.sc, self.A, self.dram
        p = "L%d_" % L
        ghc = A.alloc([8])
        sl = self.slots["c0"]
        sc.dma("sp", sl, lambda e: e.dma_start(out=ghc.ap, in_=dr[p + "ghc"]), writes=[ghc])
        sc.barrier()
        self.prep(L, [(512, 2048)], nocol_ranges=[(1024, 2048)], wout_rowscale=ghc)
        self.alloc_stageA(nxt=2, nxnT=1)
        self.alloc_stageO(nxo=2, nt1=1)
        wa2 = A.alloc([512])
        sc.dma("sp", sl, lambda e: e.dma_start(out=wa2[0:17, :], in_=dr[p + "wa2"]), writes=[wa2])
        sc.barrier()
        alT = A.alloc([ST])
        sc.op("dve", lambda e: e.memset(alT[0:17, :], 1.0), writes=[alT])
        f512 = A.alloc([512])
        spt = [A.alloc([512]) for _ in range(2)]
        enb = [A.alloc([512]) for _ in range(2)]
        ebT = A.alloc([4, ST])
        enbT = A.alloc([4, ST])
        elast = [A.alloc([4]) for _ in range(4)]
        qT = A.alloc([4, ST], BF16)
        kT = A.alloc([4, ST], BF16)
        ktok = [A.alloc([512], BF16) for _ in range(4)]
        vtok = [A.alloc([D], BF16) for _ in range(4)]
        szT = A.alloc([KC, ST], BF16)
        ogT = A.alloc([KC, ST], BF16)
        ATs = A.alloc([4, 128], BF16)
        osq = A.alloc([8, 128], BF16)
        rstd = A.alloc([4, 128])
        otmp = A.alloc([8, 128], BF16)
        Sst = A.alloc([4, 256])
        Sbf = A.alloc([4, 256], BF16)
        sc.op("dve", lambda e: e.memset(Sst.ap, 0.0), writes=[Sst])
        sc.op("dve", lambda e: e.memset(Sbf.ap, 0.0), writes=[Sbf])
        psW = [T(self.pst(0, 2), self.psbuf[0]), T(self.pst(2, 2), self.psbuf[2])]
        psA = [T(self.pst(4 + i), self.psbuf[4 + i]) for i in range(3)]
        brow = self.biasrow[(512, 2048)]
        LNS = -0.5 * float(np.log(128.0))
        ia = [0]
        iw = [0]

        def nextA():
            t = psA[ia[0] % 3]
            ia[0] += 1
            return t

        def nextW():
            t = psW[iw[0] % 2]
            iw[0] += 1
            return t

        for j in range(self.nST):
            xnT = self.stageA(j, x_in)
            pa = nextA()
            fns = [lambda e, kc=kc, pa=pa: e.matmul(pa[0:16, :], lhsT=self.Win[:, kc, 3072:3088], rhs=xnT[:, kc, :],
                                                    start=(kc == 0), stop=(kc == KC - 1)) for kc in range(KC)]
            sc.pe_group(fns, reads=[xnT, self.Win], writes=[pa])
            sc.op("act", lambda e, pa=pa: e.activation(out=alT[0:16, :], in_=pa[0:16, :], func=AF.Identity,
                                                       bias=self.biascol[0:16, 24:25]),
                  reads=[pa, self.biascol], writes=[alT])
            for tt in range(4):
                ts_ = slice(tt * 128, (tt + 1) * 128)
                pa = nextA()
                sc.pe_group([lambda e, pa=pa, ts_=ts_: e.matmul(pa.ap, lhsT=alT[0:17, ts_], rhs=wa2[0:17, :], start=True, stop=True)],
                            reads=[alT, wa2], writes=[pa])
                sp_ = spt[tt % 2]
                sc.op("act", lambda e, pa=pa: e.activation(out=f512.ap, in_=pa.ap, func=AF.Exp, scale=-1.0),
                      reads=[pa], writes=[f512])
                sc.op("act", lambda e, sp_=sp_: e.activation(out=sp_.ap, in_=f512.ap, func=AF.Ln, bias=1.0),
                      reads=[f512], writes=[sp_])
                pb = nextA()
                sc.pe_group([lambda e, pb=pb, sp_=sp_: e.matmul(pb.ap, lhsT=self.unegf.ap, rhs=sp_.ap, start=True, stop=True)],
                            reads=[self.unegf, sp_], writes=[pb])
                en_ = enb[tt % 2]
                sc.op("act", lambda e, pb=pb, en_=en_: e.activation(out=en_.ap, in_=pb.ap, func=AF.Exp, scale=-1.0),
                      reads=[pb], writes=[en_])
                pc = nextA()
                fns = [lambda e, h=h, pc=pc, sp_=sp_: e.matmul(pc[:, h * 128:(h + 1) * 128], lhsT=sp_[:, h * 128:(h + 1) * 128],
                                                               rhs=self.unegf.ap, start=True, stop=True) for h in range(4)]
                sc.pe_group(fns, reads=[self.unegf, sp_], writes=[pc])
                pc3 = pc.ap.rearrange("p (h t) -> p h t", h=4)
                sc.op("act", lambda e, pc3=pc3, ts_=ts_: e.activation(out=ebT[:, :, ts_], in_=pc3, func=AF.Exp, bias=LNS),
                      reads=[pc], writes=[ebT])
                sc.op("act", lambda e, pc3=pc3, ts_=ts_: e.activation(out=enbT[:, :, ts_], in_=pc3, func=AF.Exp, scale=-1.0),
                      reads=[pc], writes=[enbT])
                el = elast[tt]
                sc.op("act", lambda e, pc3=pc3, el=el: e.activation(out=el.ap, in_=pc3[:, :, 127], func=AF.Exp),
                      reads=[pc], writes=[el])
                pk = nextA()
                fns = [lambda e, kc=kc, pk=pk, ts_=ts_: e.matmul(pk.ap, lhsT=xnT[:, kc, ts_], rhs=self.Win[:, kc, 512:1024],
                                                                 start=(kc == 0), stop=(kc == KC - 1)) for kc in range(KC)]
                sc.pe_group(fns, reads=[xnT, self.Win], writes=[pk])
                sc.op("dve", lambda e, pk=pk: e.tensor_tensor(out=f512.ap, in0=pk.ap, in1=brow[:, 0:512], op=ALU.add),
                      reads=[pk, brow], writes=[f512])
                sc.op("pool", lambda e, tt=tt, en_=en_: e.tensor_tensor(out=ktok[tt].ap, in0=f512.ap, in1=en_.ap, op=ALU.mult),
                      reads=[f512, en_], writes=[ktok[tt]])
                pv = nextW()
                fns = []
                for nb in range(2):
                    for kc in range(KC):
                        fns.append(lambda e, nb=nb, kc=kc, pv=pv, ts_=ts_: e.matmul(
                            pv[:, nb * 512:(nb + 1) * 512], lhsT=xnT[:, kc, ts_],
                            rhs=self.Win[:, kc, 1024 + nb * 512:1024 + (nb + 1) * 512], start=(kc == 0), stop=(kc == KC - 1)))
                sc.pe_group(fns, reads=[xnT, self.Win], writes=[pv])
                sc.op("dve", lambda e, pv=pv, tt=tt: e.tensor_tensor(out=vtok[tt].ap, in0=pv.ap, in1=brow[:, 512:1536], op=ALU.add),
                      reads=[pv, brow], writes=[vtok[tt]])
            for h in range(4):
                pa = nextA()
                fns = [lambda e, kc=kc, pa=pa, h=h: e.matmul(pa.ap, lhsT=self.Win[:, kc, h * 128:(h + 1) * 128], rhs=xnT[:, kc, :],
                                                             start=(kc == 0), stop=(kc == KC - 1)) for kc in range(KC)]
                sc.pe_group(fns, reads=[xnT, self.Win], writes=[pa])
                sc.op("dve", lambda e, pa=pa, h=h: e.scalar_tensor_tensor(out=qT[:, h, :], in0=pa.ap, scalar=self.biascol[:, h:h + 1],
                                                                          in1=ebT[:, h, :], op0=ALU.add, op1=ALU.mult),
                      reads=[pa, self.biascol, ebT], writes=[qT])
                pa = nextA()
                fns = [lambda e, kc=kc, pa=pa, h=h: e.matmul(pa.ap, lhsT=self.Win[:, kc, 512 + h * 128:512 + (h + 1) * 128],
                                                             rhs=xnT[:, kc, :], start=(kc == 0), stop=(kc == KC - 1))
                       for kc in range(KC)]
                sc.pe_group(fns, reads=[xnT, self.Win], writes=[pa])
                sc.op("dve", lambda e, pa=pa, h=h: e.scalar_tensor_tensor(out=kT[:, h, :], in0=pa.ap, scalar=self.biascol[:, 4 + h:5 + h],
                                                                          in1=enbT[:, h, :], op0=ALU.add, op1=ALU.mult),
                      reads=[pa, self.biascol, enbT], writes=[kT])
            for c in range(KC):
                pa = nextA()
                fns = [lambda e, kc=kc, c=c, pa=pa: e.matmul(pa.ap, lhsT=self.Win[:, kc, 2048 + c * 128:2048 + (c + 1) * 128],
                                                             rhs=xnT[:, kc, :], start=(kc == 0), stop=(kc == KC - 1))
                       for kc in range(KC)]
                sc.pe_group(fns, reads=[xnT, self.Win], writes=[pa])
                sc.op("act", lambda e, c=c, pa=pa: e.activation(out=szT[:, c, :], in_=pa.ap, func=AF.Silu,
                                                                bias=self.biascol[:, 16 + c:17 + c]),
                      reads=[pa, self.biascol], writes=[szT])
            for tt in range(4):
                ts_ = slice(tt * 128, (tt + 1) * 128)
                pa = nextA()
                fns = [lambda e, h=h, pa=pa, ts_=ts_: e.matmul(pa[:, h * 128:(h + 1) * 128], lhsT=kT[:, h, ts_], rhs=qT[:, h, ts_],
                                                               start=True, stop=True) for h in range(4)]
                sc.pe_group(fns, reads=[kT, qT], writes=[pa])
                for h in range(4):
                    sc.op("dve", lambda e, h=h, pa=pa: e.tensor_tensor(out=ATs[:, h, :], in0=pa[:, h * 128:(h + 1) * 128],
                                                                       in1=self.trif.ap, op=ALU.mult),
                          reads=[pa, self.trif], writes=[ATs])
                po = nextW()
                fns = []
                for h in range(4):
                    for half in range(2):
                        c = 2 * h + half
                        fns.append(lambda e, h=h, c=c, half=half, po=po, tt=tt: e.matmul(
                            po[:, c * 128:(c + 1) * 128], lhsT=vtok[tt][:, c * 128:(c + 1) * 128], rhs=ATs[:, h, :],
                            start=True, stop=False))
                        fns.append(lambda e, h=h, c=c, half=half, po=po, ts_=ts_: e.matmul(
                            po[:, c * 128:(c + 1) * 128], lhsT=Sbf[:, h, half * 128:(half + 1) * 128], rhs=qT[:, h, ts_],
                            start=False, stop=True))
                sc.pe_group(fns, reads=[vtok[tt], ATs, Sbf, qT], writes=[po])
                for nb in range(2):
                    sc.op("act", lambda e, nb=nb, po=po: e.activation(
                        out=osq[:, nb * 4:(nb + 1) * 4, :], in_=po[:, nb * 512:(nb + 1) * 512].rearrange("p (c t) -> p c t", c=4),
                        func=AF.Square), reads=[po], writes=[osq])
                ps_ = nextA()
                fns = []
                for h in range(4):
                    for half in range(2):
                        fns.append(lambda e, h=h, half=half, ps_=ps_: e.matmul(
                            ps_[:, h * 128:(h + 1) * 128], lhsT=self.onesb.ap, rhs=osq[:, 2 * h + half, :],
                            start=(half == 0), stop=(half == 1)))
                sc.pe_group(fns, reads=[self.onesb, osq], writes=[ps_])
                sc.op("act", lambda e, ps_=ps_: e.activation(out=rstd.ap, in_=ps_.ap.rearrange("p (h t) -> p h t", h=4),
                                                             func=AF.Ln, scale=1.0 / 256, bias=EPS),
                      reads=[ps_], writes=[rstd])
                sc.op("act", lambda e: e.activation(out=rstd.ap, in_=rstd.ap, func=AF.Exp, scale=-0.5),
                      reads=[rstd], writes=[rstd])
                for h in range(4):
                    for half in range(2):
                        c = 2 * h + half
                        sc.op("dve", lambda e, h=h, c=c, po=po: e.tensor_tensor(out=otmp[:, c, :], in0=po[:, c * 128:(c + 1) * 128],
                                                                                in1=rstd[:, h, :], op=ALU.mult),
                              reads=[po, rstd], writes=[otmp])
                sc.op("pool", lambda e, ts_=ts_: e.tensor_tensor(out=ogT[:, :, ts_], in0=otmp.ap, in1=szT[:, :, ts_], op=ALU.mult),
                      reads=[otmp, szT], writes=[ogT])
                pP = nextW()
                fns = [lambda e, h=h, pP=pP, tt=tt: e.matmul(pP[:, h * 256:(h + 1) * 256], lhsT=ktok[tt][:, h * 128:(h + 1) * 128],
                                                             rhs=vtok[tt][:, h * 256:(h + 1) * 256], start=True, stop=True)
                       for h in range(4)]
                sc.pe_group(fns, reads=[ktok[tt], vtok[tt]], writes=[pP])
                S2 = Sst.ap.rearrange("p h v -> p (h v)")
                sc.op("dve", lambda e, pP=pP, S2=S2: e.tensor_tensor(out=S2, in0=pP.ap, in1=S2, op=ALU.add),
                      reads=[pP, Sst], writes=[Sst])
                el = elast[tt]
                for h in range(4):
                    sc.op("pool", lambda e, h=h, el=el: e.tensor_scalar(out=Sst[:, h, :], in0=Sst[:, h, :], scalar1=el[:, h:h + 1],
                                                                        scalar2=1.0, op0=ALU.mult, op1=ALU.mult),
                          reads=[Sst, el], writes=[Sst])
                sc.op("act", lambda e: e.copy(out=Sbf.ap, in_=Sst.ap), reads=[Sst], writes=[Sbf])
            self.stageO_full(j, ogT, x_in, x_out, psW)

    def layer_fox(self, L, x_in, x_out):
        sc, A, dr, S = self.sc, self.A, self.dram, self.S
        p = "L%d_" % L
        QA, KA, VS, SZ, OG = self.QA, self.KA, self.VS, self.SZ, self.OG
        self.prep(L, [(2048, 3072)])
        sl = self.slots["c0"]
        gqk = A.alloc([2])
        bfc = A.alloc([1])
        negm_f = A.alloc([128])
        negm = A.alloc([128], BF16)
        bo_f = A.alloc([128])
        bones = A.alloc([128], BF16)
        self_sel = A.alloc([64])
        ones3 = A.alloc([3, ST], BF16)
        sc.dma("sp", sl, lambda e: e.dma_start(out=gqk.ap, in_=dr[p + "gqk"]), writes=[gqk])
        sc.dma("sp", sl, lambda e: e.dma_start(out=bfc[0:16, :], in_=dr[p + "bf"]), writes=[bfc])
        sc.dma("sp", sl, lambda e: e.dma_start(out=negm_f.ap, in_=dr["negmask"]), writes=[negm_f])
        sc.dma("sp", sl, lambda e: e.dma_start(out=bo_f.ap, in_=dr["blockones"]), writes=[bo_f])
        sc.dma("sp", sl, lambda e: e.dma_start(out=self_sel[0:65, :], in_=dr["sel"]), writes=[self_sel])
        sc.barrier()
        sc.op("dve", lambda e: e.tensor_copy(out=negm.ap, in_=negm_f.ap), reads=[negm_f], writes=[negm])
        sc.op("dve", lambda e: e.tensor_copy(out=bones.ap, in_=bo_f.ap), reads=[bo_f], writes=[bones])
        sc.op("dve", lambda e: e.memset(ones3.ap, 1.0), writes=[ones3])
        sc.op("dve", lambda e: e.tensor_scalar(out=gqk[:, 0:1], in0=gqk[:, 0:1], scalar1=0.125, scalar2=None, op0=ALU.mult),
              reads=[gqk], writes=[gqk])
        nfb = A.alloc([1])
        sc.op("dve", lambda e: e.tensor_tensor(out=nfb[0:16, :], in0=self.biascol[0:16, 32:33], in1=bfc[0:16, :], op=ALU.add),
              reads=[self.biascol, bfc], writes=[nfb])
        sc.op("dve", lambda e: e.tensor_scalar(out=nfb[0:16, :], in0=nfb[0:16, :], scalar1=-1.0, scalar2=None, op0=ALU.mult),
              reads=[nfb], writes=[nfb])
        ph_mark = A.mark()
        self.alloc_stageA(nxt=2, nxnT=1)
        sq = A.alloc([ST], BF16)
        rstd = A.alloc([ST])
        tmpf = A.alloc([ST])
        qn = [A.alloc([ST], BF16) for _ in range(2)]
        vtok = [A.alloc([D], BF16) for _ in range(2)]
        szT = A.alloc([KC, ST], BF16)
        Ff = [A.alloc([ST]) for _ in range(2)]
        fe = A.alloc([ST])
        fsp = A.alloc([ST])
        fr = A.alloc([ST])
        pcs = [A.alloc([ST], BF16) for _ in range(6)]
        sc.op("dve", lambda e: e.memset(fr.ap, 1.0), writes=[fr])
        psW = [T(self.pst(0, 2), self.psbuf[0]), T(self.pst(2, 2), self.psbuf[2])]
        psA = [T(self.pst(4 + i), self.psbuf[4 + i]) for i in range(3)]
        brow = self.biasrow[(2048, 3072)]
        ia = [0]

        def nextA():
            t = psA[ia[0] % 3]
            ia[0] += 1
            return t

        iq = 0
        for j in range(self.nST):
            tok0 = j * ST
            xnT = self.stageA(j, x_in)
            for which in range(2):
                DST = QA if which == 0 else KA
                for c in range(KC):
                    col0 = which * 1024 + c * 128
                    bcol = self.biascol[:, which * 8 + c:which * 8 + c + 1]
                    pa = nextA()
                    fns = [lambda e, kc=kc, pa=pa, col0=col0: e.matmul(pa.ap, lhsT=self.Win[:, kc, col0:col0 + 128], rhs=xnT[:, kc, :],
                                                                       start=(kc == 0), stop=(kc == KC - 1)) for kc in range(KC)]
                    sc.pe_group(fns, reads=[xnT, self.Win], writes=[pa])
                    sc.op("act", lambda e, pa=pa, bcol=bcol: e.activation(out=sq.ap, in_=pa.ap, func=AF.Square, bias=bcol),
                          reads=[pa, self.biascol], writes=[sq])
                    pb = nextA()
                    sc.pe_group([lambda e, pb=pb: e.matmul(pb.ap, lhsT=bones.ap, rhs=sq.ap, start=True, stop=True)],
                                reads=[bones, sq], writes=[pb])
                    sc.op("act", lambda e, pb=pb: e.activation(out=rstd.ap, in_=pb.ap, func=AF.Ln, scale=1.0 / 64, bias=EPS),
                          reads=[pb], writes=[rstd])
                    sc.op("act", lambda e: e.activation(out=rstd.ap, in_=rstd.ap, func=AF.Exp, scale=-0.5),
                          reads=[rstd], writes=[rstd])
                    sc.op("dve", lambda e, pa=pa, bcol=bcol: e.scalar_tensor_tensor(out=tmpf.ap, in0=pa.ap, scalar=bcol, in1=rstd.ap,
                                                                                    op0=ALU.add, op1=ALU.mult),
                          reads=[pa, self.biascol, rstd], writes=[tmpf])
                    q_ = qn[iq % 2]
                    slq = self.slots["fq%d" % (iq % 2)]
                    iq += 1
                    sc.op("pool", lambda e, q_=q_, which=which: e.tensor_scalar(out=q_.ap, in0=tmpf.ap, scalar1=gqk[:, which:which + 1],
                                                                               scalar2=1.0, op0=ALU.mult, op1=ALU.mult),
                          reads=[tmpf, gqk], writes=[q_])
                    for hh in range(2):
                        sc.dma("pool", slq, lambda e, q_=q_, hh=hh, c=c, DST=DST, tok0=tok0: e.dma_start(
                            out=DST[2 * c + hh, 0:64, tok0:tok0 + ST], in_=q_[hh * 64:(hh + 1) * 64, :]), reads=[q_])
            for tt in range(4):
                ts_ = slice(tt * 128, (tt + 1) * 128)
                pv = psW[tt % 2]
                fns = []
                for nb in range(2):
                    for kc in range(KC):
                        fns.append(lambda e, nb=nb, kc=kc, pv=pv, ts_=ts_: e.matmul(
                            pv[:, nb * 512:(nb + 1) * 512], lhsT=xnT[:, kc, ts_],
                            rhs=self.Win[:, kc, 2048 + nb * 512:2048 + (nb + 1) * 512], start=(kc == 0), stop=(kc == KC - 1)))
                sc.pe_group(fns, reads=[xnT, self.Win], writes=[pv])
                v_ = vtok[tt % 2]
                sc.op("dve", lambda e, pv=pv, v_=v_: e.tensor_tensor(out=v_.ap, in0=pv.ap, in1=brow.ap, op=ALU.add),
                      reads=[pv, brow], writes=[v_])
                sc.dma("pool", self.slots["fv%d" % (tt % 2)], lambda e, v_=v_, tt=tt, tok0=tok0: e.dma_start(
                    out=VS[tok0 + tt * 128:tok0 + (tt + 1) * 128, :], in_=v_.ap), reads=[v_])
            for c in range(KC):
                pa = nextA()
                fns = [lambda e, kc=kc, c=c, pa=pa: e.matmul(pa.ap, lhsT=self.Win[:, kc, 3072 + c * 128:3072 + (c + 1) * 128],
                                                             rhs=xnT[:, kc, :], start=(kc == 0), stop=(kc == KC - 1))
                       for kc in range(KC)]
                sc.pe_group(fns, reads=[xnT, self.Win], writes=[pa])
                sc.op("act", lambda e, c=c, pa=pa: e.activation(out=szT[:, c, :], in_=pa.ap, func=AF.Silu,
                                                                bias=self.biascol[:, 24 + c:25 + c]),
                      reads=[pa, self.biascol], writes=[szT])
            sc.dma("pool", self.slots["fz0"], lambda e, tok0=tok0: e.dma_start(
                out=SZ[:, tok0:tok0 + ST].rearrange("(c p) t -> p c t", p=128), in_=szT.ap), reads=[szT])
            pa = nextA()
            fns = [lambda e, kc=kc, pa=pa: e.matmul(pa[0:16, :], lhsT=self.Win[:, kc, 4096:4112], rhs=xnT[:, kc, :],
                                                    start=(kc == 0), stop=(kc == KC - 1)) for kc in range(KC)]
            sc.pe_group(fns, reads=[xnT, self.Win], writes=[pa])
            sc.op("act", lambda e, pa=pa: e.activation(out=fe[0:16, :], in_=pa[0:16, :], func=AF.Exp, scale=-1.0, bias=nfb[0:16, :]),
                  reads=[pa, nfb], writes=[fe])
            sc.op("act", lambda e: e.activation(out=fsp[0:16, :], in_=fe[0:16, :], func=AF.Ln, bias=1.0),
                  reads=[fe], writes=[fsp])
            F_ = Ff[j % 2]
            Fp = Ff[(j + 1) % 2]
            init = 0.0 if j == 0 else Fp[0:16, ST - 1:ST]
            sc.op("dve", lambda e, F_=F_, init=init: e.tensor_tensor_scan(out=F_[0:16, :], data0=fr[0:16, :],
                                                                          data1=fsp[0:16, :], initial=init, op0=ALU.mult, op1=ALU.subtract),
                  reads=[fsp, fr] + ([Fp] if j > 0 else []), writes=[F_])
            sc.op("dve", lambda e, F_=F_: e.tensor_copy(out=pcs[0][0:16, :], in_=F_[0:16, :]), reads=[F_], writes=[pcs[0]])
            sc.op("dve", lambda e, F_=F_: e.tensor_tensor(out=fe[0:16, :], in0=F_[0:16, :], in1=pcs[0][0:16, :], op=ALU.subtract),
                  reads=[F_, pcs[0]], writes=[fe])
            sc.op("dve", lambda e: e.tensor_copy(out=pcs[1][0:16, :], in_=fe[0:16, :]), reads=[fe], writes=[pcs[1]])
            sc.op("dve", lambda e: e.tensor_tensor(out=fe[0:16, :], in0=fe[0:16, :], in1=pcs[1][0:16, :], op=ALU.subtract),
                  reads=[fe, pcs[1]], writes=[fe])
            sc.op("dve", lambda e: e.tensor_copy(out=pcs[2][0:16, :], in_=fe[0:16, :]), reads=[fe], writes=[pcs[2]])
            for i in range(3):
                sc.op("pool", lambda e, i=i: e.tensor_scalar(out=pcs[3 + i][0:16, :], in0=pcs[i][0:16, :], scalar1=-1.0, scalar2=1.0,
                                                             op0=ALU.mult, op1=ALU.mult), reads=[pcs[i]], writes=[pcs[3 + i]])
            for i in range(3):
                sc.dma("pool", self.slots["ff0"], lambda e, i=i, tok0=tok0: e.dma_start(out=QA[:, 64 + i, tok0:tok0 + ST], in_=pcs[i][0:16, :]),
                       reads=[pcs[i]])
                sc.dma("pool", self.slots["ff0"], lambda e, i=i, tok0=tok0: e.dma_start(out=KA[:, 67 + i, tok0:tok0 + ST], in_=pcs[3 + i][0:16, :]),
                       reads=[pcs[3 + i]])
            sc.dma("pool", self.slots["ff0"], lambda e, tok0=tok0: e.dma_start(out=QA[:, 67:70, tok0:tok0 + ST], in_=ones3[0:16, :, :]),
                   reads=[ones3])
            sc.dma("pool", self.slots["ff0"], lambda e, tok0=tok0: e.dma_start(out=KA[:, 64:67, tok0:tok0 + ST], in_=ones3[0:16, :, :]),
                   reads=[ones3])
        sc.barrier()
        self.ck(20)
        A.reset(ph_mark)
        NKT = S // 128
        NQB = S // ST
        KAh = [A.alloc([S], BF16) for _ in range(2)]
        Vh = [A.alloc([NKT, 65], BF16) for _ in range(2)]
        QAq = [A.alloc([ST], BF16) for _ in range(3)]
        szq = [A.alloc([ST], BF16) for _ in range(2)]
        PT = [A.alloc([2, ST], BF16) for _ in range(3)]
        Osb = [A.alloc([ST]) for _ in range(2)]
        tn = A.alloc([ST])
        ogq = [A.alloc([ST], BF16) for _ in range(2)]
        for v_ in Vh:
            sc.op("dve", lambda e, v_=v_: e.memset(v_.ap, 1.0), writes=[v_])
        psS = [T(self.ps[:, 0:2, :], self.psbuf[0]), T(self.ps[:, 2:4, :], self.psbuf[2])]
        psO = [T(self.pst(4), self.psbuf[4]), T(self.pst(5), self.psbuf[5])]
        psB = T(self.pst(6), self.psbuf[6])
        items = []
        nqb_total = 0
        for h in range(16):
            for qb in range(NQB):
                nk = 4 * qb + 4
                groups = [[kt, kt + 1] for kt in range(0, 4 * qb, 2)] + [[kt] for kt in range(4 * qb, nk)]
                for gi, g in enumerate(groups):
                    items.append(dict(h=h, qb=qb, g=g, nk=nk, first=(gi == 0), last=(gi == len(groups) - 1),
                                      iq=nqb_total, idx=len(items)))
                nqb_total += 1

        def load_head(h):
            ka = KAh[h % 2]
            vh = Vh[h % 2]
            sc.dma("sp", self.slots["fk%d" % (h % 2)], lambda e: e.dma_start(out=ka[0:70, :], in_=KA[h, :, :]), writes=[ka])
            sc.dma("sp", self.slots["fvh%d" % (h % 2)], lambda e: e.dma_start(
                out=vh[:, :, 0:64], in_=VS[:, h * 64:(h + 1) * 64].rearrange("(kt p) d -> p kt d", p=128)), writes=[vh])

        def emit_qk(it):
            h, qb, g, iq = it["h"], it["qb"], it["g"], it["iq"]
            ka = KAh[h % 2]
            qa = QAq[iq % 3]
            if it["first"]:
                q0t = qb * ST
                zq = szq[iq % 2]
                sc.dma("sp", self.slots["fqq%d" % (iq % 3)], lambda e: e.dma_start(out=qa[0:70, :], in_=QA[h, :, q0t:q0t + ST]),
                       writes=[qa])
                sc.dma("sp", self.slots["fsz%d" % (iq % 2)], lambda e: e.dma_start(out=zq[0:64, :], in_=SZ[h * 64:(h + 1) * 64, q0t:q0t + ST]),
                       writes=[zq])
            pS = psS[it["idx"] % 2]
            fns = []
            for i, kt in enumerate(g):
                r = kt - 4 * qb
                q0 = max(r, 0) * 128
                fns.append(lambda e, i=i, kt=kt, q0=q0, r=r: e.matmul(
                    pS[:, i, q0:ST], lhsT=ka[0:70, kt * 128:(kt + 1) * 128], rhs=qa[0:70, q0:ST], start=True, stop=(r < 0)))
                if r >= 0:
                    fns.append(lambda e, i=i, q0=q0: e.matmul(pS[:, i, q0:q0 + 128], lhsT=self.identb.ap, rhs=negm.ap,
                                                              start=False, stop=True))
            sc.pe_group(fns, reads=[ka, qa, self.identb, negm], writes=[pS])

        def emit_act(it):
            g, qb = it["g"], it["qb"]
            pS = psS[it["idx"] % 2]
            pt = PT[it["idx"] % 3]
            if len(g) == 2:
                sc.op("act", lambda e: e.activation(out=pt.ap, in_=pS.ap, func=AF.Exp), reads=[pS], writes=[pt])
            else:
                q0 = max(g[0] - 4 * qb, 0) * 128
                sc.op("act", lambda e: e.activation(out=pt[:, 0, q0:ST], in_=pS[:, 0, q0:ST], func=AF.Exp),
                      reads=[pS], writes=[pt])

        def emit_pv(it):
            h, qb, g, nk, iq = it["h"], it["qb"], it["g"], it["nk"], it["iq"]
            vh = Vh[h % 2]
            pt = PT[it["idx"] % 3]
            po = psO[iq % 2]
            fns = []
            for i, kt in enumerate(g):
                q0 = max(kt - 4 * qb, 0) * 128
                fns.append(lambda e, i=i, kt=kt, q0=q0: e.matmul(
                    po[0:65, q0:ST], lhsT=vh[:, kt, :], rhs=pt[:, i, q0:ST], start=(kt == 0), stop=(kt == nk - 1)))
            sc.pe_group(fns, reads=[vh, pt], writes=[po])

        def fin_a(it):
            iq = it["iq"]
            ob, po = Osb[iq % 2], psO[iq % 2]
            sc.op("dve", lambda e: e.tensor_copy(out=ob[0:65, :], in_=po[0:65, :]), reads=[po], writes=[ob])
            sc.op("dve", lambda e: e.reciprocal(out=ob[64:65, :], in_=ob[64:65, :]), reads=[ob], writes=[ob])

        def fin_b(it):
            h, qb, iq = it["h"], it["qb"], it["iq"]
            ob, og_, zq = Osb[iq % 2], ogq[iq % 2], szq[iq % 2]
            q0t = qb * ST
            sc.pe_group([lambda e: e.matmul(psB[0:64, :], lhsT=self_sel[0:65, :], rhs=ob[0:65, :], start=True, stop=True)],
                        reads=[self_sel, ob], writes=[psB])
            sc.op("dve", lambda e: e.tensor_tensor(out=tn[0:64, :], in0=ob[0:64, :], in1=psB[0:64, :], op=ALU.mult),
                  reads=[ob, psB], writes=[tn])
            sc.op("pool", lambda e: e.tensor_tensor(out=og_[0:64, :], in0=tn[0:64, :], in1=zq[0:64, :], op=ALU.mult),
                  reads=[tn, zq], writes=[og_])
            sc.dma("pool", self.slots["fog%d" % (iq % 2)], lambda e: e.dma_start(
                out=OG[h * 64:(h + 1) * 64, q0t:q0t + ST], in_=og_[0:64, :]), reads=[og_])

        load_head(0)
        emit_qk(items[0])
        pending = []
        for i, it in enumerate(items):
            if i + 1 < len(items):
                emit_qk(items[i + 1])
            emit_act(it)
            emit_pv(it)
            if it["first"] and it["qb"] == 0 and it["h"] + 1 < 16:
                load_head(it["h"] + 1)
            for pnd in pending:
                pnd[0] -= 1
            while pending and pending[0][0] <= 0:
                fin_b(pending.pop(0)[1])
            if it["last"]:
                fin_a(it)
                pending.append([2, it])
        for pnd in pending:
            fin_b(pnd[1])
        sc.barrier()
        self.ck(21)
        A.reset(ph_mark)
        self.alloc_stageA(nxt=1, nxnT=1)
        self.alloc_stageO()
        ogT = [A.alloc([KC, ST], BF16) for _ in range(2)]
        psW = [T(self.pst(0, 2), self.psbuf[0]), T(self.pst(2, 2), self.psbuf[2])]
        for j in range(self.nST):
            tok0 = j * ST
            og = ogT[j % 2]
            sc.dma("sp", self.slots["fo%d" % (j % 2)], lambda e, og=og, tok0=tok0: e.dma_start(
                out=og.ap, in_=OG[:, tok0:tok0 + ST].rearrange("(c p) t -> p c t", p=128)), writes=[og])
            self.stageO_full(j, og, x_in, x_out, psW)

    def layer_sgu(self, L, x_in, x_out):
        sc, A, dr = self.sc, self.A, self.dram
        p = "L%d_" % L
        self.prep(L, [(1024, 2048)])
        sc.barrier()
        self.ck(4)
        self.alloc_stageA()
        self.alloc_stageO()
        lng = A.alloc([D])
        lnb = A.alloc([D])
        bsb = A.alloc([4, 128])
        wsf = A.alloc([4, 128])
        WcT = A.alloc([4, 128], BF16)
        sl = self.slots["c0"]
        sc.dma("sp", sl, lambda e: e.dma_start(out=lng.ap, in_=dr[p + "lng"]), writes=[lng])
        sc.dma("sp", sl, lambda e: e.dma_start(out=lnb.ap, in_=dr[p + "lnb"]), writes=[lnb])
        sc.dma("sp", sl, lambda e: e.dma_start(out=bsb.ap, in_=dr[p + "bs"]), writes=[bsb])
        sc.dma("sp", sl, lambda e: e.dma_start(out=wsf.ap, in_=dr[p + "ws"]), writes=[wsf])
        sc.barrier()
        pw = T(self.pst(0), self.psbuf[0])
        fns = [lambda e, g=g: e.transpose(out=pw[:, g * 128:(g + 1) * 128], in_=wsf[:, g, :], identity=self.identf.ap)
               for g in range(4)]
        sc.pe_group(fns, reads=[wsf, self.identf], writes=[pw])
        for g in range(4):
            sc.op("dve", lambda e, g=g: e.tensor_tensor(out=WcT[:, g, :], in0=pw[:, g * 128:(g + 1) * 128], in1=self.trif.ap,
                                                       op=ALU.mult), reads=[pw, self.trif], writes=[WcT])
        self.dbg("Wout", self.Wout); self.dbg("WcT", WcT)
        self.ck(5)
        gl = [A.alloc([D]) for _ in range(4)]
        vln = [A.alloc([D], BF16) for _ in range(4)]
        uT = A.alloc([KC, ST], BF16)
        szT = A.alloc([KC, ST], BF16)
        ogT = A.alloc([KC, ST], BF16)
        mt = [A.alloc([ST]) for _ in range(2)]
        s1 = A.alloc([4])
        nm = A.alloc([4])
        s2 = A.alloc([4])
        lv = A.alloc([4])
        rv = A.alloc([4])
        psW = [T(self.pst(0, 2), self.psbuf[0]), T(self.pst(2, 2), self.psbuf[2])]
        psA = [T(self.pst(4 + i), self.psbuf[4 + i]) for i in range(3)]
        brow = self.biasrow[(1024, 2048)]
        ia = 0
        for j in range(self.nST):
            xnT = self.stageA(j, x_in)
            self.dbg("xnT", xnT); self.dbg("rsA", self.rsA[j % 2])
            self.ck(6)
            for tt in range(4):
                pv = psW[tt % 2]
                fns = []
                for nb in range(2):
                    for kc in range(KC):
                        fns.append(lambda e, nb=nb, kc=kc, pv=pv, tt=tt: e.matmul(
                            pv[:, nb * 512:(nb + 1) * 512], lhsT=xnT[:, kc, tt * 128:(tt + 1) * 128],
                            rhs=self.Win[:, kc, 1024 + nb * 512:1024 + (nb + 1) * 512], start=(kc == 0), stop=(kc == KC - 1)))
                sc.pe_group(fns, reads=[xnT, self.Win], writes=[pv])
                g_ = gl[tt]
                sc.op("dve", lambda e, pv=pv, g_=g_: e.tensor_tensor(out=g_.ap, in0=pv.ap, in1=brow.ap, op=ALU.add),
                      reads=[pv, brow], writes=[g_])
                sc.op("act", lambda e, g_=g_, tt=tt: e.activation(out=g_.ap, in_=g_.ap, func=AF.Gelu_apprx_tanh,
                                                                  accum_out=s1[:, tt:tt + 1]),
                      reads=[g_], writes=[g_, s1])
            self.ck(7)
            for c in range(KC):
                pa = psA[ia % 3]
                ia += 1
                fns = [lambda e, kc=kc, c=c, pa=pa: e.matmul(pa.ap, lhsT=self.Win[:, kc, c * 128:(c + 1) * 128], rhs=xnT[:, kc, :],
                                                             start=(kc == 0), stop=(kc == KC - 1)) for kc in range(KC)]
                sc.pe_group(fns, reads=[xnT, self.Win], writes=[pa])
                sc.op("act", lambda e, c=c, pa=pa: e.activation(out=uT[:, c, :], in_=pa.ap, func=AF.Gelu_apprx_tanh,
                                                                bias=self.biascol[:, c:c + 1]),
                      reads=[pa, self.biascol], writes=[uT])
            self.dbg("gl0", gl[0]); self.dbg("uT", uT); self.dbg("s1", s1)
            self.ck(8)
            sc.op("dve", lambda e: e.tensor_scalar(out=nm.ap, in0=s1.ap, scalar1=-1.0 / D, scalar2=None, op0=ALU.mult),
                  reads=[s1], writes=[nm])
            for tt in range(4):
                g_ = gl[tt]
                sc.op("act", lambda e, g_=g_, tt=tt: e.activation(out=self.junk.ap, in_=g_.ap, func=AF.Square,
                                                                  bias=nm[:, tt:tt + 1], accum_out=s2[:, tt:tt + 1]),
                      reads=[g_, nm], writes=[self.junk, s2])
            self.rstd_from_ss(s2, rv, 1.0 / D, lv)
            for tt in range(4):
                g_ = gl[tt]
                sc.op("dve", lambda e, g_=g_, tt=tt: e.tensor_scalar(out=g_.ap, in0=g_.ap, scalar1=nm[:, tt:tt + 1],
                                                                    scalar2=rv[:, tt:tt + 1], op0=ALU.add, op1=ALU.mult),
                      reads=[g_, nm, rv], writes=[g_])
                sc.op("pool", lambda e, g_=g_: e.tensor_tensor(out=g_.ap, in0=g_.ap, in1=lng.ap, op=ALU.mult),
                      reads=[g_, lng], writes=[g_])
                sc.op("pool", lambda e, g_=g_, tt=tt: e.tensor_tensor(out=vln[tt].ap, in0=g_.ap, in1=lnb.ap, op=ALU.add),
                      reads=[g_, lnb], writes=[vln[tt]])
            self.dbg("vln0", vln[0]); self.dbg("rv", rv)
            self.ck(9)
            for c in range(KC):
                pa = psA[ia % 3]
                ia += 1
                fns = [lambda e, kc=kc, c=c, pa=pa: e.matmul(pa.ap, lhsT=self.Win[:, kc, 2048 + c * 128:2048 + (c + 1) * 128],
                                                             rhs=xnT[:, kc, :], start=(kc == 0), stop=(kc == KC - 1))
                       for kc in range(KC)]
                sc.pe_group(fns, reads=[xnT, self.Win], writes=[pa])
                sc.op("act", lambda e, c=c, pa=pa: e.activation(out=szT[:, c, :], in_=pa.ap, func=AF.Silu,
                                                                bias=self.biascol[:, 16 + c:17 + c]),
                      reads=[pa, self.biascol], writes=[szT])
            self.ck(10)
            for c in range(KC):
                g = c // 2
                pa = psA[ia % 3]
                ia += 1
                fns = [lambda e, c=c, g=g, tt=tt, pa=pa: e.matmul(pa[:, tt * 128:(tt + 1) * 128], lhsT=vln[tt][:, c * 128:(c + 1) * 128],
                                                                  rhs=WcT[:, g, :], start=True, stop=True) for tt in range(4)]
                sc.pe_group(fns, reads=vln + [WcT], writes=[pa])
                m_ = mt[c % 2]
                for tt in range(4):
                    sc.op("dve", lambda e, pa=pa, m_=m_, g=g, tt=tt: e.tensor_tensor(
                        out=m_[:, tt * 128:(tt + 1) * 128], in0=pa[:, tt * 128:(tt + 1) * 128], in1=bsb[:, g, :], op=ALU.add),
                        reads=[pa, bsb], writes=[m_])
                sc.op("dve", lambda e, m_=m_, c=c: e.tensor_tensor(out=m_.ap, in0=m_.ap, in1=uT[:, c, :], op=ALU.mult),
                      reads=[m_, uT], writes=[m_])
                sc.op("pool", lambda e, m_=m_, c=c: e.tensor_tensor(out=ogT[:, c, :], in0=m_.ap, in1=szT[:, c, :], op=ALU.mult),
                      reads=[m_, szT], writes=[ogT])
            self.dbg("szT", szT); self.dbg("ogT", ogT)
            self.ck(11)
            self.stageO_full(j, ogT, x_in, x_out, psW)


def col8(v):
    return np.ascontiguousarray(v.reshape(-1, 128).T)


def rep(v, n=128):
    return np.ascontiguousarray(np.broadcast_to(v.reshape(1, -1), (n, v.size)))


def make_in_maps(inputs, layer_ids, S, n_cores):
    f = lambda a: np.ascontiguousarray(np.asarray(a, dtype=np.float32))
    x = f(inputs["x"])
    c = f(inputs["c"])
    shared = {"ident": np.eye(128, dtype=np.float32),
              "tri": np.triu(np.ones((128, 128), dtype=np.float32)),
              "uneg": np.triu(np.full((128, 128), -1.0 / 16, dtype=np.float32)),
              "negmask": np.tril(np.full((128, 128), -30000.0, dtype=np.float32), -1),
              "blockones": np.kron(np.eye(2, dtype=np.float32), np.ones((64, 64), dtype=np.float32)),
              "sel": np.concatenate([np.zeros((64, 64), np.float32), np.ones((1, 64), np.float32)], 0)}
    for L in layer_ids:
        kind, jj = L % 3, L // 3
        p = "L%d_" % L
        bm = f(inputs["b_mod"][L])
        shared[p + "wmod"] = f(inputs["w_mod"][L])
        shared[p + "bmodc"] = col8(bm)
        shared[p + "bmodg"] = rep(bm[2048:3072])
        shared[p + "gprec"] = col8(f(inputs["norm_pre_g"][L]))
        shared[p + "gpost"] = rep(f(inputs["norm_post_g"][L]))
        if kind == 0:
            shared[p + "win"] = f(inputs["gla_w_in"][jj])
            shared[p + "wout"] = f(inputs["gla_w_out"][jj])
            shared[p + "wa2"] = np.ascontiguousarray(np.concatenate([f(inputs["gla_w_a2"][jj]), f(inputs["gla_b_a"][jj])[None]], 0))
            shared[p + "ghc"] = col8(f(inputs["gla_g_head"][jj]).reshape(-1))
        if kind == 2:
            shared[p + "win"] = f(inputs["fox_w_in"][jj])
            shared[p + "wout"] = f(inputs["fox_w_out"][jj])
            shared[p + "gqk"] = np.ascontiguousarray(np.stack([np.tile(f(inputs["fox_g_q"][jj]), 2), np.tile(f(inputs["fox_g_k"][jj]), 2)], 1))
            shared[p + "bf"] = np.ascontiguousarray(f(inputs["fox_b_f"][jj])[:, None])
        if kind == 1:
            shared[p + "win"] = f(inputs["sgu_w_in"][jj])
            shared[p + "wout"] = f(inputs["sgu_w_out"][jj])
            shared[p + "lng"] = rep(f(inputs["sgu_ln_g"][jj]))
            shared[p + "lnb"] = rep(f(inputs["sgu_ln_b"][jj]))
            shared[p + "ws"] = np.ascontiguousarray(f(inputs["sgu_w_s"][jj]).transpose(1, 0, 2))
            shared[p + "bs"] = np.ascontiguousarray(np.broadcast_to(f(inputs["sgu_b_s"][jj])[None], (128, 4, 128)))
    maps = []
    for b in range(n_cores):
        m = dict(shared)
        m["x"] = np.ascontiguousarray(x[b, :S])
        m["ccol"] = col8(c[b])
        maps.append(m)
    return maps


_PROG_CACHE = {}


def run(inputs, layer_ids=(0, 1, 2, 3), S=8192, n_cores=N_CORES):
    key = (S, tuple(layer_ids))
    if key not in _PROG_CACHE:
        pr = Prog(S, layer_ids)
        pr.build()
        _PROG_CACHE[key] = pr
    pr = _PROG_CACHE[key]
    maps = make_in_maps(inputs, layer_ids, S, n_cores)
    res = run_bass_kernel_spmd(pr.nc, maps, core_ids=list(range(n_cores)))
    return np.stack([np.asarray(r["out"]) for r in res.results], axis=0)


def kernel(**inputs):
    return run(inputs).astype(np.float32)
```

```python
import numpy as np
from contextlib import ExitStack
import concourse.bass as bass
import concourse.mybir as mybir
from concourse.bass_utils import run_bass_kernel_spmd

F32 = mybir.dt.float32
BF16 = mybir.dt.bfloat16
AF = mybir.ActivationFunctionType
ALU = mybir.AluOpType

D = 1024
KC = 8
ST = 512
EPS = 1e-6
N_IN = {0: 3088, 1: 3072, 2: 4112}
N_CORES = 8


class Buf:
    __slots__ = ("w", "r", "excl")

    def __init__(self, excl=False):
        self.w = {}
        self.r = {}
        self.excl = excl


class T:
    __slots__ = ("ap", "buf")

    def __init__(self, ap, buf=None):
        self.ap = ap
        self.buf = buf if buf is not None else Buf()

    def __getitem__(self, k):
        return self.ap[k]


def _b(x):
    return x.buf if isinstance(x, T) else x


class _Rec:
    def __init__(self):
        self.call = None

    def __getattr__(self, name):
        def f(*a, **kw):
            assert self.call is None
            self.call = (name, a, kw)
            return None
        return f


def _capture(fn):
    r = _Rec()
    fn(r)
    name, a, kw = r.call
    line = fn.__code__.co_firstlineno

    def replay(eng):
        return getattr(eng, name)(*a, **kw)
    replay.line = line
    return replay


class Sched:
    ENG = ("pe", "act", "dve", "pool", "sp")

    def __init__(self, nc, es):
        self.nc = nc
        self.es = es
        self.q = {e: [] for e in self.ENG}
        self.cnt = {e: 0 for e in self.ENG}
        self.seen = {e: {} for e in self.ENG}
        self.sems = {}
        self.names = {}
        self.dcnt = {}
        for e in self.ENG:
            self.sems[e] = es.enter_context(nc.semaphore("s_" + e))

    def slot(self, name):
        k = "d_" + name
        self.sems[k] = self.es.enter_context(self.nc.semaphore(k))
        self.dcnt[k] = 0
        return k

    @staticmethod
    def _split(reads, writes):
        r2 = [b for b in reads if not _b(b).excl]
        w2 = list(writes) + [b for b in reads if _b(b).excl]
        return r2, w2

    def _waits(self, eng, reads, writes):
        reads, writes = self._split(reads, writes)
        need = {}
        for b in reads:
            for k, v in _b(b).w.items():
                if need.get(k, 0) < v:
                    need[k] = v
        for b in writes:
            bb = _b(b)
            for dct in (bb.w, bb.r):
                for k, v in dct.items():
                    if need.get(k, 0) < v:
                        need[k] = v
        waits = []
        seen = self.seen[eng]
        for k, v in need.items():
            if k in self.dcnt:
                v = self.dcnt[k]
            if k == eng and eng == "pe":
                continue
            if seen.get(k, 0) >= v:
                continue
            seen[k] = v
            waits.append((k, v))
        return waits

    def _record(self, ev, reads, writes):
        reads, writes = self._split(reads, writes)
        k, v = ev
        for b in reads:
            bb = _b(b)
            if bb.r.get(k, 0) < v:
                bb.r[k] = v
        for b in writes:
            bb = _b(b)
            bb.r = {}
            if bb.w.get(k, 0) < v:
                bb.w[k] = v

    def op(self, eng, fn, reads=(), writes=()):
        waits = self._waits(eng, reads, writes)
        self.cnt[eng] += 1
        ev = (eng, self.cnt[eng])
        self.q[eng].append((waits, _capture(fn), (eng, 1)))
        self._record(ev, reads, writes)

    def pe_group(self, fns, reads=(), writes=()):
        waits = self._waits("pe", reads, writes)
        self.cnt["pe"] += 1
        ev = ("pe", self.cnt["pe"])
        n = len(fns)
        for i, fn in enumerate(fns):
            self.q["pe"].append((waits if i == 0 else [], _capture(fn), ("pe", 1) if i == n - 1 else None))
        self._record(ev, reads, writes)

    def dma(self, q, slot, fn, reads=(), writes=()):
        waits = self._waits(q, reads, writes)
        self.dcnt[slot] += 16
        ev = (slot, self.dcnt[slot])
        self.q[q].append((waits, _capture(fn), (slot, 16)))
        self._record(ev, reads, writes)

    def barrier(self):
        for e in self.ENG:
            waits = []
            for k in list(self.ENG) + list(self.dcnt.keys()):
                if k == e:
                    continue
                v = self.cnt[k] if k in self.cnt else self.dcnt[k]
                if v > 0 and self.seen[e].get(k, 0) < v:
                    self.seen[e][k] = v
                    waits.append((k, v))
            if waits:
                self.q[e].append((waits, None, None))

    def emit(self):
        nc = self.nc

        def replay(name, eng):
            for waits, fn, inc in self.q[name]:
                for k, v in waits:
                    eng.wait_ge(self.sems[k], v)
                if fn is None:
                    continue
                ins = fn(eng)
                try:
                    self.names[ins.ins.name] = (name, fn.line)
                except Exception:
                    pass
                if inc is not None:
                    ins.then_inc(self.sems[inc[0]], inc[1])

        with nc.Block() as block:
            @block.sync
            def _(e):
                replay("sp", e)

            @block.tensor
            def _(e):
                replay("pe", e)

            @block.scalar
            def _(e):
                replay("act", e)

            @block.vector
            def _(e):
                replay("dve", e)

            @block.gpsimd
            def _(e):
                replay("pool", e)


class Arena:
    def __init__(self, ap, size):
        self.ap = ap
        self.size = size
        self.off = 0

    def mark(self):
        return self.off

    def reset(self, m):
        self.off = m

    def alloc(self, shape, dtype=F32):
        n = int(np.prod(shape))
        n32 = n if dtype == F32 else (n + 1) // 2
        assert self.off + n32 <= self.size, ("SBUF arena overflow", self.off, n32, self.size)
        v = self.ap[:, self.off:self.off + n32]
        self.off += n32
        if dtype == BF16:
            v = v.bitcast(BF16)
        if len(shape) == 2:
            v = v.rearrange("p (a b) -> p a b", a=shape[0])
        elif len(shape) == 3:
            v = v.rearrange("p (a b c) -> p a b c", a=shape[0], b=shape[1])
        return T(v)


class _Stop(Exception):
    pass


STOP_AT = [None]
STQ = "pool"
DEBUG = [False]


class Prog:
    def dbg(self, name, t):
        if not DEBUG[0]:
            return
        nm = "dbg_%s_%d" % (name, len(self.dbg_names))
        self.dbg_names.append(nm)
        shp = list(t.ap.shape)
        d = self.nc.dram_tensor(nm, shp, t.ap.dtype, kind="ExternalOutput").ap()
        sl = self.sc.slot(nm)
        self.sc.dma("sp", sl, lambda e: e.dma_start(out=d, in_=t.ap), reads=[t])

    def ck(self, n):
        if STOP_AT[0] is not None and n >= STOP_AT[0]:
            raise _Stop()

    def __init__(self, S, layer_ids):
        self.S = S
        self.layer_ids = list(layer_ids)
        self.nST = S // ST
        self.in_names = []
        self.dbg_names = []
        nc = self.nc = bass.Bass("TRN2", target_bir_lowering=False)
        self.dram = {}
        self._din("x", [S, D])
        self._din("ccol", [128, 8])
        self._din("ident", [128, 128])
        self._din("tri", [128, 128])
        self._din("uneg", [128, 128])
        self._din("negmask", [128, 128])
        self._din("blockones", [128, 128])
        self._din("sel", [65, 64])
        for L in self.layer_ids:
            kind = L % 3
            p = "L%d_" % L
            self._din(p + "wmod", [D, 3 * D])
            self._din(p + "bmodc", [128, 24])
            self._din(p + "bmodg", [128, D])
            self._din(p + "gprec", [128, 8])
            self._din(p + "gpost", [128, D])
            self._din(p + "win", [D, N_IN[kind]])
            self._din(p + "wout", [D, D])
            if kind == 0:
                self._din(p + "wa2", [17, 512])
                self._din(p + "ghc", [128, 8])
            if kind == 2:
                self._din(p + "gqk", [128, 2])
                self._din(p + "bf", [16, 1])
            if kind == 1:
                self._din(p + "lng", [128, D])
                self._din(p + "lnb", [128, D])
                self._din(p + "ws", [128, 4, 128])
                self._din(p + "bs", [128, 4, 128])
        self.out = nc.dram_tensor("out", [S, D], F32, kind="ExternalOutput").ap()
        self.xs = [nc.dram_tensor("xs%d" % i, [S, D], F32, kind="Internal").ap() for i in range(2)]
        if any(L % 3 == 2 for L in self.layer_ids):
            self.QA = nc.dram_tensor("fox_qa", [16, 70, S], BF16, kind="Internal").ap()
            self.KA = nc.dram_tensor("fox_ka", [16, 70, S], BF16, kind="Internal").ap()
            self.VS = nc.dram_tensor("fox_v", [S, D], BF16, kind="Internal").ap()
            self.SZ = nc.dram_tensor("fox_sz", [D, S], BF16, kind="Internal").ap()
            self.OG = nc.dram_tensor("fox_og", [D, S], BF16, kind="Internal").ap()

    def _din(self, name, shape, dtype=F32):
        self.in_names.append(name)
        self.dram[name] = self.nc.dram_tensor(name, shape, dtype, kind="ExternalInput").ap()

    def rstd_from_ss(self, ss, out, n_inv, tmp):
        sc = self.sc
        sc.op("act", lambda e: e.activation(out=tmp.ap, in_=ss.ap, func=AF.Ln, scale=n_inv, bias=EPS),
              reads=[ss], writes=[tmp])
        sc.op("act", lambda e: e.activation(out=out.ap, in_=tmp.ap, func=AF.Exp, scale=-0.5),
              reads=[tmp], writes=[out])

    def build(self):
        nc = self.nc
        with ExitStack() as es:
            arena_t = es.enter_context(nc.sbuf_tensor("arena", [128, 53000], F32))
            ps_t = es.enter_context(nc.psum_tensor("ps", [128, 8, 512], F32))
            self.sc = sc = Sched(nc, es)
            self.A = A = Arena(arena_t[:, :], 53000)
            self.ps = ps_t
            self.psbuf = [Buf(excl=True) for _ in range(8)]
            self.slots = {}
            for nm in ["c0", "stg0", "stg1", "x0", "x1", "x2", "x3", "xo0", "xo1", "xo2", "xo3", "xs0", "xs1", "xs2", "xs3", "sm0", "sm1", "sm2", "sm3",
                       "fq0", "fq1", "fv0", "fv1", "fz0", "ff0", "fk0", "fk1", "fvh0", "fvh1", "fqq0", "fqq1", "fqq2",
                       "fsz0", "fsz1", "fog0", "fog1", "fo0", "fo1"]:
                self.slots[nm] = sc.slot(nm)
            self.global_consts()
            x_in = self.dram["x"]
            try:
                for li, L in enumerate(self.layer_ids):
                    x_out = self.out if li == len(self.layer_ids) - 1 else self.xs[li % 2]
                    m = A.mark()
                    self.layer(L, x_in, x_out)
                    sc.barrier()
                    A.reset(m)
                    x_in = x_out
            except _Stop:
                pass
            sc.barrier()
            sc.emit()
        return nc

    def pst(self, bank, n=1):
        if n == 1:
            return self.ps[:, bank, :]
        return self.ps[:, bank:bank + n, :].rearrange("p a b -> p (a b)")

    def global_consts(self):
        sc, A, dr = self.sc, self.A, self.dram
        sl = self.slots["c0"]
        self.identf = A.alloc([128])
        self.identb = A.alloc([128], BF16)
        self.trif = A.alloc([128])
        self.unegf = A.alloc([128])
        self.onesb = A.alloc([128], BF16)
        self.ones = A.alloc([128])
        self.cond = A.alloc([8])
        self.cond_rep = A.alloc([8, 128])
        cc = A.alloc([8])
        sc.dma("sp", sl, lambda e: e.dma_start(out=self.identf.ap, in_=dr["ident"]), writes=[self.identf])
        sc.dma("sp", sl, lambda e: e.dma_start(out=self.trif.ap, in_=dr["tri"]), writes=[self.trif])
        sc.dma("sp", sl, lambda e: e.dma_start(out=self.unegf.ap, in_=dr["uneg"]), writes=[self.unegf])
        sc.dma("sp", sl, lambda e: e.dma_start(out=cc.ap, in_=dr["ccol"]), writes=[cc])
        sc.barrier()
        sc.op("dve", lambda e: e.tensor_copy(out=self.identb.ap, in_=self.identf.ap), reads=[self.identf], writes=[self.identb])
        sc.op("dve", lambda e: e.memset(self.ones.ap, 1.0), writes=[self.ones])
        sc.op("dve", lambda e: e.memset(self.onesb.ap, 1.0), writes=[self.onesb])
        sc.op("act", lambda e: e.activation(out=self.cond.ap, in_=cc.ap, func=AF.Silu), reads=[cc], writes=[self.cond])
        for kc in range(KC):
            sc.op("dve", lambda e, kc=kc: e.tensor_scalar(out=self.cond_rep[:, kc, :], in0=self.ones.ap,
                                                         scalar1=self.cond[:, kc:kc + 1], scalar2=None, op0=ALU.mult),
                  reads=[self.ones, self.cond], writes=[self.cond_rep])

    def prep(self, L, tokmajor_ranges, nocol_ranges=None, wout_rowscale=None):
        sc, A, dr = self.sc, self.A, self.dram
        kind = L % 3
        p = "L%d_" % L
        nin = N_IN[kind]
        BW = 256
        if nocol_ranges is None:
            nocol_ranges = tokmajor_ranges
        self.Win = A.alloc([KC, nin], BF16)
        self.Wout = A.alloc([KC, D], BF16)
        self.Gbc = A.alloc([D])
        nch = (nin + 127) // 128
        self.biascol = A.alloc([nch])
        self.biasrow = {r: A.alloc([r[1] - r[0]]) for r in tokmajor_ranges}
        tmp_mark = A.mark()
        stg = [A.alloc([KC, BW]) for _ in range(2)]
        stg_slot = [self.slots["stg0"], self.slots["stg1"]]
        modc = A.alloc([16])
        acol = A.alloc([8])
        shift_rep = A.alloc([KC, 128])
        small = A.alloc([24 + 8])
        bmodc, gprec = T(small[:, 0:24], small.buf), T(small[:, 24:32], small.buf)
        gtmp = A.alloc([D])
        gpost = A.alloc([D])
        sl = self.slots["c0"]
        sc.dma("sp", sl, lambda e: e.dma_start(out=bmodc.ap, in_=dr[p + "bmodc"]), writes=[small])
        sc.dma("sp", sl, lambda e: e.dma_start(out=gprec.ap, in_=dr[p + "gprec"]), writes=[small])
        sc.dma("sp", sl, lambda e: e.dma_start(out=gtmp.ap, in_=dr[p + "bmodg"]), writes=[gtmp])
        sc.dma("sp", sl, lambda e: e.dma_start(out=gpost.ap, in_=dr[p + "gpost"]), writes=[gpost])
        sc.barrier()
        blk = [0]

        def load_block(src, c0, w):
            i = blk[0] % 2
            blk[0] += 1
            s = stg[i]
            sc.dma("sp", stg_slot[i],
                   lambda e: e.dma_start(out=s[:, :, 0:w], in_=src[:, c0:c0 + w].rearrange("(kc p) n -> p kc n", p=128)),
                   writes=[s])
            return s

        PB_MOD, PB_G, PB_BC, PB_BR = 0, 1, 3, 4
        psmod = T(self.pst(PB_MOD), self.psbuf[PB_MOD])
        psG = [T(self.pst(PB_G + i), self.psbuf[PB_G + i]) for i in range(2)]
        wmod = dr[p + "wmod"]
        for j in range(12):
            s = load_block(wmod, j * BW, BW)
            if j < 8:
                fns = []
                for h in range(2):
                    ch = j * 2 + h
                    for kc in range(KC):
                        fns.append(lambda e, ch=ch, h=h, kc=kc, s=s: e.matmul(
                            psmod[:, ch:ch + 1], lhsT=s[:, kc, h * 128:(h + 1) * 128], rhs=self.cond[:, kc:kc + 1],
                            start=(kc == 0), stop=(kc == KC - 1)))
                sc.pe_group(fns, reads=[s, self.cond], writes=[psmod])
            else:
                g = j - 8
                fns = [lambda e, kc=kc, s=s, g=g: e.matmul(
                    psG[g // 2][:, (g % 2) * BW:(g % 2 + 1) * BW], lhsT=self.cond_rep[:, kc, :], rhs=s[:, kc, :],
                    start=(kc == 0), stop=(kc == KC - 1)) for kc in range(KC)]
                sc.pe_group(fns, reads=[s, self.cond_rep], writes=[psG[g // 2]])
        self.ck(1)
        sc.op("dve", lambda e: e.tensor_tensor(out=modc.ap, in0=psmod[:, 0:16], in1=bmodc[:, 0:16], op=ALU.add),
              reads=[psmod, small], writes=[modc])
        sc.op("dve", lambda e: e.scalar_tensor_tensor(out=acol.ap, in0=modc[:, 8:16], scalar=1.0, in1=gprec.ap,
                                                      op0=ALU.add, op1=ALU.mult),
              reads=[modc, small], writes=[acol])
        for kc in range(KC):
            sc.op("dve", lambda e, kc=kc: e.tensor_scalar(out=shift_rep[:, kc, :], in0=self.ones.ap,
                                                         scalar1=modc[:, kc:kc + 1], scalar2=None, op0=ALU.mult),
                  reads=[self.ones, modc], writes=[shift_rep])
        for i in range(2):
            sc.op("dve", lambda e, i=i: e.tensor_tensor(out=gtmp[:, i * 512:(i + 1) * 512], in0=psG[i].ap,
                                                       in1=gtmp[:, i * 512:(i + 1) * 512], op=ALU.add),
                  reads=[psG[i], gtmp], writes=[gtmp])
        sc.op("dve", lambda e: e.tensor_tensor(out=self.Gbc.ap, in0=gtmp.ap, in1=gpost.ap, op=ALU.mult),
              reads=[gtmp, gpost], writes=[self.Gbc])
        self.dbg("modc", modc); self.dbg("acol", acol); self.dbg("Gbc", self.Gbc)
        self.ck(2)
        win = dr[p + "win"]
        psbc = T(self.pst(PB_BC), self.psbuf[PB_BC])
        psbr = [T(self.pst(PB_BR + i), self.psbuf[PB_BR + i]) for i in range(2)]
        written = []
        c0 = 0
        tog = 0
        while c0 < nin:
            w = min(BW, nin - c0)
            s = load_block(win, c0, w)
            rng = None
            for r in tokmajor_ranges:
                if r[0] <= c0 < r[1]:
                    rng = r
            if rng is not None:
                pb = psbr[tog % 2]
                tog += 1
                fns = [lambda e, kc=kc, s=s, pb=pb, w=w: e.matmul(pb[:, 0:w], lhsT=shift_rep[:, kc, :], rhs=s[:, kc, 0:w],
                                                                   start=(kc == 0), stop=(kc == KC - 1)) for kc in range(KC)]
                sc.pe_group(fns, reads=[s, shift_rep], writes=[pb])
                br = self.biasrow[rng]
                o = c0 - rng[0]
                sc.op("act", lambda e, br=br, o=o, w=w, pb=pb: e.copy(out=br[:, o:o + w], in_=pb[:, 0:w]),
                      reads=[pb], writes=[br])
            if not any(r[0] <= c0 < r[1] for r in nocol_ranges):
                fns = []
                for h in range((w + 127) // 128):
                    ch = c0 // 128 + h
                    hw = min(128, w - h * 128)
                    written.append((ch, hw))
                    for kc in range(KC):
                        fns.append(lambda e, ch=ch, h=h, hw=hw, kc=kc, s=s: e.matmul(
                            psbc[0:hw, ch:ch + 1], lhsT=s[:, kc, h * 128:h * 128 + hw], rhs=modc[:, kc:kc + 1],
                            start=(kc == 0), stop=(kc == KC - 1)))
                sc.pe_group(fns, reads=[s, modc], writes=[psbc])
            for kc in range(KC):
                eng = "dve" if kc % 2 == 0 else "pool"
                if eng == "dve":
                    sc.op(eng, lambda e, kc=kc, s=s, c0=c0, w=w: e.tensor_scalar(
                        out=self.Win[:, kc, c0:c0 + w], in0=s[:, kc, 0:w], scalar1=acol[:, kc:kc + 1], scalar2=None,
                        op0=ALU.mult), reads=[s, acol], writes=[self.Win])
                else:
                    sc.op(eng, lambda e, kc=kc, s=s, c0=c0, w=w: e.tensor_scalar(
                        out=self.Win[:, kc, c0:c0 + w], in0=s[:, kc, 0:w], scalar1=acol[:, kc:kc + 1], scalar2=1.0,
                        op0=ALU.mult, op1=ALU.mult), reads=[s, acol], writes=[self.Win])
            c0 += w
        for ch, hw in written:
            sc.op("dve", lambda e, ch=ch, hw=hw: e.tensor_copy(out=self.biascol[0:hw, ch:ch + 1], in_=psbc[0:hw, ch:ch + 1]),
                  reads=[psbc], writes=[self.biascol])
        self.dbg("Win", self.Win); self.dbg("biascol", T(self.biascol[:, 0:8], self.biascol.buf))
        for r_, t_ in self.biasrow.items():
            self.dbg("biasrow", t_)
        self.ck(3)
        wout = dr[p + "wout"]
        for j in range(D // BW):
            s = load_block(wout, j * BW, BW)
            if wout_rowscale is None:
                sc.op("dve", lambda e, s=s, j=j: e.tensor_copy(out=self.Wout[:, 0:4, j * BW:(j + 1) * BW], in_=s[:, 0:4, :]),
                      reads=[s], writes=[self.Wout])
                sc.op("act", lambda e, s=s, j=j: e.copy(out=self.Wout[:, 4:8, j * BW:(j + 1) * BW], in_=s[:, 4:8, :]),
                      reads=[s], writes=[self.Wout])
            else:
                for kc in range(KC):
                    sc.op("dve", lambda e, s=s, j=j, kc=kc: e.tensor_scalar(
                        out=self.Wout[:, kc, j * BW:(j + 1) * BW], in0=s[:, kc, :], scalar1=wout_rowscale[:, kc:kc + 1],
                        scalar2=None, op0=ALU.mult), reads=[s, wout_rowscale], writes=[self.Wout])
        sc.barrier()
        A.reset(tmp_mark)

    def alloc_stageA(self, nxt=4, nxnT=2):
        A = self.A
        self.xt = [A.alloc([D]) for _ in range(nxt)]
        self.xn = [A.alloc([D], BF16) for _ in range(2)]
        self.xnT = [A.alloc([KC, ST], BF16) for _ in range(nxnT)]
        self.junk = A.alloc([D], BF16)
        self.ssA = [A.alloc([4]) for _ in range(2)]
        self.lnA = [A.alloc([4]) for _ in range(2)]
        self.rsA = [A.alloc([4]) for _ in range(2)]
        self.psT = T(self.ps[:, 7, :].bitcast(BF16).rearrange("p (a b) -> p a b", a=8), self.psbuf[7])

    def stageA(self, j, x_in):
        sc = self.sc
        ss, ln, rs, xnT = self.ssA[j % 2], self.lnA[j % 2], self.rsA[j % 2], self.xnT[j % len(self.xnT)]
        nxt = len(self.xt)
        for tt in range(4):
            tok = j * ST + tt * 128
            xt = self.xt[tt % nxt]
            xn = self.xn[tt % 2]
            sc.dma("sp", self.slots["x%d" % (tt % nxt)], lambda e, xt=xt, tok=tok: e.dma_start(out=xt.ap, in_=x_in[tok:tok + 128, :]),
                   writes=[xt])
            sc.op("act", lambda e, xt=xt, tt=tt, ss=ss: e.activation(out=self.junk.ap, in_=xt.ap, func=AF.Square,
                                                                     accum_out=ss[:, tt:tt + 1]),
                  reads=[xt], writes=[self.junk, ss])
            sc.op("act", lambda e, tt=tt: e.activation(out=ln[:, tt:tt + 1], in_=ss[:, tt:tt + 1], func=AF.Ln, scale=1.0 / D, bias=EPS),
                  reads=[ss], writes=[ln])
            sc.op("act", lambda e, tt=tt: e.activation(out=rs[:, tt:tt + 1], in_=ln[:, tt:tt + 1], func=AF.Exp, scale=-0.5),
                  reads=[ln], writes=[rs])
            sc.op("act", lambda e, xt=xt, xn=xn, tt=tt, rs=rs: e.activation(out=xn.ap, in_=xt.ap, func=AF.Copy,
                                                                            scale=rs[:, tt:tt + 1]),
                  reads=[xt, rs], writes=[xn])
            fns = [lambda e, c=c, xn=xn: e.transpose(out=self.psT[:, c, :], in_=xn[:, c * 128:(c + 1) * 128],
                                                     identity=self.identb.ap) for c in range(KC)]
            sc.pe_group(fns, reads=[xn, self.identb], writes=[self.psT])
            sc.op("dve", lambda e, tt=tt, xnT=xnT: e.tensor_copy(out=xnT[:, :, tt * 128:(tt + 1) * 128], in_=self.psT.ap),
                  reads=[self.psT], writes=[xnT])
        return xnT

    def alloc_stageO(self, nxo=4, nt1=2):
        A = self.A
        self.xo = [A.alloc([D]) for _ in range(nxo)]
        self.t1 = [A.alloc([D]) for _ in range(nt1)]
        self.ssO = [A.alloc([4]) for _ in range(2)]
        self.lnO = [A.alloc([4]) for _ in range(2)]
        self.rsO = [A.alloc([4]) for _ in range(2)]

    def stageO(self, j, ogT, x_in, x_out, psY):
        sc = self.sc
        ss, ln, rs = self.ssO[j % 2], self.lnO[j % 2], self.rsO[j % 2]
        for tt in range(4):
            tok = j * ST + tt * 128
            py = psY[tt % 2]
            xo = self.xo[tt % 2]
            t1 = self.t1[tt % 2]
            sc.dma("sp", self.slots["xo%d" % (tt % 2)], lambda e, xo=xo, tok=tok: e.dma_start(out=xo.ap, in_=x_in[tok:tok + 128, :]),
                   writes=[xo])
            fns = []
            for nb in range(2):
                for c in range(KC):
                    fns.append(lambda e, nb=nb, c=c, py=py, tt=tt: e.matmul(
                        py[:, nb * 512:(nb + 1) * 512], lhsT=ogT[:, c, tt * 128:(tt + 1) * 128],
                        rhs=self.Wout[:, c, nb * 512:(nb + 1) * 512], start=(c == 0), stop=(c == KC - 1)))
            sc.pe_group(fns, reads=[ogT, self.Wout], writes=[py])
            sc.op("act", lambda e, py=py, tt=tt, ss=ss: e.activation(out=self.junk.ap, in_=py.ap, func=AF.Square,
                                                                     accum_out=ss[:, tt:tt + 1]),
                  reads=[py], writes=[self.junk, ss])
            sc.op("dve", lambda e, py=py, t1=t1: e.tensor_tensor(out=t1.ap, in0=py.ap, in1=self.Gbc.ap, op=ALU.mult),
                  reads=[py, self.Gbc], writes=[t1])
        self.rstd_from_ss(ss, rs, 1.0 / D, ln)
        return ss, rs

    def stageO_full(self, j, ogT, x_in, x_out, psY):
        for tt in range(4):
            self.stageO_tile(j, tt, ogT, x_in, x_out, psY)

    def stageO_tile(self, j, tt, ogT, x_in, x_out, psY, og_dep=None):
        sc = self.sc
        og_dep = ogT if og_dep is None else og_dep
        if True:
            tok = j * ST + tt * 128
            k = (j * 4 + tt) % 2
            py = psY[k]
            xi = tt % len(self.xo)
            xo = self.xo[xi]
            t1 = self.t1[k % len(self.t1)]
            ss, ln, rs = self.ssO[k], self.lnO[k], self.rsO[k]
            sc.dma("sp", self.slots["xo%d" % xi], lambda e, xo=xo, tok=tok: e.dma_start(out=xo.ap, in_=x_in[tok:tok + 128, :]),
                   writes=[xo])
            fns = []
            for nb in range(2):
                for c in range(KC):
                    fns.append(lambda e, nb=nb, c=c, py=py, tt=tt: e.matmul(
                        py[:, nb * 512:(nb + 1) * 512], lhsT=ogT[:, c, tt * 128:(tt + 1) * 128],
                        rhs=self.Wout[:, c, nb * 512:(nb + 1) * 512], start=(c == 0), stop=(c == KC - 1)))
            self.ck(12)
            sc.pe_group(fns, reads=[og_dep, self.Wout], writes=[py])
            self.ck(13)
            for nb in range(2):
                sc.op("act", lambda e, py=py, ss=ss, nb=nb: e.activation(out=self.junk[:, nb * 512:(nb + 1) * 512], in_=py[:, nb * 512:(nb + 1) * 512],
                                                                         func=AF.Square, accum_out=ss[:, 1 + nb:2 + nb]),
                      reads=[py], writes=[self.junk, ss])
                sc.op("dve", lambda e, py=py, t1=t1, nb=nb: e.tensor_tensor(out=t1[:, nb * 512:(nb + 1) * 512], in0=py[:, nb * 512:(nb + 1) * 512],
                                                                            in1=self.Gbc[:, nb * 512:(nb + 1) * 512], op=ALU.mult),
                      reads=[py, self.Gbc], writes=[t1])
            self.ck(14)
            sc.op("dve", lambda e, ss=ss: e.tensor_tensor(out=ss[:, 0:1], in0=ss[:, 1:2], in1=ss[:, 2:3], op=ALU.add),
                  reads=[ss], writes=[ss])
            sc.op("act", lambda e, ss=ss, ln=ln: e.activation(out=ln[:, 0:1], in_=ss[:, 0:1], func=AF.Ln, scale=1.0 / D, bias=EPS),
                  reads=[ss], writes=[ln])
            sc.op("act", lambda e, rs=rs, ln=ln: e.activation(out=rs[:, 0:1], in_=ln[:, 0:1], func=AF.Exp, scale=-0.5),
                  reads=[ln], writes=[rs])
            self.ck(15)
            sc.op("dve", lambda e, t1=t1, xo=xo, rs=rs: e.scalar_tensor_tensor(out=xo.ap, in0=t1.ap, scalar=rs[:, 0:1], in1=xo.ap,
                                                                               op0=ALU.mult, op1=ALU.add),
                  reads=[t1, xo, rs], writes=[xo])
            self.ck(16)
            sc.dma(STQ, self.slots["xs%d" % xi], lambda e, xo=xo, tok=tok: e.dma_start(out=x_out[tok:tok + 128, :], in_=xo.ap),
                   reads=[xo])

    def layer(self, L, x_in, x_out):
        kind = L % 3
        if kind == 1:
            self.layer_sgu(L, x_in, x_out)
        elif kind == 0:
            self.layer_gla(L, x_in, x_out)
        else:
            self.layer_fox(L, x_in, x_out)


    def layer_gla(self, L, x_in, x_out):
        sc, A, dr = self.sc, self.A, self.dram
        p = "L%d_" % L
        ghc = A.alloc([8])
        sl = self.slots["c0"]
        sc.dma("sp", sl, lambda e: e.dma_start(out=ghc.ap, in_=dr[p + "ghc"]), writes=[ghc])
        sc.barrier()
        self.prep(L, [(512, 2048)], nocol_ranges=[(1024, 2048)], wout_rowscale=ghc)
        self.alloc_stageA(nxt=2, nxnT=2)
        self.alloc_stageO(nxo=2, nt1=1)
        wa2 = A.alloc([512])
        sc.dma("sp", sl, lambda e: e.dma_start(out=wa2[0:17, :], in_=dr[p + "wa2"]), writes=[wa2])
        sc.barrier()
        alT = A.alloc([ST])
        sc.op("dve", lambda e: e.memset(alT[0:17, :], 1.0), writes=[alT])
        f512 = A.alloc([512])
        spt = [A.alloc([512]) for _ in range(2)]
        enb = [A.alloc([512]) for _ in range(1)]
        ebT = A.alloc([4, ST])
        enbT = A.alloc([4, ST])
        elast = [A.alloc([4]) for _ in range(4)]
        qT = A.alloc([4, ST], BF16)
        kT = A.alloc([4, ST], BF16)
        ktok = [A.alloc([512], BF16) for _ in range(4)]
        vtok = [A.alloc([D], BF16) for _ in range(4)]
        szT = A.alloc([KC, ST], BF16)
        ogT = A.alloc([KC, ST], BF16)
        ogp = [T(ogT.ap, Buf()) for _ in range(4)]
        ATs = [A.alloc([4, 128], BF16) for _ in range(2)]
        osq = A.alloc([8, 128], BF16)
        rstd = A.alloc([4, 128])
        otmp = A.alloc([8, 128], BF16)
        Sst = A.alloc([4, 256])
        Sbf = A.alloc([4, 256], BF16)
        sc.op("dve", lambda e: e.memset(Sst.ap, 0.0), writes=[Sst])
        sc.op("dve", lambda e: e.memset(Sbf.ap, 0.0), writes=[Sbf])
        psW = [T(self.pst(0, 2), self.psbuf[0]), T(self.pst(2, 2), self.psbuf[2])]
        psA = [T(self.pst(4 + i), self.psbuf[4 + i]) for i in range(3)]
        brow = self.biasrow[(512, 2048)]
        LNS = -0.5 * float(np.log(128.0))
        tri4 = self.trif.ap.unsqueeze(1).to_broadcast([128, 4, 128])
        ia = [0]
        iw = [0]

        def nextA():
            t = psA[ia[0] % 3]
            ia[0] += 1
            return t

        def nextW():
            t = psW[iw[0] % 2]
            iw[0] += 1
            return t

        xnT_next = self.stageA(0, x_in)
        for j in range(self.nST):
            xnT = xnT_next
            for c in range(KC):
                pa = nextA()
                fns = [lambda e, kc=kc: e.matmul(pa.ap, lhsT=self.Win[:, kc, 2048 + c * 128:2048 + (c + 1) * 128],
                                                 rhs=xnT[:, kc, :], start=(kc == 0), stop=(kc == KC - 1)) for kc in range(KC)]
                sc.pe_group(fns, reads=[xnT, self.Win], writes=[pa])
                sc.op("act", lambda e: e.activation(out=szT[:, c, :], in_=pa.ap, func=AF.Silu, bias=self.biascol[:, 16 + c:17 + c]),
                      reads=[pa, self.biascol], writes=[szT])
            pa = nextA()
            fns = [lambda e, kc=kc: e.matmul(pa[0:16, :], lhsT=self.Win[:, kc, 3072:3088], rhs=xnT[:, kc, :],
                                             start=(kc == 0), stop=(kc == KC - 1)) for kc in range(KC)]
            sc.pe_group(fns, reads=[xnT, self.Win], writes=[pa])
            sc.op("act", lambda e: e.activation(out=alT[0:16, :], in_=pa[0:16, :], func=AF.Identity, bias=self.biascol[0:16, 24:25]),
                  reads=[pa, self.biascol], writes=[alT])
            def emit_xa(tt):
                ts_ = slice(tt * 128, (tt + 1) * 128)
                pa = nextA()
                sc.pe_group([lambda e: e.matmul(pa.ap, lhsT=alT[0:17, ts_], rhs=wa2[0:17, :], start=True, stop=True)],
                            reads=[alT, wa2], writes=[pa])
                sp_ = spt[tt % 2]
                sc.op("act", lambda e: e.activation(out=f512.ap, in_=pa.ap, func=AF.Exp, scale=-1.0), reads=[pa], writes=[f512])
                sc.op("act", lambda e: e.activation(out=sp_.ap, in_=f512.ap, func=AF.Ln, bias=1.0), reads=[f512], writes=[sp_])

            emit_xa(0)
            emit_xa(1)
            for tt in range(4):
                ts_ = slice(tt * 128, (tt + 1) * 128)
                sp_ = spt[tt % 2]
                pb = nextA()
                sc.pe_group([lambda e: e.matmul(pb.ap, lhsT=self.unegf.ap, rhs=sp_.ap, start=True, stop=True)],
                            reads=[self.unegf, sp_], writes=[pb])
                en_ = enb[0]
                sc.op("act", lambda e: e.activation(out=en_.ap, in_=pb.ap, func=AF.Exp, scale=-1.0), reads=[pb], writes=[en_])
                pc = nextA()
                fns = [lambda e, h=h: e.matmul(pc[:, h * 128:(h + 1) * 128], lhsT=sp_[:, h * 128:(h + 1) * 128],
                                               rhs=self.unegf.ap, start=True, stop=True) for h in range(4)]
                sc.pe_group(fns, reads=[self.unegf, sp_], writes=[pc])
                pc3 = pc.ap.rearrange("p (h t) -> p h t", h=4)
                sc.op("act", lambda e: e.activation(out=ebT[:, :, ts_], in_=pc3, func=AF.Exp, bias=LNS), reads=[pc], writes=[ebT])
                sc.op("act", lambda e: e.activation(out=enbT[:, :, ts_], in_=pc3, func=AF.Exp, scale=-1.0), reads=[pc], writes=[enbT])
                el = elast[tt]
                sc.op("act", lambda e: e.activation(out=el.ap, in_=pc3[:, :, 127], func=AF.Exp), reads=[pc], writes=[el])
                if tt + 2 < 4:
                    emit_xa(tt + 2)
                pk = nextA()
                fns = [lambda e, kc=kc: e.matmul(pk.ap, lhsT=xnT[:, kc, ts_], rhs=self.Win[:, kc, 512:1024],
                                                 start=(kc == 0), stop=(kc == KC - 1)) for kc in range(KC)]
                sc.pe_group(fns, reads=[xnT, self.Win], writes=[pk])
                sc.op("dve", lambda e: e.tensor_tensor(out=f512.ap, in0=pk.ap, in1=brow[:, 0:512], op=ALU.add),
                      reads=[pk, brow], writes=[f512])
                sc.op("pool", lambda e: e.tensor_tensor(out=ktok[tt].ap, in0=f512.ap, in1=en_.ap, op=ALU.mult),
                      reads=[f512, en_], writes=[ktok[tt]])
                pv = nextW()
                fns = []
                for nb in range(2):
                    for kc in range(KC):
                        fns.append(lambda e, nb=nb, kc=kc: e.matmul(
                            pv[:, nb * 512:(nb + 1) * 512], lhsT=xnT[:, kc, ts_],
                            rhs=self.Win[:, kc, 1024 + nb * 512:1024 + (nb + 1) * 512], start=(kc == 0), stop=(kc == KC - 1)))
                sc.pe_group(fns, reads=[xnT, self.Win], writes=[pv])
                sc.op("dve", lambda e: e.tensor_tensor(out=vtok[tt].ap, in0=pv.ap, in1=brow[:, 512:1536], op=ALU.add),
                      reads=[pv, brow], writes=[vtok[tt]])
            for h in range(4):
                pa = nextA()
                fns = [lambda e, kc=kc: e.matmul(pa.ap, lhsT=self.Win[:, kc, h * 128:(h + 1) * 128], rhs=xnT[:, kc, :],
                                                 start=(kc == 0), stop=(kc == KC - 1)) for kc in range(KC)]
                sc.pe_group(fns, reads=[xnT, self.Win], writes=[pa])
                sc.op("dve", lambda e: e.scalar_tensor_tensor(out=qT[:, h, :], in0=pa.ap, scalar=self.biascol[:, h:h + 1],
                                                              in1=ebT[:, h, :], op0=ALU.add, op1=ALU.mult),
                      reads=[pa, self.biascol, ebT], writes=[qT])
                pa2 = nextA()
                fns = [lambda e, kc=kc: e.matmul(pa2.ap, lhsT=self.Win[:, kc, 512 + h * 128:512 + (h + 1) * 128],
                                                 rhs=xnT[:, kc, :], start=(kc == 0), stop=(kc == KC - 1)) for kc in range(KC)]
                sc.pe_group(fns, reads=[xnT, self.Win], writes=[pa2])
                sc.op("dve", lambda e: e.scalar_tensor_tensor(out=kT[:, h, :], in0=pa2.ap, scalar=self.biascol[:, 4 + h:5 + h],
                                                              in1=enbT[:, h, :], op0=ALU.add, op1=ALU.mult),
                      reads=[pa2, self.biascol, enbT], writes=[kT])
            if j + 1 < self.nST:
                xnT_next = self.stageA(j + 1, x_in)

            def emit_AT(tt):
                ts_ = slice(tt * 128, (tt + 1) * 128)
                pa = nextA()
                fns = [lambda e, h=h: e.matmul(pa[:, h * 128:(h + 1) * 128], lhsT=kT[:, h, ts_], rhs=qT[:, h, ts_],
                                               start=True, stop=True) for h in range(4)]
                sc.pe_group(fns, reads=[kT, qT], writes=[pa])
                at = ATs[tt % 2]
                sc.op("dve", lambda e: e.tensor_tensor(out=at.ap, in0=pa.ap.rearrange("p (h t) -> p h t", h=4), in1=tri4, op=ALU.mult),
                      reads=[pa, self.trif], writes=[at])

            emit_AT(0)
            for tt in range(4):
                ts_ = slice(tt * 128, (tt + 1) * 128)
                if tt + 1 < 4:
                    emit_AT(tt + 1)
                at = ATs[tt % 2]
                po = psW[0]
                fns = []
                for h in range(4):
                    for half in range(2):
                        c = 2 * h + half
                        fns.append(lambda e, h=h, c=c: e.matmul(po[:, c * 128:(c + 1) * 128], lhsT=vtok[tt][:, c * 128:(c + 1) * 128],
                                                                rhs=at[:, h, :], start=True, stop=False))
                        fns.append(lambda e, h=h, c=c, half=half: e.matmul(po[:, c * 128:(c + 1) * 128],
                                                                           lhsT=Sbf[:, h, half * 128:(half + 1) * 128],
                                                                           rhs=qT[:, h, ts_], start=False, stop=True))
                sc.pe_group(fns, reads=[vtok[tt], at, Sbf, qT], writes=[po])
                el = elast[tt]
                for hp in range(2):
                    pP = nextA()
                    fns = [lambda e, hh=hh: e.matmul(pP[:, hh * 256:(hh + 1) * 256],
                                                     lhsT=ktok[tt][:, (2 * hp + hh) * 128:(2 * hp + hh + 1) * 128],
                                                     rhs=vtok[tt][:, (2 * hp + hh) * 256:(2 * hp + hh + 1) * 256], start=True, stop=True)
                           for hh in range(2)]
                    sc.pe_group(fns, reads=[ktok[tt], vtok[tt]], writes=[pP])
                    Sh = Sst[:, 2 * hp:2 * hp + 2, :]
                    sc.op("dve", lambda e: e.tensor_tensor(out=Sh, in0=pP.ap.rearrange("p (h v) -> p h v", h=2), in1=Sh, op=ALU.add),
                          reads=[pP, Sst], writes=[Sst])
                sc.op("pool", lambda e: e.tensor_tensor(out=Sst.ap, in0=Sst.ap, in1=el.ap.unsqueeze(2).to_broadcast([128, 4, 256]),
                                                        op=ALU.mult), reads=[Sst, el], writes=[Sst])
                sc.op("act", lambda e: e.copy(out=Sbf.ap, in_=Sst.ap), reads=[Sst], writes=[Sbf])
                for nb in range(2):
                    sc.op("act", lambda e, nb=nb: e.activation(
                        out=osq[:, nb * 4:(nb + 1) * 4, :], in_=po[:, nb * 512:(nb + 1) * 512].rearrange("p (c t) -> p c t", c=4),
                        func=AF.Square), reads=[po], writes=[osq])
                ps_ = nextA()
                fns = []
                for h in range(4):
                    for half in range(2):
                        fns.append(lambda e, h=h, half=half: e.matmul(ps_[:, h * 128:(h + 1) * 128], lhsT=self.onesb.ap,
                                                                      rhs=osq[:, 2 * h + half, :], start=(half == 0), stop=(half == 1)))
                sc.pe_group(fns, reads=[self.onesb, osq], writes=[ps_])
                sc.op("act", lambda e: e.activation(out=rstd.ap, in_=ps_.ap.rearrange("p (h t) -> p h t", h=4),
                                                    func=AF.Ln, scale=1.0 / 256, bias=EPS), reads=[ps_], writes=[rstd])
                sc.op("act", lambda e: e.activation(out=rstd.ap, in_=rstd.ap, func=AF.Exp, scale=-0.5), reads=[rstd], writes=[rstd])
                for nb in range(2):
                    sc.op("dve", lambda e, nb=nb: e.tensor_tensor(
                        out=otmp[:, nb * 4:(nb + 1) * 4, :].rearrange("p (h f) t -> p h f t", h=2),
                        in0=po[:, nb * 512:(nb + 1) * 512].rearrange("p (h f t) -> p h f t", h=2, f=2),
                        in1=rstd[:, 2 * nb:2 * nb + 2, :].unsqueeze(2).to_broadcast([128, 2, 2, 128]), op=ALU.mult),
                        reads=[po, rstd], writes=[otmp])
                sc.op("pool", lambda e: e.tensor_tensor(out=ogT[:, :, ts_], in0=otmp.ap, in1=szT[:, :, ts_], op=ALU.mult),
                      reads=[otmp, szT], writes=[ogp[tt]])
                if tt > 0:
                    self.stageO_tile(j, tt - 1, ogT, x_in, x_out, [psW[1], psW[1]], og_dep=ogp[tt - 1])
            self.stageO_tile(j, 3, ogT, x_in, x_out, [psW[1], psW[1]], og_dep=ogp[3])
        print("GLA arena words used:", A.off)

    def layer_fox(self, L, x_in, x_out):
        sc, A, dr, S = self.sc, self.A, self.dram, self.S
        p = "L%d_" % L
        QA, KA, VS, SZ, OG = self.QA, self.KA, self.VS, self.SZ, self.OG
        self.prep(L, [(2048, 3072)])
        sl = self.slots["c0"]
        gqk = A.alloc([2])
        bfc = A.alloc([1])
        negm_f = A.alloc([128])
        negm = A.alloc([128], BF16)
        bo_f = A.alloc([128])
        bones = A.alloc([128], BF16)
        self_sel = A.alloc([64])
        ones3 = A.alloc([3, ST], BF16)
        sc.dma("sp", sl, lambda e: e.dma_start(out=gqk.ap, in_=dr[p + "gqk"]), writes=[gqk])
        sc.dma("sp", sl, lambda e: e.dma_start(out=bfc[0:16, :], in_=dr[p + "bf"]), writes=[bfc])
        sc.dma("sp", sl, lambda e: e.dma_start(out=negm_f.ap, in_=dr["negmask"]), writes=[negm_f])
        sc.dma("sp", sl, lambda e: e.dma_start(out=bo_f.ap, in_=dr["blockones"]), writes=[bo_f])
        sc.dma("sp", sl, lambda e: e.dma_start(out=self_sel[0:65, :], in_=dr["sel"]), writes=[self_sel])
        sc.barrier()
        sc.op("dve", lambda e: e.tensor_copy(out=negm.ap, in_=negm_f.ap), reads=[negm_f], writes=[negm])
        sc.op("dve", lambda e: e.tensor_copy(out=bones.ap, in_=bo_f.ap), reads=[bo_f], writes=[bones])
        sc.op("dve", lambda e: e.memset(ones3.ap, 1.0), writes=[ones3])
        sc.op("dve", lambda e: e.tensor_scalar(out=gqk[:, 0:1], in0=gqk[:, 0:1], scalar1=0.125, scalar2=None, op0=ALU.mult),
              reads=[gqk], writes=[gqk])
        nfb = A.alloc([1])
        sc.op("dve", lambda e: e.tensor_tensor(out=nfb[0:16, :], in0=self.biascol[0:16, 32:33], in1=bfc[0:16, :], op=ALU.add),
              reads=[self.biascol, bfc], writes=[nfb])
        sc.op("dve", lambda e: e.tensor_scalar(out=nfb[0:16, :], in0=nfb[0:16, :], scalar1=-1.0, scalar2=None, op0=ALU.mult),
              reads=[nfb], writes=[nfb])
        ph_mark = A.mark()
        self.alloc_stageA(nxt=2, nxnT=2)
        sq = [A.alloc([ST], BF16) for _ in range(2)]
        rstd = [A.alloc([ST]) for _ in range(2)]
        tmpf = [A.alloc([ST]) for _ in range(2)]
        qn = [A.alloc([ST], BF16) for _ in range(2)]
        vtok = [A.alloc([D], BF16) for _ in range(2)]
        szT = A.alloc([KC, ST], BF16)
        Ff = [A.alloc([ST]) for _ in range(2)]
        fe = A.alloc([ST])
        fsp = A.alloc([ST])
        fr = A.alloc([ST])
        pcs = [A.alloc([ST], BF16) for _ in range(6)]
        sc.op("dve", lambda e: e.memset(fr.ap, 1.0), writes=[fr])
        psW = [T(self.pst(0, 2), self.psbuf[0]), T(self.pst(0, 2), self.psbuf[0])]
        psA = [T(self.pst(2 + i), self.psbuf[2 + i]) for i in range(5)]
        brow = self.biasrow[(2048, 3072)]
        ia = [0]

        def nextA():
            t = psA[ia[0] % 5]
            ia[0] += 1
            return t

        xnT_next = self.stageA(0, x_in)
        for j in range(self.nST):
            tok0 = j * ST
            xnT = xnT_next
            chunks = [(which, c) for which in range(2) for c in range(KC)]
            pas = {}

            def mmA(i):
                which, c = chunks[i]
                col0 = which * 1024 + c * 128
                pa = nextA()
                pas[i] = pa
                fns = [lambda e, kc=kc: e.matmul(pa.ap, lhsT=self.Win[:, kc, col0:col0 + 128], rhs=xnT[:, kc, :],
                                                 start=(kc == 0), stop=(kc == KC - 1)) for kc in range(KC)]
                sc.pe_group(fns, reads=[xnT, self.Win], writes=[pa])

            mmA(0)
            for i, (which, c) in enumerate(chunks):
                if i + 1 < len(chunks):
                    mmA(i + 1)
                DST = QA if which == 0 else KA
                bcol = self.biascol[:, which * 8 + c:which * 8 + c + 1]
                pa = pas.pop(i)
                sq_, rs_, tf_, q_ = sq[i % 2], rstd[i % 2], tmpf[i % 2], qn[i % 2]
                slq = self.slots["fq%d" % (i % 2)]
                sc.op("act", lambda e: e.activation(out=sq_.ap, in_=pa.ap, func=AF.Square, bias=bcol),
                      reads=[pa, self.biascol], writes=[sq_])
                pb = nextA()
                sc.pe_group([lambda e: e.matmul(pb.ap, lhsT=bones.ap, rhs=sq_.ap, start=True, stop=True)],
                            reads=[bones, sq_], writes=[pb])
                sc.op("act", lambda e: e.activation(out=rs_.ap, in_=pb.ap, func=AF.Ln, scale=1.0 / 64, bias=EPS),
                      reads=[pb], writes=[rs_])
                sc.op("act", lambda e: e.activation(out=rs_.ap, in_=rs_.ap, func=AF.Exp, scale=-0.5),
                      reads=[rs_], writes=[rs_])
                sc.op("dve", lambda e: e.scalar_tensor_tensor(out=tf_.ap, in0=pa.ap, scalar=bcol, in1=rs_.ap,
                                                              op0=ALU.add, op1=ALU.mult),
                      reads=[pa, self.biascol, rs_], writes=[tf_])
                sc.op("pool", lambda e: e.tensor_scalar(out=q_.ap, in0=tf_.ap, scalar1=gqk[:, which:which + 1],
                                                        scalar2=1.0, op0=ALU.mult, op1=ALU.mult),
                      reads=[tf_, gqk], writes=[q_])
                for hh in range(2):
                    sc.dma(STQ, slq, lambda e, hh=hh: e.dma_start(
                        out=DST[2 * c + hh, 0:64, tok0:tok0 + ST], in_=q_[hh * 64:(hh + 1) * 64, :]), reads=[q_])
            if j + 1 < self.nST:
                xnT_next = self.stageA(j + 1, x_in)
            for tt in range(4):
                ts_ = slice(tt * 128, (tt + 1) * 128)
                pv = psW[tt % 2]
                fns = []
                for nb in range(2):
                    for kc in range(KC):
                        fns.append(lambda e, nb=nb, kc=kc, pv=pv, ts_=ts_: e.matmul(
                            pv[:, nb * 512:(nb + 1) * 512], lhsT=xnT[:, kc, ts_],
                            rhs=self.Win[:, kc, 2048 + nb * 512:2048 + (nb + 1) * 512], start=(kc == 0), stop=(kc == KC - 1)))
                sc.pe_group(fns, reads=[xnT, self.Win], writes=[pv])
                v_ = vtok[tt % 2]
                sc.op("dve", lambda e, pv=pv, v_=v_: e.tensor_tensor(out=v_.ap, in0=pv.ap, in1=brow.ap, op=ALU.add),
                      reads=[pv, brow], writes=[v_])
                sc.dma(STQ, self.slots["fv%d" % (tt % 2)], lambda e, v_=v_, tt=tt, tok0=tok0: e.dma_start(
                    out=VS[tok0 + tt * 128:tok0 + (tt + 1) * 128, :], in_=v_.ap), reads=[v_])
            for c in range(KC):
                pa = nextA()
                fns = [lambda e, kc=kc, c=c, pa=pa: e.matmul(pa.ap, lhsT=self.Win[:, kc, 3072 + c * 128:3072 + (c + 1) * 128],
                                                             rhs=xnT[:, kc, :], start=(kc == 0), stop=(kc == KC - 1))
                       for kc in range(KC)]
                sc.pe_group(fns, reads=[xnT, self.Win], writes=[pa])
                sc.op("act", lambda e, c=c, pa=pa: e.activation(out=szT[:, c, :], in_=pa.ap, func=AF.Silu,
                                                                bias=self.biascol[:, 24 + c:25 + c]),
                      reads=[pa, self.biascol], writes=[szT])
            sc.dma(STQ, self.slots["fz0"], lambda e, tok0=tok0: e.dma_start(
                out=SZ[:, tok0:tok0 + ST].rearrange("(c p) t -> p c t", p=128), in_=szT.ap), reads=[szT])
            pa = nextA()
            fns = [lambda e, kc=kc, pa=pa: e.matmul(pa[0:16, :], lhsT=self.Win[:, kc, 4096:4112], rhs=xnT[:, kc, :],
                                                    start=(kc == 0), stop=(kc == KC - 1)) for kc in range(KC)]
            sc.pe_group(fns, reads=[xnT, self.Win], writes=[pa])
            sc.op("act", lambda e, pa=pa: e.activation(out=fe[0:16, :], in_=pa[0:16, :], func=AF.Exp, scale=-1.0, bias=nfb[0:16, :]),
                  reads=[pa, nfb], writes=[fe])
            sc.op("act", lambda e: e.activation(out=fsp[0:16, :], in_=fe[0:16, :], func=AF.Ln, bias=1.0),
                  reads=[fe], writes=[fsp])
            F_ = Ff[j % 2]
            Fp = Ff[(j + 1) % 2]
            init = 0.0 if j == 0 else Fp[0:16, ST - 1:ST]
            sc.op("dve", lambda e, F_=F_, init=init: e.tensor_tensor_scan(out=F_[0:16, :], data0=fr[0:16, :],
                                                                          data1=fsp[0:16, :], initial=init, op0=ALU.mult, op1=ALU.subtract),
                  reads=[fsp, fr] + ([Fp] if j > 0 else []), writes=[F_])
            sc.op("dve", lambda e, F_=F_: e.tensor_copy(out=pcs[0][0:16, :], in_=F_[0:16, :]), reads=[F_], writes=[pcs[0]])
            sc.op("dve", lambda e, F_=F_: e.tensor_tensor(out=fe[0:16, :], in0=F_[0:16, :], in1=pcs[0][0:16, :], op=ALU.subtract),
                  reads=[F_, pcs[0]], writes=[fe])
            sc.op("dve", lambda e: e.tensor_copy(out=pcs[1][0:16, :], in_=fe[0:16, :]), reads=[fe], writes=[pcs[1]])
            sc.op("dve", lambda e: e.tensor_tensor(out=fe[0:16, :], in0=fe[0:16, :], in1=pcs[1][0:16, :], op=ALU.subtract),
                  reads=[fe, pcs[1]], writes=[fe])
            sc.op("dve", lambda e: e.tensor_copy(out=pcs[2][0:16, :], in_=fe[0:16, :]), reads=[fe], writes=[pcs[2]])
            for i in range(3):
                sc.op("pool", lambda e, i=i: e.tensor_scalar(out=pcs[3 + i][0:16, :], in0=pcs[i][0:16, :], scalar1=-1.0, scalar2=1.0,
                                                             op0=ALU.mult, op1=ALU.mult), reads=[pcs[i]], writes=[pcs[3 + i]])
            for i in range(3):
                sc.dma(STQ, self.slots["ff0"], lambda e, i=i, tok0=tok0: e.dma_start(out=QA[:, 64 + i, tok0:tok0 + ST], in_=pcs[i][0:16, :]),
                       reads=[pcs[i]])
                sc.dma(STQ, self.slots["ff0"], lambda e, i=i, tok0=tok0: e.dma_start(out=KA[:, 67 + i, tok0:tok0 + ST], in_=pcs[3 + i][0:16, :]),
                       reads=[pcs[3 + i]])
            sc.dma(STQ, self.slots["ff0"], lambda e, tok0=tok0: e.dma_start(out=QA[:, 67:70, tok0:tok0 + ST], in_=ones3[0:16, :, :]),
                   reads=[ones3])
            sc.dma(STQ, self.slots["ff0"], lambda e, tok0=tok0: e.dma_start(out=KA[:, 64:67, tok0:tok0 + ST], in_=ones3[0:16, :, :]),
                   reads=[ones3])
        sc.barrier()
        self.ck(20)
        A.reset(ph_mark)
        NKT = S // 128
        NQB = S // ST
        KAh = [A.alloc([S], BF16) for _ in range(2)]
        Vh = [A.alloc([NKT, 65], BF16) for _ in range(2)]
        QAq = [A.alloc([ST], BF16) for _ in range(3)]
        szq = [A.alloc([ST], BF16) for _ in range(2)]
        PT = [A.alloc([2, ST], BF16) for _ in range(3)]
        Osb = [A.alloc([ST]) for _ in range(2)]
        tn = A.alloc([ST])
        ogq = [A.alloc([ST], BF16) for _ in range(2)]
        for v_ in Vh:
            sc.op("dve", lambda e, v_=v_: e.memset(v_.ap, 1.0), writes=[v_])
        psS = [T(self.ps[:, 0:2, :], self.psbuf[0]), T(self.ps[:, 2:4, :], self.psbuf[2])]
        psO = [T(self.pst(4), self.psbuf[4]), T(self.pst(5), self.psbuf[5])]
        psB = T(self.pst(6), self.psbuf[6])
        items = []
        nqb_total = 0
        for h in range(16):
            for qb in range(NQB):
                nk = 4 * qb + 4
                groups = [[kt, kt + 1] for kt in range(0, 4 * qb, 2)] + [[kt] for kt in range(4 * qb, nk)]
                for gi, g in enumerate(groups):
                    items.append(dict(h=h, qb=qb, g=g, nk=nk, first=(gi == 0), last=(gi == len(groups) - 1),
                                      iq=nqb_total, idx=len(items)))
                nqb_total += 1

        def load_head(h):
            ka = KAh[h % 2]
            vh = Vh[h % 2]
            sc.dma("sp", self.slots["fk%d" % (h % 2)], lambda e: e.dma_start(out=ka[0:70, :], in_=KA[h, :, :]), writes=[ka])
            sc.dma("sp", self.slots["fvh%d" % (h % 2)], lambda e: e.dma_start(
                out=vh[:, :, 0:64], in_=VS[:, h * 64:(h + 1) * 64].rearrange("(kt p) d -> p kt d", p=128)), writes=[vh])

        def emit_qk(it):
            h, qb, g, iq = it["h"], it["qb"], it["g"], it["iq"]
            ka = KAh[h % 2]
            qa = QAq[iq % 3]
            if it["first"]:
                q0t = qb * ST
                zq = szq[iq % 2]
                sc.dma("sp", self.slots["fqq%d" % (iq % 3)], lambda e: e.dma_start(out=qa[0:70, :], in_=QA[h, :, q0t:q0t + ST]),
                       writes=[qa])
                sc.dma("sp", self.slots["fsz%d" % (iq % 2)], lambda e: e.dma_start(out=zq[0:64, :], in_=SZ[h * 64:(h + 1) * 64, q0t:q0t + ST]),
                       writes=[zq])
            pS = psS[it["idx"] % 2]
            fns = []
            for i, kt in enumerate(g):
                r = kt - 4 * qb
                q0 = max(r, 0) * 128
                fns.append(lambda e, i=i, kt=kt, q0=q0, r=r: e.matmul(
                    pS[:, i, q0:ST], lhsT=ka[0:70, kt * 128:(kt + 1) * 128], rhs=qa[0:70, q0:ST], start=True, stop=(r < 0)))
                if r >= 0:
                    fns.append(lambda e, i=i, q0=q0: e.matmul(pS[:, i, q0:q0 + 128], lhsT=self.identb.ap, rhs=negm.ap,
                                                              start=False, stop=True))
            sc.pe_group(fns, reads=[ka, qa, self.identb, negm], writes=[pS])

        def emit_act(it):
            g, qb = it["g"], it["qb"]
            pS = psS[it["idx"] % 2]
            pt = PT[it["idx"] % 3]
            if len(g) == 2:
                sc.op("act", lambda e: e.activation(out=pt.ap, in_=pS.ap, func=AF.Exp), reads=[pS], writes=[pt])
            else:
                q0 = max(g[0] - 4 * qb, 0) * 128
                sc.op("act", lambda e: e.activation(out=pt[:, 0, q0:ST], in_=pS[:, 0, q0:ST], func=AF.Exp),
                      reads=[pS], writes=[pt])

        def emit_pv(it):
            h, qb, g, nk, iq = it["h"], it["qb"], it["g"], it["nk"], it["iq"]
            vh = Vh[h % 2]
            pt = PT[it["idx"] % 3]
            po = psO[iq % 2]
            fns = []
            for i, kt in enumerate(g):
                q0 = max(kt - 4 * qb, 0) * 128
                fns.append(lambda e, i=i, kt=kt, q0=q0: e.matmul(
                    po[0:65, q0:ST], lhsT=vh[:, kt, :], rhs=pt[:, i, q0:ST], start=(kt == 0), stop=(kt == nk - 1)))
            sc.pe_group(fns, reads=[vh, pt], writes=[po])

        def fin_a(it):
            iq = it["iq"]
            ob, po = Osb[iq % 2], psO[iq % 2]
            sc.op("dve", lambda e: e.tensor_copy(out=ob[0:65, :], in_=po[0:65, :]), reads=[po], writes=[ob])
            sc.op("dve", lambda e: e.reciprocal(out=ob[64:65, :], in_=ob[64:65, :]), reads=[ob], writes=[ob])

        def fin_b(it):
            h, qb, iq = it["h"], it["qb"], it["iq"]
            ob, og_, zq = Osb[iq % 2], ogq[iq % 2], szq[iq % 2]
            q0t = qb * ST
            sc.pe_group([lambda e: e.matmul(psB[0:64, :], lhsT=self_sel[0:65, :], rhs=ob[0:65, :], start=True, stop=True)],
                        reads=[self_sel, ob], writes=[psB])
            sc.op("dve", lambda e: e.tensor_tensor(out=tn[0:64, :], in0=ob[0:64, :], in1=psB[0:64, :], op=ALU.mult),
                  reads=[ob, psB], writes=[tn])
            sc.op("pool", lambda e: e.tensor_tensor(out=og_[0:64, :], in0=tn[0:64, :], in1=zq[0:64, :], op=ALU.mult),
                  reads=[tn, zq], writes=[og_])
            sc.dma(STQ, self.slots["fog%d" % (iq % 2)], lambda e: e.dma_start(
                out=OG[h * 64:(h + 1) * 64, q0t:q0t + ST], in_=og_[0:64, :]), reads=[og_])

        load_head(0)
        emit_qk(items[0])
        pending = []
        for i, it in enumerate(items):
            if i + 1 < len(items):
                emit_qk(items[i + 1])
            emit_act(it)
            emit_pv(it)
            if it["first"] and it["qb"] == 0 and it["h"] + 1 < 16:
                load_head(it["h"] + 1)
            for pnd in pending:
                pnd[0] -= 1
            while pending and pending[0][0] <= 0:
                fin_b(pending.pop(0)[1])
            if it["last"]:
                fin_a(it)
                pending.append([2, it])
        for pnd in pending:
            fin_b(pnd[1])
        sc.barrier()
        self.ck(21)
        A.reset(ph_mark)
        self.alloc_stageA(nxt=1, nxnT=1)
        self.alloc_stageO()
        ogT = [A.alloc([KC, ST], BF16) for _ in range(2)]
        psW = [T(self.pst(0, 2), self.psbuf[0]), T(self.pst(2, 2), self.psbuf[2])]
        for j in range(self.nST):
            tok0 = j * ST
            og = ogT[j % 2]
            sc.dma("sp", self.slots["fo%d" % (j % 2)], lambda e, og=og, tok0=tok0: e.dma_start(
                out=og.ap, in_=OG[:, tok0:tok0 + ST].rearrange("(c p) t -> p c t", p=128)), writes=[og])
            self.stageO_full(j, og, x_in, x_out, psW)

    def layer_sgu(self, L, x_in, x_out):
        sc, A, dr = self.sc, self.A, self.dram
        p = "L%d_" % L
        self.prep(L, [(1024, 2048)])
        sc.barrier()
        self.ck(4)
        self.alloc_stageA()
        self.alloc_stageO()
        lng = A.alloc([D])
        lnb = A.alloc([D])
        bsb = A.alloc([4, 128])
        wsf = A.alloc([4, 128])
        WcT = A.alloc([4, 128], BF16)
        sl = self.slots["c0"]
        sc.dma("sp", sl, lambda e: e.dma_start(out=lng.ap, in_=dr[p + "lng"]), writes=[lng])
        sc.dma("sp", sl, lambda e: e.dma_start(out=lnb.ap, in_=dr[p + "lnb"]), writes=[lnb])
        sc.dma("sp", sl, lambda e: e.dma_start(out=bsb.ap, in_=dr[p + "bs"]), writes=[bsb])
        sc.dma("sp", sl, lambda e: e.dma_start(out=wsf.ap, in_=dr[p + "ws"]), writes=[wsf])
        sc.barrier()
        pw = T(self.pst(0), self.psbuf[0])
        fns = [lambda e, g=g: e.transpose(out=pw[:, g * 128:(g + 1) * 128], in_=wsf[:, g, :], identity=self.identf.ap)
               for g in range(4)]
        sc.pe_group(fns, reads=[wsf, self.identf], writes=[pw])
        for g in range(4):
            sc.op("dve", lambda e, g=g: e.tensor_tensor(out=WcT[:, g, :], in0=pw[:, g * 128:(g + 1) * 128], in1=self.trif.ap,
                                                       op=ALU.mult), reads=[pw, self.trif], writes=[WcT])
        self.dbg("Wout", self.Wout); self.dbg("WcT", WcT)
        self.ck(5)
        gl = [A.alloc([D]) for _ in range(4)]
        vln = [A.alloc([D], BF16) for _ in range(4)]
        uT = A.alloc([KC, ST], BF16)
        szT = A.alloc([KC, ST], BF16)
        ogT = A.alloc([KC, ST], BF16)
        mt = [A.alloc([ST]) for _ in range(2)]
        s1 = A.alloc([4])
        nm = A.alloc([4])
        s2 = A.alloc([4])
        lv = A.alloc([4])
        rv = A.alloc([4])
        psW = [T(self.pst(0, 2), self.psbuf[0]), T(self.pst(2, 2), self.psbuf[2])]
        psA = [T(self.pst(4 + i), self.psbuf[4 + i]) for i in range(3)]
        brow = self.biasrow[(1024, 2048)]
        ia = 0
        for j in range(self.nST):
            xnT = self.stageA(j, x_in)
            self.dbg("xnT", xnT); self.dbg("rsA", self.rsA[j % 2])
            self.ck(6)
            for tt in range(4):
                pv = psW[tt % 2]
                fns = []
                for nb in range(2):
                    for kc in range(KC):
                        fns.append(lambda e, nb=nb, kc=kc, pv=pv, tt=tt: e.matmul(
                            pv[:, nb * 512:(nb + 1) * 512], lhsT=xnT[:, kc, tt * 128:(tt + 1) * 128],
                            rhs=self.Win[:, kc, 1024 + nb * 512:1024 + (nb + 1) * 512], start=(kc == 0), stop=(kc == KC - 1)))
                sc.pe_group(fns, reads=[xnT, self.Win], writes=[pv])
                g_ = gl[tt]
                sc.op("dve", lambda e, pv=pv, g_=g_: e.tensor_tensor(out=g_.ap, in0=pv.ap, in1=brow.ap, op=ALU.add),
                      reads=[pv, brow], writes=[g_])
                sc.op("act", lambda e, g_=g_, tt=tt: e.activation(out=g_.ap, in_=g_.ap, func=AF.Gelu_apprx_tanh,
                                                                  accum_out=s1[:, tt:tt + 1]),
                      reads=[g_], writes=[g_, s1])
            self.ck(7)
            for c in range(KC):
                pa = psA[ia % 3]
                ia += 1
                fns = [lambda e, kc=kc, c=c, pa=pa: e.matmul(pa.ap, lhsT=self.Win[:, kc, c * 128:(c + 1) * 128], rhs=xnT[:, kc, :],
                                                             start=(kc == 0), stop=(kc == KC - 1)) for kc in range(KC)]
                sc.pe_group(fns, reads=[xnT, self.Win], writes=[pa])
                sc.op("act", lambda e, c=c, pa=pa: e.activation(out=uT[:, c, :], in_=pa.ap, func=AF.Gelu_apprx_tanh,
                                                                bias=self.biascol[:, c:c + 1]),
                      reads=[pa, self.biascol], writes=[uT])
            self.dbg("gl0", gl[0]); self.dbg("uT", uT); self.dbg("s1", s1)
            self.ck(8)
            sc.op("dve", lambda e: e.tensor_scalar(out=nm.ap, in0=s1.ap, scalar1=-1.0 / D, scalar2=None, op0=ALU.mult),
                  reads=[s1], writes=[nm])
            for tt in range(4):
                g_ = gl[tt]
                sc.op("act", lambda e, g_=g_, tt=tt: e.activation(out=self.junk.ap, in_=g_.ap, func=AF.Square,
                                                                  bias=nm[:, tt:tt + 1], accum_out=s2[:, tt:tt + 1]),
                      reads=[g_, nm], writes=[self.junk, s2])
            self.rstd_from_ss(s2, rv, 1.0 / D, lv)
            for tt in range(4):
                g_ = gl[tt]
                sc.op("dve", lambda e, g_=g_, tt=tt: e.tensor_scalar(out=g_.ap, in0=g_.ap, scalar1=nm[:, tt:tt + 1],
                                                                    scalar2=rv[:, tt:tt + 1], op0=ALU.add, op1=ALU.mult),
                      reads=[g_, nm, rv], writes=[g_])
                sc.op("pool", lambda e, g_=g_: e.tensor_tensor(out=g_.ap, in0=g_.ap, in1=lng.ap, op=ALU.mult),
                      reads=[g_, lng], writes=[g_])
                sc.op("pool", lambda e, g_=g_, tt=tt: e.tensor_tensor(out=vln[tt].ap, in0=g_.ap, in1=lnb.ap, op=ALU.add),
                      reads=[g_, lnb], writes=[vln[tt]])
            self.dbg("vln0", vln[0]); self.dbg("rv", rv)
            self.ck(9)
            for c in range(KC):
                pa = psA[ia % 3]
                ia += 1
                fns = [lambda e, kc=kc, c=c, pa=pa: e.matmul(pa.ap, lhsT=self.Win[:, kc, 2048 + c * 128:2048 + (c + 1) * 128],
                                                             rhs=xnT[:, kc, :], start=(kc == 0), stop=(kc == KC - 1))
                       for kc in range(KC)]
                sc.pe_group(fns, reads=[xnT, self.Win], writes=[pa])
                sc.op("act", lambda e, c=c, pa=pa: e.activation(out=szT[:, c, :], in_=pa.ap, func=AF.Silu,
                                                                bias=self.biascol[:, 16 + c:17 + c]),
                      reads=[pa, self.biascol], writes=[szT])
            self.ck(10)
            for c in range(KC):
                g = c // 2
                pa = psA[ia % 3]
                ia += 1
                fns = [lambda e, c=c, g=g, tt=tt, pa=pa: e.matmul(pa[:, tt * 128:(tt + 1) * 128], lhsT=vln[tt][:, c * 128:(c + 1) * 128],
                                                                  rhs=WcT[:, g, :], start=True, stop=True) for tt in range(4)]
                sc.pe_group(fns, reads=vln + [WcT], writes=[pa])
                m_ = mt[c % 2]
                for tt in range(4):
                    sc.op("dve", lambda e, pa=pa, m_=m_, g=g, tt=tt: e.tensor_tensor(
                        out=m_[:, tt * 128:(tt + 1) * 128], in0=pa[:, tt * 128:(tt + 1) * 128], in1=bsb[:, g, :], op=ALU.add),
                        reads=[pa, bsb], writes=[m_])
                sc.op("dve", lambda e, m_=m_, c=c: e.tensor_tensor(out=m_.ap, in0=m_.ap, in1=uT[:, c, :], op=ALU.mult),
                      reads=[m_, uT], writes=[m_])
                sc.op("pool", lambda e, m_=m_, c=c: e.tensor_tensor(out=ogT[:, c, :], in0=m_.ap, in1=szT[:, c, :], op=ALU.mult),
                      reads=[m_, szT], writes=[ogT])
            self.dbg("szT", szT); self.dbg("ogT", ogT)
            self.ck(11)
            self.stageO_full(j, ogT, x_in, x_out, psW)


def col8(v):
    return np.ascontiguousarray(v.reshape(-1, 128).T)


def rep(v, n=128):
    return np.ascontiguousarray(np.broadcast_to(v.reshape(1, -1), (n, v.size)))


def make_in_maps(inputs, layer_ids, S, n_cores):
    f = lambda a: np.ascontiguousarray(np.asarray(a, dtype=np.float32))
    x = f(inputs["x"])
    c = f(inputs["c"])
    shared = {"ident": np.eye(128, dtype=np.float32),
              "tri": np.triu(np.ones((128, 128), dtype=np.float32)),
              "uneg": np.triu(np.full((128, 128), -1.0 / 16, dtype=np.float32)),
              "negmask": np.tril(np.full((128, 128), -30000.0, dtype=np.float32), -1),
              "blockones": np.kron(np.eye(2, dtype=np.float32), np.ones((64, 64), dtype=np.float32)),
              "sel": np.concatenate([np.zeros((64, 64), np.float32), np.ones((1, 64), np.float32)], 0)}
    for L in layer_ids:
        kind, jj = L % 3, L // 3
        p = "L%d_" % L
        bm = f(inputs["b_mod"][L])
        shared[p + "wmod"] = f(inputs["w_mod"][L])
        shared[p + "bmodc"] = col8(bm)
        shared[p + "bmodg"] = rep(bm[2048:3072])
        shared[p + "gprec"] = col8(f(inputs["norm_pre_g"][L]))
        shared[p + "gpost"] = rep(f(inputs["norm_post_g"][L]))
        if kind == 0:
            shared[p + "win"] = f(inputs["gla_w_in"][jj])
            shared[p + "wout"] = f(inputs["gla_w_out"][jj])
            shared[p + "wa2"] = np.ascontiguousarray(np.concatenate([f(inputs["gla_w_a2"][jj]), f(inputs["gla_b_a"][jj])[None]], 0))
            shared[p + "ghc"] = col8(f(inputs["gla_g_head"][jj]).reshape(-1))
        if kind == 2:
            shared[p + "win"] = f(inputs["fox_w_in"][jj])
            shared[p + "wout"] = f(inputs["fox_w_out"][jj])
            shared[p + "gqk"] = np.ascontiguousarray(np.stack([np.tile(f(inputs["fox_g_q"][jj]), 2), np.tile(f(inputs["fox_g_k"][jj]), 2)], 1))
            shared[p + "bf"] = np.ascontiguousarray(f(inputs["fox_b_f"][jj])[:, None])
        if kind == 1:
            shared[p + "win"] = f(inputs["sgu_w_in"][jj])
            shared[p + "wout"] = f(inputs["sgu_w_out"][jj])
            shared[p + "lng"] = rep(f(inputs["sgu_ln_g"][jj]))
            shared[p + "lnb"] = rep(f(inputs["sgu_ln_b"][jj]))
            shared[p + "ws"] = np.ascontiguousarray(f(inputs["sgu_w_s"][jj]).transpose(1, 0, 2))
            shared[p + "bs"] = np.ascontiguousarray(np.broadcast_to(f(inputs["sgu_b_s"][jj])[None], (128, 4, 128)))
    maps = []
    for b in range(n_cores):
        m = dict(shared)
        m["x"] = np.ascontiguousarray(x[b, :S])
        m["ccol"] = col8(c[b])
        maps.append(m)
    return maps


_PROG_CACHE = {}


def run(inputs, layer_ids=(0, 1, 2, 3), S=8192, n_cores=N_CORES):
    key = (S, tuple(layer_ids))
    if key not in _PROG_CACHE:
        pr = Prog(S, layer_ids)
        pr.build()
        _PROG_CACHE[key] = pr
    pr = _PROG_CACHE[key]
    maps = make_in_maps(inputs, layer_ids, S, n_cores)
    res = run_bass_kernel_spmd(pr.nc, maps, core_ids=list(range(n_cores)))
    return np.stack([np.asarray(r["out"]) for r in res.results], axis=0)


def kernel(**inputs):
    return run(inputs).astype(np.float32)
```

```python
import numpy as np
from contextlib import ExitStack
import concourse.bass as bass
import concourse.mybir as mybir
from concourse.bass_utils import run_bass_kernel_spmd

F32 = mybir.dt.float32
BF16 = mybir.dt.bfloat16
AF = mybir.ActivationFunctionType
ALU = mybir.AluOpType

D = 1024
KC = 8
ST = 512
EPS = 1e-6
N_IN = {0: 3088, 1: 3072, 2: 4112}
N_CORES = 8


class Buf:
    __slots__ = ("w", "r", "excl")

    def __init__(self, excl=False):
        self.w = {}
        self.r = {}
        self.excl = excl


class T:
    __slots__ = ("ap", "buf")

    def __init__(self, ap, buf=None):
        self.ap = ap
        self.buf = buf if buf is not None else Buf()

    def __getitem__(self, k):
        return self.ap[k]


def _b(x):
    return x.buf if isinstance(x, T) else x


class _Rec:
    def __init__(self):
        self.call = None

    def __getattr__(self, name):
        def f(*a, **kw):
            assert self.call is None
            self.call = (name, a, kw)
            return None
        return f


def _capture(fn):
    r = _Rec()
    fn(r)
    name, a, kw = r.call
    line = fn.__code__.co_firstlineno

    def replay(eng):
        return getattr(eng, name)(*a, **kw)
    replay.line = line
    return replay


class Sched:
    ENG = ("pe", "act", "dve", "pool", "sp")

    def __init__(self, nc, es):
        self.nc = nc
        self.es = es
        self.q = {e: [] for e in self.ENG}
        self.cnt = {e: 0 for e in self.ENG}
        self.seen = {e: {} for e in self.ENG}
        self.sems = {}
        self.names = {}
        self.dcnt = {}
        for e in self.ENG:
            self.sems[e] = es.enter_context(nc.semaphore("s_" + e))

    def slot(self, name):
        k = "d_" + name
        self.sems[k] = self.es.enter_context(self.nc.semaphore(k))
        self.dcnt[k] = 0
        return k

    @staticmethod
    def _split(reads, writes):
        r2 = [b for b in reads if not _b(b).excl]
        w2 = list(writes) + [b for b in reads if _b(b).excl]
        return r2, w2

    def _waits(self, eng, reads, writes):
        reads, writes = self._split(reads, writes)
        need = {}
        for b in reads:
            for k, v in _b(b).w.items():
                if need.get(k, 0) < v:
                    need[k] = v
        for b in writes:
            bb = _b(b)
            for dct in (bb.w, bb.r):
                for k, v in dct.items():
                    if need.get(k, 0) < v:
                        need[k] = v
        waits = []
        seen = self.seen[eng]
        for k, v in need.items():
            if k in self.dcnt:
                v = self.dcnt[k]
            if k == eng and eng == "pe":
                continue
            if seen.get(k, 0) >= v:
                continue
            seen[k] = v
            waits.append((k, v))
        return waits

    def _record(self, ev, reads, writes):
        reads, writes = self._split(reads, writes)
        k, v = ev
        for b in reads:
            bb = _b(b)
            if bb.r.get(k, 0) < v:
                bb.r[k] = v
        for b in writes:
            bb = _b(b)
            bb.r = {}
            if bb.w.get(k, 0) < v:
                bb.w[k] = v

    def op(self, eng, fn, reads=(), writes=()):
        waits = self._waits(eng, reads, writes)
        self.cnt[eng] += 1
        ev = (eng, self.cnt[eng])
        self.q[eng].append((waits, _capture(fn), (eng, 1)))
        self._record(ev, reads, writes)

    def pe_group(self, fns, reads=(), writes=()):
        waits = self._waits("pe", reads, writes)
        self.cnt["pe"] += 1
        ev = ("pe", self.cnt["pe"])
        n = len(fns)
        for i, fn in enumerate(fns):
            self.q["pe"].append((waits if i == 0 else [], _capture(fn), ("pe", 1) if i == n - 1 else None))
        self._record(ev, reads, writes)

    def dma(self, q, slot, fn, reads=(), writes=()):
        waits = self._waits(q, reads, writes)
        self.dcnt[slot] += 16
        ev = (slot, self.dcnt[slot])
        self.q[q].append((waits, _capture(fn), (slot, 16)))
        self._record(ev, reads, writes)

    def barrier(self):
        for e in self.ENG:
            waits = []
            for k in list(self.ENG) + list(self.dcnt.keys()):
                if k == e:
                    continue
                v = self.cnt[k] if k in self.cnt else self.dcnt[k]
                if v > 0 and self.seen[e].get(k, 0) < v:
                    self.seen[e][k] = v
                    waits.append((k, v))
            if waits:
                self.q[e].append((waits, None, None))

    def emit(self):
        nc = self.nc

        def replay(name, eng):
            for waits, fn, inc in self.q[name]:
                for k, v in waits:
                    eng.wait_ge(self.sems[k], v)
                if fn is None:
                    continue
                ins = fn(eng)
                try:
                    self.names[ins.ins.name] = (name, fn.line)
                except Exception:
                    pass
                if inc is not None:
                    ins.then_inc(self.sems[inc[0]], inc[1])

        with nc.Block() as block:
            @block.sync
            def _(e):
                replay("sp", e)

            @block.tensor
            def _(e):
                replay("pe", e)

            @block.scalar
            def _(e):
                replay("act", e)

            @block.vector
            def _(e):
                replay("dve", e)

            @block.gpsimd
            def _(e):
                replay("pool", e)


class Arena:
    def __init__(self, ap, size):
        self.ap = ap
        self.size = size
        self.off = 0

    def mark(self):
        return self.off

    def reset(self, m):
        self.off = m

    def alloc(self, shape, dtype=F32):
        n = int(np.prod(shape))
        n32 = n if dtype == F32 else (n + 1) // 2
        assert self.off + n32 <= self.size, ("SBUF arena overflow", self.off, n32, self.size)
        v = self.ap[:, self.off:self.off + n32]
        self.off += n32
        if dtype == BF16:
            v = v.bitcast(BF16)
        if len(shape) == 2:
            v = v.rearrange("p (a b) -> p a b", a=shape[0])
        elif len(shape) == 3:
            v = v.rearrange("p (a b c) -> p a b c", a=shape[0], b=shape[1])
        return T(v)


class _Stop(Exception):
    pass


STOP_AT = [None]
STQ = "pool"
DEBUG = [False]


class Prog:
    def dbg(self, name, t):
        if not DEBUG[0]:
            return
        nm = "dbg_%s_%d" % (name, len(self.dbg_names))
        self.dbg_names.append(nm)
        shp = list(t.ap.shape)
        d = self.nc.dram_tensor(nm, shp, t.ap.dtype, kind="ExternalOutput").ap()
        sl = self.sc.slot(nm)
        self.sc.dma("sp", sl, lambda e: e.dma_start(out=d, in_=t.ap), reads=[t])

    def ck(self, n):
        if STOP_AT[0] is not None and n >= STOP_AT[0]:
            raise _Stop()

    def __init__(self, S, layer_ids):
        self.S = S
        self.layer_ids = list(layer_ids)
        self.nST = S // ST
        self.in_names = []
        self.dbg_names = []
        nc = self.nc = bass.Bass("TRN2", target_bir_lowering=False)
        self.dram = {}
        self._din("x", [S, D])
        self._din("ccol", [128, 8])
        self._din("ident", [128, 128])
        self._din("tri", [128, 128])
        self._din("uneg", [128, 128])
        self._din("negmask", [128, 128])
        self._din("blockones", [128, 128])
        self._din("sel", [65, 64])
        for L in self.layer_ids:
            kind = L % 3
            p = "L%d_" % L
            self._din(p + "wmod", [D, 3 * D])
            self._din(p + "bmodc", [128, 24])
            self._din(p + "bmodg", [128, D])
            self._din(p + "gprec", [128, 8])
            self._din(p + "gpost", [128, D])
            self._din(p + "win", [D, N_IN[kind]])
            self._din(p + "wout", [D, D])
            if kind == 0:
                self._din(p + "wa2", [17, 512])
                self._din(p + "ghc", [128, 8])
            if kind == 2:
                self._din(p + "gqk", [128, 2])
                self._din(p + "bf", [16, 1])
            if kind == 1:
                self._din(p + "lng", [128, D])
                self._din(p + "lnb", [128, D])
                self._din(p + "ws", [128, 4, 128])
                self._din(p + "bs", [128, 4, 128])
        self.out = nc.dram_tensor("out", [S, D], F32, kind="ExternalOutput").ap()
        self.xs = [nc.dram_tensor("xs%d" % i, [S, D], F32, kind="Internal").ap() for i in range(2)]
        if any(L % 3 == 2 for L in self.layer_ids):
            self.QA = nc.dram_tensor("fox_qa", [16, 70, S], BF16, kind="Internal").ap()
            self.KA = nc.dram_tensor("fox_ka", [16, 70, S], BF16, kind="Internal").ap()
            self.VS = nc.dram_tensor("fox_v", [S, D], BF16, kind="Internal").ap()
            self.SZ = nc.dram_tensor("fox_sz", [D, S], BF16, kind="Internal").ap()
            self.OG = nc.dram_tensor("fox_og", [D, S], BF16, kind="Internal").ap()

    def _din(self, name, shape, dtype=F32):
        self.in_names.append(name)
        self.dram[name] = self.nc.dram_tensor(name, shape, dtype, kind="ExternalInput").ap()

    def rstd_from_ss(self, ss, out, n_inv, tmp):
        sc = self.sc
        sc.op("act", lambda e: e.activation(out=tmp.ap, in_=ss.ap, func=AF.Ln, scale=n_inv, bias=EPS),
              reads=[ss], writes=[tmp])
        sc.op("act", lambda e: e.activation(out=out.ap, in_=tmp.ap, func=AF.Exp, scale=-0.5),
              reads=[tmp], writes=[out])

    def build(self):
        nc = self.nc
        with ExitStack() as es:
            arena_t = es.enter_context(nc.sbuf_tensor("arena", [128, 53000], F32))
            ps_t = es.enter_context(nc.psum_tensor("ps", [128, 8, 512], F32))
            self.sc = sc = Sched(nc, es)
            self.A = A = Arena(arena_t[:, :], 53000)
            self.ps = ps_t
            self.psbuf = [Buf(excl=True) for _ in range(8)]
            self.slots = {}
            for nm in ["c0", "stg0", "stg1", "x0", "x1", "x2", "x3", "xo0", "xo1", "xo2", "xo3", "xs0", "xs1", "xs2", "xs3", "sm0", "sm1", "sm2", "sm3",
                       "fq0", "fq1", "fv0", "fv1", "fz0", "ff0", "fk0", "fk1", "fvh0", "fvh1", "fqq0", "fqq1", "fqq2",
                       "fsz0", "fsz1", "fog0", "fog1", "fo0", "fo1"]:
                self.slots[nm] = sc.slot(nm)
            self.global_consts()
            x_in = self.dram["x"]
            try:
                for li, L in enumerate(self.layer_ids):
                    x_out = self.out if li == len(self.layer_ids) - 1 else self.xs[li % 2]
                    m = A.mark()
                    self.layer(L, x_in, x_out)
                    sc.barrier()
                    A.reset(m)
                    x_in = x_out
            except _Stop:
                pass
            sc.barrier()
            sc.emit()
        return nc

    def pst(self, bank, n=1):
        if n == 1:
            return self.ps[:, bank, :]
        return self.ps[:, bank:bank + n, :].rearrange("p a b -> p (a b)")

    def global_consts(self):
        sc, A, dr = self.sc, self.A, self.dram
        sl = self.slots["c0"]
        self.identf = A.alloc([128])
        self.identb = A.alloc([128], BF16)
        self.trif = A.alloc([128])
        self.unegf = A.alloc([128])
        self.onesb = A.alloc([128], BF16)
        self.ones = A.alloc([128])
        self.cond = A.alloc([8])
        self.cond_rep = A.alloc([8, 128])
        cc = A.alloc([8])
        sc.dma("sp", sl, lambda e: e.dma_start(out=self.identf.ap, in_=dr["ident"]), writes=[self.identf])
        sc.dma("sp", sl, lambda e: e.dma_start(out=self.trif.ap, in_=dr["tri"]), writes=[self.trif])
        sc.dma("sp", sl, lambda e: e.dma_start(out=self.unegf.ap, in_=dr["uneg"]), writes=[self.unegf])
        sc.dma("sp", sl, lambda e: e.dma_start(out=cc.ap, in_=dr["ccol"]), writes=[cc])
        sc.barrier()
        sc.op("dve", lambda e: e.tensor_copy(out=self.identb.ap, in_=self.identf.ap), reads=[self.identf], writes=[self.identb])
        sc.op("dve", lambda e: e.memset(self.ones.ap, 1.0), writes=[self.ones])
        sc.op("dve", lambda e: e.memset(self.onesb.ap, 1.0), writes=[self.onesb])
        sc.op("act", lambda e: e.activation(out=self.cond.ap, in_=cc.ap, func=AF.Silu), reads=[cc], writes=[self.cond])
        for kc in range(KC):
            sc.op("dve", lambda e, kc=kc: e.tensor_scalar(out=self.cond_rep[:, kc, :], in0=self.ones.ap,
                                                         scalar1=self.cond[:, kc:kc + 1], scalar2=None, op0=ALU.mult),
                  reads=[self.ones, self.cond], writes=[self.cond_rep])

    def prep(self, L, tokmajor_ranges, nocol_ranges=None, wout_rowscale=None):
        sc, A, dr = self.sc, self.A, self.dram
        kind = L % 3
        p = "L%d_" % L
        nin = N_IN[kind]
        BW = 256
        if nocol_ranges is None:
            nocol_ranges = tokmajor_ranges
        self.Win = A.alloc([KC, nin], BF16)
        self.Wout = A.alloc([KC, D], BF16)
        self.Gbc = A.alloc([D])
        nch = (nin + 127) // 128
        self.biascol = A.alloc([nch])
        self.biasrow = {r: A.alloc([r[1] - r[0]]) for r in tokmajor_ranges}
        tmp_mark = A.mark()
        stg = [A.alloc([KC, BW]) for _ in range(2)]
        stg_slot = [self.slots["stg0"], self.slots["stg1"]]
        modc = A.alloc([16])
        acol = A.alloc([8])
        shift_rep = A.alloc([KC, 128])
        small = A.alloc([24 + 8])
        bmodc, gprec = T(small[:, 0:24], small.buf), T(small[:, 24:32], small.buf)
        gtmp = A.alloc([D])
        gpost = A.alloc([D])
        sl = self.slots["c0"]
        sc.dma("sp", sl, lambda e: e.dma_start(out=bmodc.ap, in_=dr[p + "bmodc"]), writes=[small])
        sc.dma("sp", sl, lambda e: e.dma_start(out=gprec.ap, in_=dr[p + "gprec"]), writes=[small])
        sc.dma("sp", sl, lambda e: e.dma_start(out=gtmp.ap, in_=dr[p + "bmodg"]), writes=[gtmp])
        sc.dma("sp", sl, lambda e: e.dma_start(out=gpost.ap, in_=dr[p + "gpost"]), writes=[gpost])
        sc.barrier()
        blk = [0]

        def load_block(src, c0, w):
            i = blk[0] % 2
            blk[0] += 1
            s = stg[i]
            sc.dma("sp", stg_slot[i],
                   lambda e: e.dma_start(out=s[:, :, 0:w], in_=src[:, c0:c0 + w].rearrange("(kc p) n -> p kc n", p=128)),
                   writes=[s])
            return s

        PB_MOD, PB_G, PB_BC, PB_BR = 0, 1, 3, 4
        psmod = T(self.pst(PB_MOD), self.psbuf[PB_MOD])
        psG = [T(self.pst(PB_G + i), self.psbuf[PB_G + i]) for i in range(2)]
        wmod = dr[p + "wmod"]
        for j in range(12):
            s = load_block(wmod, j * BW, BW)
            if j < 8:
                fns = []
                for h in range(2):
                    ch = j * 2 + h
                    for kc in range(KC):
                        fns.append(lambda e, ch=ch, h=h, kc=kc, s=s: e.matmul(
                            psmod[:, ch:ch + 1], lhsT=s[:, kc, h * 128:(h + 1) * 128], rhs=self.cond[:, kc:kc + 1],
                            start=(kc == 0), stop=(kc == KC - 1)))
                sc.pe_group(fns, reads=[s, self.cond], writes=[psmod])
            else:
                g = j - 8
                fns = [lambda e, kc=kc, s=s, g=g: e.matmul(
                    psG[g // 2][:, (g % 2) * BW:(g % 2 + 1) * BW], lhsT=self.cond_rep[:, kc, :], rhs=s[:, kc, :],
                    start=(kc == 0), stop=(kc == KC - 1)) for kc in range(KC)]
                sc.pe_group(fns, reads=[s, self.cond_rep], writes=[psG[g // 2]])
        self.ck(1)
        sc.op("dve", lambda e: e.tensor_tensor(out=modc.ap, in0=psmod[:, 0:16], in1=bmodc[:, 0:16], op=ALU.add),
              reads=[psmod, small], writes=[modc])
        sc.op("dve", lambda e: e.scalar_tensor_tensor(out=acol.ap, in0=modc[:, 8:16], scalar=1.0, in1=gprec.ap,
                                                      op0=ALU.add, op1=ALU.mult),
              reads=[modc, small], writes=[acol])
        for kc in range(KC):
            sc.op("dve", lambda e, kc=kc: e.tensor_scalar(out=shift_rep[:, kc, :], in0=self.ones.ap,
                                                         scalar1=modc[:, kc:kc + 1], scalar2=None, op0=ALU.mult),
                  reads=[self.ones, modc], writes=[shift_rep])
        for i in range(2):
            sc.op("dve", lambda e, i=i: e.tensor_tensor(out=gtmp[:, i * 512:(i + 1) * 512], in0=psG[i].ap,
                                                       in1=gtmp[:, i * 512:(i + 1) * 512], op=ALU.add),
                  reads=[psG[i], gtmp], writes=[gtmp])
        sc.op("dve", lambda e: e.tensor_tensor(out=self.Gbc.ap, in0=gtmp.ap, in1=gpost.ap, op=ALU.mult),
              reads=[gtmp, gpost], writes=[self.Gbc])
        self.dbg("modc", modc); self.dbg("acol", acol); self.dbg("Gbc", self.Gbc)
        self.ck(2)
        win = dr[p + "win"]
        psbc = T(self.pst(PB_BC), self.psbuf[PB_BC])
        psbr = [T(self.pst(PB_BR + i), self.psbuf[PB_BR + i]) for i in range(2)]
        written = []
        c0 = 0
        tog = 0
        while c0 < nin:
            w = min(BW, nin - c0)
            s = load_block(win, c0, w)
            rng = None
            for r in tokmajor_ranges:
                if r[0] <= c0 < r[1]:
                    rng = r
            if rng is not None:
                pb = psbr[tog % 2]
                tog += 1
                fns = [lambda e, kc=kc, s=s, pb=pb, w=w: e.matmul(pb[:, 0:w], lhsT=shift_rep[:, kc, :], rhs=s[:, kc, 0:w],
                                                                   start=(kc == 0), stop=(kc == KC - 1)) for kc in range(KC)]
                sc.pe_group(fns, reads=[s, shift_rep], writes=[pb])
                br = self.biasrow[rng]
                o = c0 - rng[0]
                sc.op("act", lambda e, br=br, o=o, w=w, pb=pb: e.copy(out=br[:, o:o + w], in_=pb[:, 0:w]),
                      reads=[pb], writes=[br])
            if not any(r[0] <= c0 < r[1] for r in nocol_ranges):
                fns = []
                for h in range((w + 127) // 128):
                    ch = c0 // 128 + h
                    hw = min(128, w - h * 128)
                    written.append((ch, hw))
                    for kc in range(KC):
                        fns.append(lambda e, ch=ch, h=h, hw=hw, kc=kc, s=s: e.matmul(
                            psbc[0:hw, ch:ch + 1], lhsT=s[:, kc, h * 128:h * 128 + hw], rhs=modc[:, kc:kc + 1],
                            start=(kc == 0), stop=(kc == KC - 1)))
                sc.pe_group(fns, reads=[s, modc], writes=[psbc])
            for kc in range(KC):
                eng = "dve" if kc % 2 == 0 else "pool"
                if eng == "dve":
                    sc.op(eng, lambda e, kc=kc, s=s, c0=c0, w=w: e.tensor_scalar(
                        out=self.Win[:, kc, c0:c0 + w], in0=s[:, kc, 0:w], scalar1=acol[:, kc:kc + 1], scalar2=None,
                        op0=ALU.mult), reads=[s, acol], writes=[self.Win])
                else:
                    sc.op(eng, lambda e, kc=kc, s=s, c0=c0, w=w: e.tensor_scalar(
                        out=self.Win[:, kc, c0:c0 + w], in0=s[:, kc, 0:w], scalar1=acol[:, kc:kc + 1], scalar2=1.0,
                        op0=ALU.mult, op1=ALU.mult), reads=[s, acol], writes=[self.Win])
            c0 += w
        for ch, hw in written:
            sc.op("dve", lambda e, ch=ch, hw=hw: e.tensor_copy(out=self.biascol[0:hw, ch:ch + 1], in_=psbc[0:hw, ch:ch + 1]),
                  reads=[psbc], writes=[self.biascol])
        self.dbg("Win", self.Win); self.dbg("biascol", T(self.biascol[:, 0:8], self.biascol.buf))
        for r_, t_ in self.biasrow.items():
            self.dbg("biasrow", t_)
        self.ck(3)
        wout = dr[p + "wout"]
        for j in range(D // BW):
            s = load_block(wout, j * BW, BW)
            if wout_rowscale is None:
                sc.op("dve", lambda e, s=s, j=j: e.tensor_copy(out=self.Wout[:, 0:4, j * BW:(j + 1) * BW], in_=s[:, 0:4, :]),
                      reads=[s], writes=[self.Wout])
                sc.op("act", lambda e, s=s, j=j: e.copy(out=self.Wout[:, 4:8, j * BW:(j + 1) * BW], in_=s[:, 4:8, :]),
                      reads=[s], writes=[self.Wout])
            else:
                for kc in range(KC):
                    sc.op("dve", lambda e, s=s, j=j, kc=kc: e.tensor_scalar(
                        out=self.Wout[:, kc, j * BW:(j + 1) * BW], in0=s[:, kc, :], scalar1=wout_rowscale[:, kc:kc + 1],
                        scalar2=None, op0=ALU.mult), reads=[s, wout_rowscale], writes=[self.Wout])
        sc.barrier()
        A.reset(tmp_mark)

    def alloc_stageA(self, nxt=4, nxnT=2):
        A = self.A
        self.xt = [A.alloc([D]) for _ in range(nxt)]
        self.xn = [A.alloc([D], BF16) for _ in range(2)]
        self.xnT = [A.alloc([KC, ST], BF16) for _ in range(nxnT)]
        self.junk = A.alloc([D], BF16)
        self.ssA = [A.alloc([4]) for _ in range(2)]
        self.lnA = [A.alloc([4]) for _ in range(2)]
        self.rsA = [A.alloc([4]) for _ in range(2)]
        self.psT = T(self.ps[:, 7, :].bitcast(BF16).rearrange("p (a b) -> p a b", a=8), self.psbuf[7])

    def stageA(self, j, x_in):
        sc = self.sc
        ss, ln, rs, xnT = self.ssA[j % 2], self.lnA[j % 2], self.rsA[j % 2], self.xnT[j % len(self.xnT)]
        nxt = len(self.xt)
        for tt in range(4):
            tok = j * ST + tt * 128
            xt = self.xt[tt % nxt]
            xn = self.xn[tt % 2]
            sc.dma("sp", self.slots["x%d" % (tt % nxt)], lambda e, xt=xt, tok=tok: e.dma_start(out=xt.ap, in_=x_in[tok:tok + 128, :]),
                   writes=[xt])
            sc.op("act", lambda e, xt=xt, tt=tt, ss=ss: e.activation(out=self.junk.ap, in_=xt.ap, func=AF.Square,
                                                                     accum_out=ss[:, tt:tt + 1]),
                  reads=[xt], writes=[self.junk, ss])
            sc.op("act", lambda e, tt=tt: e.activation(out=ln[:, tt:tt + 1], in_=ss[:, tt:tt + 1], func=AF.Ln, scale=1.0 / D, bias=EPS),
                  reads=[ss], writes=[ln])
            sc.op("act", lambda e, tt=tt: e.activation(out=rs[:, tt:tt + 1], in_=ln[:, tt:tt + 1], func=AF.Exp, scale=-0.5),
                  reads=[ln], writes=[rs])
            sc.op("act", lambda e, xt=xt, xn=xn, tt=tt, rs=rs: e.activation(out=xn.ap, in_=xt.ap, func=AF.Copy,
                                                                            scale=rs[:, tt:tt + 1]),
                  reads=[xt, rs], writes=[xn])
            fns = [lambda e, c=c, xn=xn: e.transpose(out=self.psT[:, c, :], in_=xn[:, c * 128:(c + 1) * 128],
                                                     identity=self.identb.ap) for c in range(KC)]
            sc.pe_group(fns, reads=[xn, self.identb], writes=[self.psT])
            sc.op("dve", lambda e, tt=tt, xnT=xnT: e.tensor_copy(out=xnT[:, :, tt * 128:(tt + 1) * 128], in_=self.psT.ap),
                  reads=[self.psT], writes=[xnT])
        return xnT

    def alloc_stageO(self, nxo=4, nt1=2):
        A = self.A
        self.xo = [A.alloc([D]) for _ in range(nxo)]
        self.t1 = [A.alloc([D]) for _ in range(nt1)]
        self.ssO = [A.alloc([4]) for _ in range(2)]
        self.lnO = [A.alloc([4]) for _ in range(2)]
        self.rsO = [A.alloc([4]) for _ in range(2)]

    def stageO(self, j, ogT, x_in, x_out, psY):
        sc = self.sc
        ss, ln, rs = self.ssO[j % 2], self.lnO[j % 2], self.rsO[j % 2]
        for tt in range(4):
            tok = j * ST + tt * 128
            py = psY[tt % 2]
            xo = self.xo[tt % 2]
            t1 = self.t1[tt % 2]
            sc.dma("sp", self.slots["xo%d" % (tt % 2)], lambda e, xo=xo, tok=tok: e.dma_start(out=xo.ap, in_=x_in[tok:tok + 128, :]),
                   writes=[xo])
            fns = []
            for nb in range(2):
                for c in range(KC):
                    fns.append(lambda e, nb=nb, c=c, py=py, tt=tt: e.matmul(
                        py[:, nb * 512:(nb + 1) * 512], lhsT=ogT[:, c, tt * 128:(tt + 1) * 128],
                        rhs=self.Wout[:, c, nb * 512:(nb + 1) * 512], start=(c == 0), stop=(c == KC - 1)))
            sc.pe_group(fns, reads=[ogT, self.Wout], writes=[py])
            sc.op("act", lambda e, py=py, tt=tt, ss=ss: e.activation(out=self.junk.ap, in_=py.ap, func=AF.Square,
                                                                     accum_out=ss[:, tt:tt + 1]),
                  reads=[py], writes=[self.junk, ss])
            sc.op("dve", lambda e, py=py, t1=t1: e.tensor_tensor(out=t1.ap, in0=py.ap, in1=self.Gbc.ap, op=ALU.mult),
                  reads=[py, self.Gbc], writes=[t1])
        self.rstd_from_ss(ss, rs, 1.0 / D, ln)
        return ss, rs

    def stageO_full(self, j, ogT, x_in, x_out, psY):
        for tt in range(4):
            self.stageO_tile(j, tt, ogT, x_in, x_out, psY)

    def stageO_tile(self, j, tt, ogT, x_in, x_out, psY, og_dep=None):
        sc = self.sc
        og_dep = ogT if og_dep is None else og_dep
        if True:
            tok = j * ST + tt * 128
            k = (j * 4 + tt) % 2
            py = psY[k]
            xi = tt % len(self.xo)
            xo = self.xo[xi]
            t1 = self.t1[k % len(self.t1)]
            ss, ln, rs = self.ssO[k], self.lnO[k], self.rsO[k]
            sc.dma("sp", self.slots["xo%d" % xi], lambda e, xo=xo, tok=tok: e.dma_start(out=xo.ap, in_=x_in[tok:tok + 128, :]),
                   writes=[xo])
            fns = []
            for nb in range(2):
                for c in range(KC):
                    fns.append(lambda e, nb=nb, c=c, py=py, tt=tt: e.matmul(
                        py[:, nb * 512:(nb + 1) * 512], lhsT=ogT[:, c, tt * 128:(tt + 1) * 128],
                        rhs=self.Wout[:, c, nb * 512:(nb + 1) * 512], start=(c == 0), stop=(c == KC - 1)))
            self.ck(12)
            sc.pe_group(fns, reads=[og_dep, self.Wout], writes=[py])
            self.ck(13)
            for nb in range(2):
                sc.op("act", lambda e, py=py, ss=ss, nb=nb: e.activation(out=self.junk[:, nb * 512:(nb + 1) * 512], in_=py[:, nb * 512:(nb + 1) * 512],
                                                                         func=AF.Square, accum_out=ss[:, 1 + nb:2 + nb]),
                      reads=[py], writes=[self.junk, ss])
                sc.op("dve", lambda e, py=py, t1=t1, nb=nb: e.tensor_tensor(out=t1[:, nb * 512:(nb + 1) * 512], in0=py[:, nb * 512:(nb + 1) * 512],
                                                                            in1=self.Gbc[:, nb * 512:(nb + 1) * 512], op=ALU.mult),
                      reads=[py, self.Gbc], writes=[t1])
            self.ck(14)
            sc.op("dve", lambda e, ss=ss: e.tensor_tensor(out=ss[:, 0:1], in0=ss[:, 1:2], in1=ss[:, 2:3], op=ALU.add),
                  reads=[ss], writes=[ss])
            sc.op("act", lambda e, ss=ss, ln=ln: e.activation(out=ln[:, 0:1], in_=ss[:, 0:1], func=AF.Ln, scale=1.0 / D, bias=EPS),
                  reads=[ss], writes=[ln])
            sc.op("act", lambda e, rs=rs, ln=ln: e.activation(out=rs[:, 0:1], in_=ln[:, 0:1], func=AF.Exp, scale=-0.5),
                  reads=[ln], writes=[rs])
            self.ck(15)
            sc.op("dve", lambda e, t1=t1, xo=xo, rs=rs: e.scalar_tensor_tensor(out=xo.ap, in0=t1.ap, scalar=rs[:, 0:1], in1=xo.ap,
                                                                               op0=ALU.mult, op1=ALU.add),
                  reads=[t1, xo, rs], writes=[xo])
            self.ck(16)
            sc.dma(STQ, self.slots["xs%d" % xi], lambda e, xo=xo, tok=tok: e.dma_start(out=x_out[tok:tok + 128, :], in_=xo.ap),
                   reads=[xo])

    def layer(self, L, x_in, x_out):
        kind = L % 3
        if kind == 1:
            self.layer_sgu(L, x_in, x_out)
        elif kind == 0:
            self.layer_gla(L, x_in, x_out)
        else:
            self.layer_fox(L, x_in, x_out)


    def layer_gla(self, L, x_in, x_out):
        sc, A, dr = self.sc, self.A, self.dram
        p = "L%d_" % L
        ghc = A.alloc([8])
        sl = self.slots["c0"]
        sc.dma("sp", sl, lambda e: e.dma_start(out=ghc.ap, in_=dr[p + "ghc"]), writes=[ghc])
        sc.barrier()
        self.prep(L, [(512, 2048)], nocol_ranges=[(1024, 2048)], wout_rowscale=ghc)
        self.alloc_stageA(nxt=2, nxnT=2)
        self.alloc_stageO(nxo=2, nt1=1)
        wa2 = A.alloc([512])
        sc.dma("sp", sl, lambda e: e.dma_start(out=wa2[0:17, :], in_=dr[p + "wa2"]), writes=[wa2])
        sc.barrier()
        alT = A.alloc([ST])
        sc.op("dve", lambda e: e.memset(alT[0:17, :], 1.0), writes=[alT])
        f512 = A.alloc([512])
        spt = [A.alloc([512]) for _ in range(2)]
        enb = [A.alloc([512]) for _ in range(1)]
        ebT = A.alloc([4, ST])
        enbT = A.alloc([4, ST])
        elast = [A.alloc([4]) for _ in range(4)]
        qT = A.alloc([4, ST], BF16)
        kT = A.alloc([4, ST], BF16)
        ktok = [A.alloc([512], BF16) for _ in range(4)]
        vtok = [A.alloc([D], BF16) for _ in range(4)]
        szT = A.alloc([KC, ST], BF16)
        ogT = A.alloc([KC, ST], BF16)
        ogp = [T(ogT.ap, Buf()) for _ in range(4)]
        ATs = [A.alloc([4, 128], BF16) for _ in range(2)]
        osq = A.alloc([8, 128], BF16)
        rstd = A.alloc([4, 128])
        otmp = A.alloc([8, 128], BF16)
        Sst = A.alloc([4, 256])
        Sbf = A.alloc([4, 256], BF16)
        sc.op("dve", lambda e: e.memset(Sst.ap, 0.0), writes=[Sst])
        sc.op("dve", lambda e: e.memset(Sbf.ap, 0.0), writes=[Sbf])
        psW = [T(self.pst(0, 2), self.psbuf[0]), T(self.pst(2, 2), self.psbuf[2])]
        psA = [T(self.pst(4 + i), self.psbuf[4 + i]) for i in range(3)]
        brow = self.biasrow[(512, 2048)]
        LNS = -0.5 * float(np.log(128.0))
        tri4 = self.trif.ap.unsqueeze(1).to_broadcast([128, 4, 128])
        ia = [0]
        iw = [0]

        def nextA():
            t = psA[ia[0] % 3]
            ia[0] += 1
            return t

        def nextW():
            t = psW[iw[0] % 2]
            iw[0] += 1
            return t

        xnT_next = self.stageA(0, x_in)
        for j in range(self.nST):
            xnT = xnT_next
            for c in range(KC):
                pa = nextA()
                fns = [lambda e, kc=kc: e.matmul(pa.ap, lhsT=self.Win[:, kc, 2048 + c * 128:2048 + (c + 1) * 128],
                                                 rhs=xnT[:, kc, :], start=(kc == 0), stop=(kc == KC - 1)) for kc in range(KC)]
                sc.pe_group(fns, reads=[xnT, self.Win], writes=[pa])
                sc.op("act", lambda e: e.activation(out=szT[:, c, :], in_=pa.ap, func=AF.Silu, bias=self.biascol[:, 16 + c:17 + c]),
                      reads=[pa, self.biascol], writes=[szT])
            pa = nextA()
            fns = [lambda e, kc=kc: e.matmul(pa[0:16, :], lhsT=self.Win[:, kc, 3072:3088], rhs=xnT[:, kc, :],
                                             start=(kc == 0), stop=(kc == KC - 1)) for kc in range(KC)]
            sc.pe_group(fns, reads=[xnT, self.Win], writes=[pa])
            sc.op("act", lambda e: e.activation(out=alT[0:16, :], in_=pa[0:16, :], func=AF.Identity, bias=self.biascol[0:16, 24:25]),
                  reads=[pa, self.biascol], writes=[alT])
            def emit_xa(tt):
                ts_ = slice(tt * 128, (tt + 1) * 128)
                pa = nextA()
                sc.pe_group([lambda e: e.matmul(pa.ap, lhsT=alT[0:17, ts_], rhs=wa2[0:17, :], start=True, stop=True)],
                            reads=[alT, wa2], writes=[pa])
                sp_ = spt[tt % 2]
                sc.op("act", lambda e: e.activation(out=f512.ap, in_=pa.ap, func=AF.Exp, scale=-1.0), reads=[pa], writes=[f512])
                sc.op("act", lambda e: e.activation(out=sp_.ap, in_=f512.ap, func=AF.Ln, bias=1.0), reads=[f512], writes=[sp_])

            emit_xa(0)
            emit_xa(1)
            for tt in range(4):
                ts_ = slice(tt * 128, (tt + 1) * 128)
                sp_ = spt[tt % 2]
                pb = nextA()
                sc.pe_group([lambda e: e.matmul(pb.ap, lhsT=self.unegf.ap, rhs=sp_.ap, start=True, stop=True)],
                            reads=[self.unegf, sp_], writes=[pb])
                en_ = enb[0]
                sc.op("act", lambda e: e.activation(out=en_.ap, in_=pb.ap, func=AF.Exp, scale=-1.0), reads=[pb], writes=[en_])
                pc = nextA()
                fns = [lambda e, h=h: e.matmul(pc[:, h * 128:(h + 1) * 128], lhsT=sp_[:, h * 128:(h + 1) * 128],
                                               rhs=self.unegf.ap, start=True, stop=True) for h in range(4)]
                sc.pe_group(fns, reads=[self.unegf, sp_], writes=[pc])
                pc3 = pc.ap.rearrange("p (h t) -> p h t", h=4)
                sc.op("act", lambda e: e.activation(out=ebT[:, :, ts_], in_=pc3, func=AF.Exp, bias=LNS), reads=[pc], writes=[ebT])
                sc.op("act", lambda e: e.activation(out=enbT[:, :, ts_], in_=pc3, func=AF.Exp, scale=-1.0), reads=[pc], writes=[enbT])
                el = elast[tt]
                sc.op("act", lambda e: e.activation(out=el.ap, in_=pc3[:, :, 127], func=AF.Exp), reads=[pc], writes=[el])
                if tt + 2 < 4:
                    emit_xa(tt + 2)
                pk = nextA()
                fns = [lambda e, kc=kc: e.matmul(pk.ap, lhsT=xnT[:, kc, ts_], rhs=self.Win[:, kc, 512:1024],
                                                 start=(kc == 0), stop=(kc == KC - 1)) for kc in range(KC)]
                sc.pe_group(fns, reads=[xnT, self.Win], writes=[pk])
                sc.op("dve", lambda e: e.tensor_tensor(out=f512.ap, in0=pk.ap, in1=brow[:, 0:512], op=ALU.add),
                      reads=[pk, brow], writes=[f512])
                sc.op("pool", lambda e: e.tensor_tensor(out=ktok[tt].ap, in0=f512.ap, in1=en_.ap, op=ALU.mult),
                      reads=[f512, en_], writes=[ktok[tt]])
                pv = nextW()
                fns = []
                for nb in range(2):
                    for kc in range(KC):
                        fns.append(lambda e, nb=nb, kc=kc: e.matmul(
                            pv[:, nb * 512:(nb + 1) * 512], lhsT=xnT[:, kc, ts_],
                            rhs=self.Win[:, kc, 1024 + nb * 512:1024 + (nb + 1) * 512], start=(kc == 0), stop=(kc == KC - 1)))
                sc.pe_group(fns, reads=[xnT, self.Win], writes=[pv])
                sc.op("dve", lambda e: e.tensor_tensor(out=vtok[tt].ap, in0=pv.ap, in1=brow[:, 512:1536], op=ALU.add),
                      reads=[pv, brow], writes=[vtok[tt]])
            for h in range(4):
                pa = nextA()
                fns = [lambda e, kc=kc: e.matmul(pa.ap, lhsT=self.Win[:, kc, h * 128:(h + 1) * 128], rhs=xnT[:, kc, :],
                                                 start=(kc == 0), stop=(kc == KC - 1)) for kc in range(KC)]
                sc.pe_group(fns, reads=[xnT, self.Win], writes=[pa])
                sc.op("dve", lambda e: e.scalar_tensor_tensor(out=qT[:, h, :], in0=pa.ap, scalar=self.biascol[:, h:h + 1],
                                                              in1=ebT[:, h, :], op0=ALU.add, op1=ALU.mult),
                      reads=[pa, self.biascol, ebT], writes=[qT])
                pa2 = nextA()
                fns = [lambda e, kc=kc: e.matmul(pa2.ap, lhsT=self.Win[:, kc, 512 + h * 128:512 + (h + 1) * 128],
                                                 rhs=xnT[:, kc, :], start=(kc == 0), stop=(kc == KC - 1)) for kc in range(KC)]
                sc.pe_group(fns, reads=[xnT, self.Win], writes=[pa2])
                sc.op("dve", lambda e: e.scalar_tensor_tensor(out=kT[:, h, :], in0=pa2.ap, scalar=self.biascol[:, 4 + h:5 + h],
                                                              in1=enbT[:, h, :], op0=ALU.add, op1=ALU.mult),
                      reads=[pa2, self.biascol, enbT], writes=[kT])
            if j + 1 < self.nST:
                xnT_next = self.stageA(j + 1, x_in)

            def emit_AT(tt):
                ts_ = slice(tt * 128, (tt + 1) * 128)
                pa = nextA()
                fns = [lambda e, h=h: e.matmul(pa[:, h * 128:(h + 1) * 128], lhsT=kT[:, h, ts_], rhs=qT[:, h, ts_],
                                               start=True, stop=True) for h in range(4)]
                sc.pe_group(fns, reads=[kT, qT], writes=[pa])
                at = ATs[tt % 2]
                sc.op("dve", lambda e: e.tensor_tensor(out=at.ap, in0=pa.ap.rearrange("p (h t) -> p h t", h=4), in1=tri4, op=ALU.mult),
                      reads=[pa, self.trif], writes=[at])

            emit_AT(0)
            for tt in range(4):
                ts_ = slice(tt * 128, (tt + 1) * 128)
                if tt + 1 < 4:
                    emit_AT(tt + 1)
                at = ATs[tt % 2]
                po = psW[0]
                fns = []
                for h in range(4):
                    for half in range(2):
                        c = 2 * h + half
                        fns.append(lambda e, h=h, c=c: e.matmul(po[:, c * 128:(c + 1) * 128], lhsT=vtok[tt][:, c * 128:(c + 1) * 128],
                                                                rhs=at[:, h, :], start=True, stop=False))
                        fns.append(lambda e, h=h, c=c, half=half: e.matmul(po[:, c * 128:(c + 1) * 128],
                                                                           lhsT=Sbf[:, h, half * 128:(half + 1) * 128],
                                                                           rhs=qT[:, h, ts_], start=False, stop=True))
                sc.pe_group(fns, reads=[vtok[tt], at, Sbf, qT], writes=[po])
                el = elast[tt]
                for hp in range(2):
                    pP = nextA()
                    fns = [lambda e, hh=hh: e.matmul(pP[:, hh * 256:(hh + 1) * 256],
                                                     lhsT=ktok[tt][:, (2 * hp + hh) * 128:(2 * hp + hh + 1) * 128],
                                                     rhs=vtok[tt][:, (2 * hp + hh) * 256:(2 * hp + hh + 1) * 256], start=True, stop=True)
                           for hh in range(2)]
                    sc.pe_group(fns, reads=[ktok[tt], vtok[tt]], writes=[pP])
                    Sh = Sst[:, 2 * hp:2 * hp + 2, :]
                    sc.op("dve", lambda e: e.tensor_tensor(out=Sh, in0=pP.ap.rearrange("p (h v) -> p h v", h=2), in1=Sh, op=ALU.add),
                          reads=[pP, Sst], writes=[Sst])
                sc.op("pool", lambda e: e.tensor_tensor(out=Sst.ap, in0=Sst.ap, in1=el.ap.unsqueeze(2).to_broadcast([128, 4, 256]),
                                                        op=ALU.mult), reads=[Sst, el], writes=[Sst])
                sc.op("act", lambda e: e.copy(out=Sbf.ap, in_=Sst.ap), reads=[Sst], writes=[Sbf])
                for nb in range(2):
                    sc.op("act", lambda e, nb=nb: e.activation(
                        out=osq[:, nb * 4:(nb + 1) * 4, :], in_=po[:, nb * 512:(nb + 1) * 512].rearrange("p (c t) -> p c t", c=4),
                        func=AF.Square), reads=[po], writes=[osq])
                ps_ = nextA()
                fns = []
                for h in range(4):
                    for half in range(2):
                        fns.append(lambda e, h=h, half=half: e.matmul(ps_[:, h * 128:(h + 1) * 128], lhsT=self.onesb.ap,
                                                                      rhs=osq[:, 2 * h + half, :], start=(half == 0), stop=(half == 1)))
                sc.pe_group(fns, reads=[self.onesb, osq], writes=[ps_])
                sc.op("act", lambda e: e.activation(out=rstd.ap, in_=ps_.ap.rearrange("p (h t) -> p h t", h=4),
                                                    func=AF.Ln, scale=1.0 / 256, bias=EPS), reads=[ps_], writes=[rstd])
                sc.op("act", lambda e: e.activation(out=rstd.ap, in_=rstd.ap, func=AF.Exp, scale=-0.5), reads=[rstd], writes=[rstd])
                for nb in range(2):
                    sc.op("dve", lambda e, nb=nb: e.tensor_tensor(
                        out=otmp[:, nb * 4:(nb + 1) * 4, :].rearrange("p (h f) t -> p h f t", h=2),
                        in0=po[:, nb * 512:(nb + 1) * 512].rearrange("p (h f t) -> p h f t", h=2, f=2),
                        in1=rstd[:, 2 * nb:2 * nb + 2, :].unsqueeze(2).to_broadcast([128, 2, 2, 128]), op=ALU.mult),
                        reads=[po, rstd], writes=[otmp])
                sc.op("pool", lambda e: e.tensor_tensor(out=ogT[:, :, ts_], in0=otmp.ap, in1=szT[:, :, ts_], op=ALU.mult),
                      reads=[otmp, szT], writes=[ogp[tt]])
                if tt > 0:
                    self.stageO_tile(j, tt - 1, ogT, x_in, x_out, [psW[1], psW[1]], og_dep=ogp[tt - 1])
            self.stageO_tile(j, 3, ogT, x_in, x_out, [psW[1], psW[1]], og_dep=ogp[3])
        print("GLA arena words used:", A.off)

    def layer_fox(self, L, x_in, x_out):
        sc, A, dr, S = self.sc, self.A, self.dram, self.S
        p = "L%d_" % L
        QA, KA, VS, SZ, OG = self.QA, self.KA, self.VS, self.SZ, self.OG
        self.prep(L, [(2048, 3072)])
        sl = self.slots["c0"]
        gqk = A.alloc([2])
        bfc = A.alloc([1])
        negm_f = A.alloc([128])
        negm = A.alloc([128], BF16)
        bo_f = A.alloc([128])
        bones = A.alloc([128], BF16)
        self_sel = A.alloc([64])
        ones3 = A.alloc([3, ST], BF16)
        sc.dma("sp", sl, lambda e: e.dma_start(out=gqk.ap, in_=dr[p + "gqk"]), writes=[gqk])
        sc.dma("sp", sl, lambda e: e.dma_start(out=bfc[0:16, :], in_=dr[p + "bf"]), writes=[bfc])
        sc.dma("sp", sl, lambda e: e.dma_start(out=negm_f.ap, in_=dr["negmask"]), writes=[negm_f])
        sc.dma("sp", sl, lambda e: e.dma_start(out=bo_f.ap, in_=dr["blockones"]), writes=[bo_f])
        sc.dma("sp", sl, lambda e: e.dma_start(out=self_sel[0:65, :], in_=dr["sel"]), writes=[self_sel])
        sc.barrier()
        sc.op("dve", lambda e: e.tensor_copy(out=negm.ap, in_=negm_f.ap), reads=[negm_f], writes=[negm])
        sc.op("dve", lambda e: e.tensor_copy(out=bones.ap, in_=bo_f.ap), reads=[bo_f], writes=[bones])
        sc.op("dve", lambda e: e.memset(ones3.ap, 1.0), writes=[ones3])
        sc.op("dve", lambda e: e.tensor_scalar(out=gqk[:, 0:1], in0=gqk[:, 0:1], scalar1=0.125, scalar2=None, op0=ALU.mult),
              reads=[gqk], writes=[gqk])
        nfb = A.alloc([1])
        sc.op("dve", lambda e: e.tensor_tensor(out=nfb[0:16, :], in0=self.biascol[0:16, 32:33], in1=bfc[0:16, :], op=ALU.add),
              reads=[self.biascol, bfc], writes=[nfb])
        sc.op("dve", lambda e: e.tensor_scalar(out=nfb[0:16, :], in0=nfb[0:16, :], scalar1=-1.0, scalar2=None, op0=ALU.mult),
              reads=[nfb], writes=[nfb])
        ph_mark = A.mark()
        self.alloc_stageA(nxt=2, nxnT=2)
        sq = [A.alloc([ST], BF16) for _ in range(2)]
        rstd = [A.alloc([ST]) for _ in range(2)]
        tmpf = [A.alloc([ST]) for _ in range(2)]
        qn = [A.alloc([ST], BF16) for _ in range(2)]
        vtok = [A.alloc([D], BF16) for _ in range(2)]
        szT = A.alloc([KC, ST], BF16)
        Ff = [A.alloc([ST]) for _ in range(2)]
        fe = A.alloc([ST])
        fsp = A.alloc([ST])
        fr = A.alloc([ST])
        pcs = [A.alloc([ST], BF16) for _ in range(6)]
        sc.op("dve", lambda e: e.memset(fr.ap, 1.0), writes=[fr])
        psW = [T(self.pst(0, 2), self.psbuf[0]), T(self.pst(0, 2), self.psbuf[0])]
        psA = [T(self.pst(2 + i), self.psbuf[2 + i]) for i in range(5)]
        brow = self.biasrow[(2048, 3072)]
        ia = [0]

        def nextA():
            t = psA[ia[0] % 5]
            ia[0] += 1
            return t

        xnT_next = self.stageA(0, x_in)
        for j in range(self.nST):
            tok0 = j * ST
            xnT = xnT_next
            chunks = [(which, c) for which in range(2) for c in range(KC)]
            pas = {}

            def mmA(i):
                which, c = chunks[i]
                col0 = which * 1024 + c * 128
                pa = nextA()
                pas[i] = pa
                fns = [lambda e, kc=kc: e.matmul(pa.ap, lhsT=self.Win[:, kc, col0:col0 + 128], rhs=xnT[:, kc, :],
                                                 start=(kc == 0), stop=(kc == KC - 1)) for kc in range(KC)]
                sc.pe_group(fns, reads=[xnT, self.Win], writes=[pa])

            mmA(0)
            for i, (which, c) in enumerate(chunks):
                if i + 1 < len(chunks):
                    mmA(i + 1)
                DST = QA if which == 0 else KA
                bcol = self.biascol[:, which * 8 + c:which * 8 + c + 1]
                pa = pas.pop(i)
                sq_, rs_, tf_, q_ = sq[i % 2], rstd[i % 2], tmpf[i % 2], qn[i % 2]
                slq = self.slots["fq%d" % (i % 2)]
                sc.op("act", lambda e: e.activation(out=sq_.ap, in_=pa.ap, func=AF.Square, bias=bcol),
                      reads=[pa, self.biascol], writes=[sq_])
                pb = nextA()
                sc.pe_group([lambda e: e.matmul(pb.ap, lhsT=bones.ap, rhs=sq_.ap, start=True, stop=True)],
                            reads=[bones, sq_], writes=[pb])
                sc.op("act", lambda e: e.activation(out=rs_.ap, in_=pb.ap, func=AF.Ln, scale=1.0 / 64, bias=EPS),
                      reads=[pb], writes=[rs_])
                sc.op("act", lambda e: e.activation(out=rs_.ap, in_=rs_.ap, func=AF.Exp, scale=-0.5),
                      reads=[rs_], writes=[rs_])
                sc.op("dve", lambda e: e.scalar_tensor_tensor(out=tf_.ap, in0=pa.ap, scalar=bcol, in1=rs_.ap,
                                                              op0=ALU.add, op1=ALU.mult),
                      reads=[pa, self.biascol, rs_], writes=[tf_])
                sc.op("pool", lambda e: e.tensor_scalar(out=q_.ap, in0=tf_.ap, scalar1=gqk[:, which:which + 1],
                                                        scalar2=1.0, op0=ALU.mult, op1=ALU.mult),
                      reads=[tf_, gqk], writes=[q_])
                for hh in range(2):
                    sc.dma(STQ, slq, lambda e, hh=hh: e.dma_start(
                        out=DST[2 * c + hh, 0:64, tok0:tok0 + ST], in_=q_[hh * 64:(hh + 1) * 64, :]), reads=[q_])
            if j + 1 < self.nST:
                xnT_next = self.stageA(j + 1, x_in)
            for tt in range(4):
                ts_ = slice(tt * 128, (tt + 1) * 128)
                pv = psW[tt % 2]
                fns = []
                for nb in range(2):
                    for kc in range(KC):
                        fns.append(lambda e, nb=nb, kc=kc, pv=pv, ts_=ts_: e.matmul(
                            pv[:, nb * 512:(nb + 1) * 512], lhsT=xnT[:, kc, ts_],
                            rhs=self.Win[:, kc, 2048 + nb * 512:2048 + (nb + 1) * 512], start=(kc == 0), stop=(kc == KC - 1)))
                sc.pe_group(fns, reads=[xnT, self.Win], writes=[pv])
                v_ = vtok[tt % 2]
                sc.op("dve", lambda e, pv=pv, v_=v_: e.tensor_tensor(out=v_.ap, in0=pv.ap, in1=brow.ap, op=ALU.add),
                      reads=[pv, brow], writes=[v_])
                sc.dma(STQ, self.slots["fv%d" % (tt % 2)], lambda e, v_=v_, tt=tt, tok0=tok0: e.dma_start(
                    out=VS[tok0 + tt * 128:tok0 + (tt + 1) * 128, :], in_=v_.ap), reads=[v_])
            for c in range(KC):
                pa = nextA()
                fns = [lambda e, kc=kc, c=c, pa=pa: e.matmul(pa.ap, lhsT=self.Win[:, kc, 3072 + c * 128:3072 + (c + 1) * 128],
                                                             rhs=xnT[:, kc, :], start=(kc == 0), stop=(kc == KC - 1))
                       for kc in range(KC)]
                sc.pe_group(fns, reads=[xnT, self.Win], writes=[pa])
                sc.op("act", lambda e, c=c, pa=pa: e.activation(out=szT[:, c, :], in_=pa.ap, func=AF.Silu,
                                                                bias=self.biascol[:, 24 + c:25 + c]),
                      reads=[pa, self.biascol], writes=[szT])
            sc.dma(STQ, self.slots["fz0"], lambda e, tok0=tok0: e.dma_start(
                out=SZ[:, tok0:tok0 + ST].rearrange("(c p) t -> p c t", p=128), in_=szT.ap), reads=[szT])
            pa = nextA()
            fns = [lambda e, kc=kc, pa=pa: e.matmul(pa[0:16, :], lhsT=self.Win[:, kc, 4096:4112], rhs=xnT[:, kc, :],
                                                    start=(kc == 0), stop=(kc == KC - 1)) for kc in range(KC)]
            sc.pe_group(fns, reads=[xnT, self.Win], writes=[pa])
            sc.op("act", lambda e, pa=pa: e.activation(out=fe[0:16, :], in_=pa[0:16, :], func=AF.Exp, scale=-1.0, bias=nfb[0:16, :]),
                  reads=[pa, nfb], writes=[fe])
            sc.op("act", lambda e: e.activation(out=fsp[0:16, :], in_=fe[0:16, :], func=AF.Ln, bias=1.0),
                  reads=[fe], writes=[fsp])
            F_ = Ff[j % 2]
            Fp = Ff[(j + 1) % 2]
            init = 0.0 if j == 0 else Fp[0:16, ST - 1:ST]
            sc.op("dve", lambda e, F_=F_, init=init: e.tensor_tensor_scan(out=F_[0:16, :], data0=fr[0:16, :],
                                                                          data1=fsp[0:16, :], initial=init, op0=ALU.mult, op1=ALU.subtract),
                  reads=[fsp, fr] + ([Fp] if j > 0 else []), writes=[F_])
            sc.op("dve", lambda e, F_=F_: e.tensor_copy(out=pcs[0][0:16, :], in_=F_[0:16, :]), reads=[F_], writes=[pcs[0]])
            sc.op("dve", lambda e, F_=F_: e.tensor_tensor(out=fe[0:16, :], in0=F_[0:16, :], in1=pcs[0][0:16, :], op=ALU.subtract),
                  reads=[F_, pcs[0]], writes=[fe])
            sc.op("dve", lambda e: e.tensor_copy(out=pcs[1][0:16, :], in_=fe[0:16, :]), reads=[fe], writes=[pcs[1]])
            sc.op("dve", lambda e: e.tensor_tensor(out=fe[0:16, :], in0=fe[0:16, :], in1=pcs[1][0:16, :], op=ALU.subtract),
                  reads=[fe, pcs[1]], writes=[fe])
            sc.op("dve", lambda e: e.tensor_copy(out=pcs[2][0:16, :], in_=fe[0:16, :]), reads=[fe], writes=[pcs[2]])
            for i in range(3):
                sc.op("pool", lambda e, i=i: e.tensor_scalar(out=pcs[3 + i][0:16, :], in0=pcs[i][0:16, :], scalar1=-1.0, scalar2=1.0,
                                                             op0=ALU.mult, op1=ALU.mult), reads=[pcs[i]], writes=[pcs[3 + i]])
            for i in range(3):
                sc.dma(STQ, self.slots["ff0"], lambda e, i=i, tok0=tok0: e.dma_start(out=QA[:, 64 + i, tok0:tok0 + ST], in_=pcs[i][0:16, :]),
                       reads=[pcs[i]])
                sc.dma(STQ, self.slots["ff0"], lambda e, i=i, tok0=tok0: e.dma_start(out=KA[:, 67 + i, tok0:tok0 + ST], in_=pcs[3 + i][0:16, :]),
                       reads=[pcs[3 + i]])
            sc.dma(STQ, self.slots["ff0"], lambda e, tok0=tok0: e.dma_start(out=QA[:, 67:70, tok0:tok0 + ST], in_=ones3[0:16, :, :]),
                   reads=[ones3])
            sc.dma(STQ, self.slots["ff0"], lambda e, tok0=tok0: e.dma_start(out=KA[:, 64:67, tok0:tok0 + ST], in_=ones3[0:16, :, :]),
                   reads=[ones3])
        sc.barrier()
        self.ck(20)
        A.reset(ph_mark)
        NKT = S // 128
        NQB = S // ST
        KAh = [A.alloc([S], BF16) for _ in range(2)]
        Vh = [A.alloc([NKT, 128], BF16) for _ in range(2)]
        QAq = [A.alloc([ST], BF16) for _ in range(3)]
        szq = [A.alloc([ST], BF16) for _ in range(2)]
        PT = [A.alloc([2, ST], BF16) for _ in range(4)]
        rcs = [A.alloc([ST]) for _ in range(2)]
        tn = A.alloc([ST])
        ogq = [A.alloc([ST], BF16) for _ in range(2)]
        for v_ in Vh:
            sc.op("dve", lambda e, v_=v_: e.memset(v_.ap, 1.0), writes=[v_])
        psS = [T(self.ps[:, 2 * i:2 * i + 2, :], self.psbuf[2 * i]) for i in range(3)]
        psO = [T(self.pst(6), self.psbuf[6]), T(self.pst(7), self.psbuf[7])]
        items = []
        nqb_total = 0
        for h in range(16):
            for qb in range(NQB):
                nk = 4 * qb + 4
                groups = [[kt, kt + 1] for kt in range(0, 4 * qb, 2)] + [[kt] for kt in range(4 * qb, nk)]
                for gi, g in enumerate(groups):
                    items.append(dict(h=h, qb=qb, g=g, nk=nk, first=(gi == 0), last=(gi == len(groups) - 1),
                                      iq=nqb_total, idx=len(items)))
                nqb_total += 1

        def load_head(h):
            ka = KAh[h % 2]
            vh = Vh[h % 2]
            sc.dma("sp", self.slots["fk%d" % (h % 2)], lambda e: e.dma_start(out=ka[0:70, :], in_=KA[h, :, :]), writes=[ka])
            sc.dma("sp", self.slots["fvh%d" % (h % 2)], lambda e: e.dma_start(
                out=vh[:, :, 0:64], in_=VS[:, h * 64:(h + 1) * 64].rearrange("(kt p) d -> p kt d", p=128)), writes=[vh])

        def emit_qk(it):
            h, qb, g, iq = it["h"], it["qb"], it["g"], it["iq"]
            ka = KAh[h % 2]
            qa = QAq[iq % 3]
            if it["first"]:
                q0t = qb * ST
                zq = szq[iq % 2]
                sc.dma("sp", self.slots["fqq%d" % (iq % 3)], lambda e: e.dma_start(out=qa[0:70, :], in_=QA[h, :, q0t:q0t + ST]),
                       writes=[qa])
                sc.dma("sp", self.slots["fsz%d" % (iq % 2)], lambda e: e.dma_start(out=zq[0:64, :], in_=SZ[h * 64:(h + 1) * 64, q0t:q0t + ST]),
                       writes=[zq])
            pS = psS[it["idx"] % 3]
            fns = []
            for i, kt in enumerate(g):
                r = kt - 4 * qb
                q0 = max(r, 0) * 128
                fns.append(lambda e, i=i, kt=kt, q0=q0, r=r: e.matmul(
                    pS[:, i, q0:ST], lhsT=ka[0:70, kt * 128:(kt + 1) * 128], rhs=qa[0:70, q0:ST], start=True, stop=(r < 0)))
                if r >= 0:
                    fns.append(lambda e, i=i, q0=q0: e.matmul(pS[:, i, q0:q0 + 128], lhsT=self.identb.ap, rhs=negm.ap,
                                                              start=False, stop=True))
            sc.pe_group(fns, reads=[ka, qa, self.identb, negm], writes=[pS])

        def emit_act(it):
            g, qb = it["g"], it["qb"]
            pS = psS[it["idx"] % 3]
            pt = PT[it["idx"] % 4]
            if len(g) == 2:
                sc.op("act", lambda e: e.activation(out=pt.ap, in_=pS.ap, func=AF.Exp), reads=[pS], writes=[pt])
            else:
                q0 = max(g[0] - 4 * qb, 0) * 128
                sc.op("act", lambda e: e.activation(out=pt[:, 0, q0:ST], in_=pS[:, 0, q0:ST], func=AF.Exp),
                      reads=[pS], writes=[pt])

        def emit_pv(it):
            h, qb, g, nk, iq = it["h"], it["qb"], it["g"], it["nk"], it["iq"]
            vh = Vh[h % 2]
            pt = PT[it["idx"] % 4]
            po = psO[iq % 2]
            fns = []
            for i, kt in enumerate(g):
                q0 = max(kt - 4 * qb, 0) * 128
                fns.append(lambda e, i=i, kt=kt, q0=q0: e.matmul(
                    po[:, q0:ST], lhsT=vh[:, kt, :], rhs=pt[:, i, q0:ST], start=(kt == 0), stop=(kt == nk - 1)))
            sc.pe_group(fns, reads=[vh, pt], writes=[po])

        def fin_a(it):
            h, qb, iq = it["h"], it["qb"], it["iq"]
            rc, po, og_, zq = rcs[iq % 2], psO[iq % 2], ogq[iq % 2], szq[iq % 2]
            q0t = qb * ST
            sc.op("dve", lambda e: e.reciprocal(out=rc[64:128, :], in_=po[64:128, :]), reads=[po], writes=[rc])
            sc.op("dve", lambda e: e.tensor_tensor(out=tn[0:64, :], in0=po[0:64, :], in1=rc[64:128, :], op=ALU.mult),
                  reads=[po, rc], writes=[tn])
            sc.op("pool", lambda e: e.tensor_tensor(out=og_[0:64, :], in0=tn[0:64, :], in1=zq[0:64, :], op=ALU.mult),
                  reads=[tn, zq], writes=[og_])
            sc.dma(STQ, self.slots["fog%d" % (iq % 2)], lambda e: e.dma_start(
                out=OG[h * 64:(h + 1) * 64, q0t:q0t + ST], in_=og_[0:64, :]), reads=[og_])

        def fin_b(it):
            pass

        load_head(0)
        emit_qk(items[0])
        emit_qk(items[1])
        pending = []
        for i, it in enumerate(items):
            if i + 2 < len(items):
                emit_qk(items[i + 2])
            emit_act(it)
            emit_pv(it)
            if it["first"] and it["qb"] == 0 and it["h"] + 1 < 16:
                load_head(it["h"] + 1)
            for pnd in pending:
                pnd[0] -= 1
            while pending and pending[0][0] <= 0:
                fin_b(pending.pop(0)[1])
            if it["last"]:
                fin_a(it)
                pending.append([2, it])
        for pnd in pending:
            fin_b(pnd[1])
        sc.barrier()
        self.ck(21)
        A.reset(ph_mark)
        self.alloc_stageA(nxt=1, nxnT=1)
        self.alloc_stageO()
        ogT = [A.alloc([KC, ST], BF16) for _ in range(2)]
        psW = [T(self.pst(0, 2), self.psbuf[0]), T(self.pst(2, 2), self.psbuf[2])]
        for j in range(self.nST):
            tok0 = j * ST
            og = ogT[j % 2]
            sc.dma("sp", self.slots["fo%d" % (j % 2)], lambda e, og=og, tok0=tok0: e.dma_start(
                out=og.ap, in_=OG[:, tok0:tok0 + ST].rearrange("(c p) t -> p c t", p=128)), writes=[og])
            self.stageO_full(j, og, x_in, x_out, psW)

    def layer_sgu(self, L, x_in, x_out):
        sc, A, dr = self.sc, self.A, self.dram
        p = "L%d_" % L
        self.prep(L, [(1024, 2048)])
        sc.barrier()
        self.ck(4)
        self.alloc_stageA()
        self.alloc_stageO()
        lng = A.alloc([D])
        lnb = A.alloc([D])
        bsb = A.alloc([4, 128])
        wsf = A.alloc([4, 128])
        WcT = A.alloc([4, 128], BF16)
        sl = self.slots["c0"]
        sc.dma("sp", sl, lambda e: e.dma_start(out=lng.ap, in_=dr[p + "lng"]), writes=[lng])
        sc.dma("sp", sl, lambda e: e.dma_start(out=lnb.ap, in_=dr[p + "lnb"]), writes=[lnb])
        sc.dma("sp", sl, lambda e: e.dma_start(out=bsb.ap, in_=dr[p + "bs"]), writes=[bsb])
        sc.dma("sp", sl, lambda e: e.dma_start(out=wsf.ap, in_=dr[p + "ws"]), writes=[wsf])
        sc.barrier()
        pw = T(self.pst(0), self.psbuf[0])
        fns = [lambda e, g=g: e.transpose(out=pw[:, g * 128:(g + 1) * 128], in_=wsf[:, g, :], identity=self.identf.ap)
               for g in range(4)]
        sc.pe_group(fns, reads=[wsf, self.identf], writes=[pw])
        for g in range(4):
            sc.op("dve", lambda e, g=g: e.tensor_tensor(out=WcT[:, g, :], in0=pw[:, g * 128:(g + 1) * 128], in1=self.trif.ap,
                                                       op=ALU.mult), reads=[pw, self.trif], writes=[WcT])
        self.dbg("Wout", self.Wout); self.dbg("WcT", WcT)
        self.ck(5)
        gl = [A.alloc([D]) for _ in range(4)]
        vln = [A.alloc([D], BF16) for _ in range(4)]
        uT = A.alloc([KC, ST], BF16)
        szT = A.alloc([KC, ST], BF16)
        ogT = A.alloc([KC, ST], BF16)
        mt = [A.alloc([ST]) for _ in range(2)]
        s1 = A.alloc([4])
        nm = A.alloc([4])
        s2 = A.alloc([4])
        lv = A.alloc([4])
        rv = A.alloc([4])
        psW = [T(self.pst(0, 2), self.psbuf[0]), T(self.pst(2, 2), self.psbuf[2])]
        psA = [T(self.pst(4 + i), self.psbuf[4 + i]) for i in range(3)]
        brow = self.biasrow[(1024, 2048)]
        ia = 0
        for j in range(self.nST):
            xnT = self.stageA(j, x_in)
            self.dbg("xnT", xnT); self.dbg("rsA", self.rsA[j % 2])
            self.ck(6)
            for tt in range(4):
                pv = psW[tt % 2]
                fns = []
                for nb in range(2):
                    for kc in range(KC):
                        fns.append(lambda e, nb=nb, kc=kc, pv=pv, tt=tt: e.matmul(
                            pv[:, nb * 512:(nb + 1) * 512], lhsT=xnT[:, kc, tt * 128:(tt + 1) * 128],
                            rhs=self.Win[:, kc, 1024 + nb * 512:1024 + (nb + 1) * 512], start=(kc == 0), stop=(kc == KC - 1)))
                sc.pe_group(fns, reads=[xnT, self.Win], writes=[pv])
                g_ = gl[tt]
                sc.op("dve", lambda e, pv=pv, g_=g_: e.tensor_tensor(out=g_.ap, in0=pv.ap, in1=brow.ap, op=ALU.add),
                      reads=[pv, brow], writes=[g_])
                sc.op("act", lambda e, g_=g_, tt=tt: e.activation(out=g_.ap, in_=g_.ap, func=AF.Gelu_apprx_tanh,
                                                                  accum_out=s1[:, tt:tt + 1]),
                      reads=[g_], writes=[g_, s1])
            self.ck(7)
            for c in range(KC):
                pa = psA[ia % 3]
                ia += 1
                fns = [lambda e, kc=kc, c=c, pa=pa: e.matmul(pa.ap, lhsT=self.Win[:, kc, c * 128:(c + 1) * 128], rhs=xnT[:, kc, :],
                                                             start=(kc == 0), stop=(kc == KC - 1)) for kc in range(KC)]
                sc.pe_group(fns, reads=[xnT, self.Win], writes=[pa])
                sc.op("act", lambda e, c=c, pa=pa: e.activation(out=uT[:, c, :], in_=pa.ap, func=AF.Gelu_apprx_tanh,
                                                                bias=self.biascol[:, c:c + 1]),
                      reads=[pa, self.biascol], writes=[uT])
            self.dbg("gl0", gl[0]); self.dbg("uT", uT); self.dbg("s1", s1)
            self.ck(8)
            sc.op("dve", lambda e: e.tensor_scalar(out=nm.ap, in0=s1.ap, scalar1=-1.0 / D, scalar2=None, op0=ALU.mult),
                  reads=[s1], writes=[nm])
            for tt in range(4):
                g_ = gl[tt]
                sc.op("act", lambda e, g_=g_, tt=tt: e.activation(out=self.junk.ap, in_=g_.ap, func=AF.Square,
                                                                  bias=nm[:, tt:tt + 1], accum_out=s2[:, tt:tt + 1]),
                      reads=[g_, nm], writes=[self.junk, s2])
            self.rstd_from_ss(s2, rv, 1.0 / D, lv)
            for tt in range(4):
                g_ = gl[tt]
                sc.op("dve", lambda e, g_=g_, tt=tt: e.tensor_scalar(out=g_.ap, in0=g_.ap, scalar1=nm[:, tt:tt + 1],
                                                                    scalar2=rv[:, tt:tt + 1], op0=ALU.add, op1=ALU.mult),
                      reads=[g_, nm, rv], writes=[g_])
                sc.op("pool", lambda e, g_=g_: e.tensor_tensor(out=g_.ap, in0=g_.ap, in1=lng.ap, op=ALU.mult),
                      reads=[g_, lng], writes=[g_])
                sc.op("pool", lambda e, g_=g_, tt=tt: e.tensor_tensor(out=vln[tt].ap, in0=g_.ap, in1=lnb.ap, op=ALU.add),
                      reads=[g_, lnb], writes=[vln[tt]])
            self.dbg("vln0", vln[0]); self.dbg("rv", rv)
            self.ck(9)
            for c in range(KC):
                pa = psA[ia % 3]
                ia += 1
                fns = [lambda e, kc=kc, c=c, pa=pa: e.matmul(pa.ap, lhsT=self.Win[:, kc, 2048 + c * 128:2048 + (c + 1) * 128],
                                                             rhs=xnT[:, kc, :], start=(kc == 0), stop=(kc == KC - 1))
                       for kc in range(KC)]
                sc.pe_group(fns, reads=[xnT, self.Win], writes=[pa])
                sc.op("act", lambda e, c=c, pa=pa: e.activation(out=szT[:, c, :], in_=pa.ap, func=AF.Silu,
                                                                bias=self.biascol[:, 16 + c:17 + c]),
                      reads=[pa, self.biascol], writes=[szT])
            self.ck(10)
            for c in range(KC):
                g = c // 2
                pa = psA[ia % 3]
                ia += 1
                fns = [lambda e, c=c, g=g, tt=tt, pa=pa: e.matmul(pa[:, tt * 128:(tt + 1) * 128], lhsT=vln[tt][:, c * 128:(c + 1) * 128],
                                                                  rhs=WcT[:, g, :], start=True, stop=True) for tt in range(4)]
                sc.pe_group(fns, reads=vln + [WcT], writes=[pa])
                m_ = mt[c % 2]
                for tt in range(4):
                    sc.op("dve", lambda e, pa=pa, m_=m_, g=g, tt=tt: e.tensor_tensor(
                        out=m_[:, tt * 128:(tt + 1) * 128], in0=pa[:, tt * 128:(tt + 1) * 128], in1=bsb[:, g, :], op=ALU.add),
                        reads=[pa, bsb], writes=[m_])
                sc.op("dve", lambda e, m_=m_, c=c: e.tensor_tensor(out=m_.ap, in0=m_.ap, in1=uT[:, c, :], op=ALU.mult),
                      reads=[m_, uT], writes=[m_])
                sc.op("pool", lambda e, m_=m_, c=c: e.tensor_tensor(out=ogT[:, c, :], in0=m_.ap, in1=szT[:, c, :], op=ALU.mult),
                      reads=[m_, szT], writes=[ogT])
            self.dbg("szT", szT); self.dbg("ogT", ogT)
            self.ck(11)
            self.stageO_full(j, ogT, x_in, x_out, psW)


def col8(v):
    return np.ascontiguousarray(v.reshape(-1, 128).T)


def rep(v, n=128):
    return np.ascontiguousarray(np.broadcast_to(v.reshape(1, -1), (n, v.size)))


def make_in_maps(inputs, layer_ids, S, n_cores):
    f = lambda a: np.ascontiguousarray(np.asarray(a, dtype=np.float32))
    x = f(inputs["x"])
    c = f(inputs["c"])
    shared = {"ident": np.eye(128, dtype=np.float32),
              "tri": np.triu(np.ones((128, 128), dtype=np.float32)),
              "uneg": np.triu(np.full((128, 128), -1.0 / 16, dtype=np.float32)),
              "negmask": np.tril(np.full((128, 128), -30000.0, dtype=np.float32), -1),
              "blockones": np.kron(np.eye(2, dtype=np.float32), np.ones((64, 64), dtype=np.float32)),
              "sel": np.concatenate([np.zeros((64, 64), np.float32), np.ones((1, 64), np.float32)], 0)}
    for L in layer_ids:
        kind, jj = L % 3, L // 3
        p = "L%d_" % L
        bm = f(inputs["b_mod"][L])
        shared[p + "wmod"] = f(inputs["w_mod"][L])
        shared[p + "bmodc"] = col8(bm)
        shared[p + "bmodg"] = rep(bm[2048:3072])
        shared[p + "gprec"] = col8(f(inputs["norm_pre_g"][L]))
        shared[p + "gpost"] = rep(f(inputs["norm_post_g"][L]))
        if kind == 0:
            shared[p + "win"] = f(inputs["gla_w_in"][jj])
            shared[p + "wout"] = f(inputs["gla_w_out"][jj])
            shared[p + "wa2"] = np.ascontiguousarray(np.concatenate([f(inputs["gla_w_a2"][jj]), f(inputs["gla_b_a"][jj])[None]], 0))
            shared[p + "ghc"] = col8(f(inputs["gla_g_head"][jj]).reshape(-1))
        if kind == 2:
            shared[p + "win"] = f(inputs["fox_w_in"][jj])
            shared[p + "wout"] = f(inputs["fox_w_out"][jj])
            shared[p + "gqk"] = np.ascontiguousarray(np.stack([np.tile(f(inputs["fox_g_q"][jj]), 2), np.tile(f(inputs["fox_g_k"][jj]), 2)], 1))
            shared[p + "bf"] = np.ascontiguousarray(f(inputs["fox_b_f"][jj])[:, None])
        if kind == 1:
            shared[p + "win"] = f(inputs["sgu_w_in"][jj])
            shared[p + "wout"] = f(inputs["sgu_w_out"][jj])
            shared[p + "lng"] = rep(f(inputs["sgu_ln_g"][jj]))
            shared[p + "lnb"] = rep(f(inputs["sgu_ln_b"][jj]))
            shared[p + "ws"] = np.ascontiguousarray(f(inputs["sgu_w_s"][jj]).transpose(1, 0, 2))
            shared[p + "bs"] = np.ascontiguousarray(np.broadcast_to(f(inputs["sgu_b_s"][jj])[None], (128, 4, 128)))
    maps = []
    for b in range(n_cores):
        m = dict(shared)
        m["x"] = np.ascontiguousarray(x[b, :S])
        m["ccol"] = col8(c[b])
        maps.append(m)
    return maps


_PROG_CACHE = {}


def run(inputs, layer_ids=(0, 1, 2, 3), S=8192, n_cores=N_CORES):
    key = (S, tuple(layer_ids))
    if key not in _PROG_CACHE:
        pr = Prog(S, layer_ids)
        pr.build()
        _PROG_CACHE[key] = pr
    pr = _PROG_CACHE[key]
    maps = make_in_maps(inputs, layer_ids, S, n_cores)
    res = run_bass_kernel_spmd(pr.nc, maps, core_ids=list(range(n_cores)))
    return np.stack([np.asarray(r["out"]) for r in res.results], axis=0)


def kernel(**inputs):
    return run(inputs).astype(np.float32)
```

```python
import numpy as np
from contextlib import ExitStack
import concourse.bass as bass
import concourse.mybir as mybir
from concourse.bass_utils import run_bass_kernel_spmd

F32 = mybir.dt.float32
BF16 = mybir.dt.bfloat16
AF = mybir.ActivationFunctionType
ALU = mybir.AluOpType

D = 1024
KC = 8
ST = 512
EPS = 1e-6
N_IN = {0: 3088, 1: 3072, 2: 4112}
N_CORES = 8


class Buf:
    __slots__ = ("w", "r", "excl")

    def __init__(self, excl=False):
        self.w = {}
        self.r = {}
        self.excl = excl


class T:
    __slots__ = ("ap", "buf")

    def __init__(self, ap, buf=None):
        self.ap = ap
        self.buf = buf if buf is not None else Buf()

    def __getitem__(self, k):
        return self.ap[k]


def _b(x):
    return x.buf if isinstance(x, T) else x


class _Rec:
    def __init__(self):
        self.call = None

    def __getattr__(self, name):
        def f(*a, **kw):
            assert self.call is None
            self.call = (name, a, kw)
            return None
        return f


def _capture(fn):
    r = _Rec()
    fn(r)
    name, a, kw = r.call
    line = fn.__code__.co_firstlineno

    def replay(eng):
        return getattr(eng, name)(*a, **kw)
    replay.line = line
    return replay


class Sched:
    ENG = ("pe", "act", "dve", "pool", "sp")

    def __init__(self, nc, es):
        self.nc = nc
        self.es = es
        self.q = {e: [] for e in self.ENG}
        self.cnt = {e: 0 for e in self.ENG}
        self.seen = {e: {} for e in self.ENG}
        self.sems = {}
        self.names = {}
        self.dcnt = {}
        for e in self.ENG:
            self.sems[e] = es.enter_context(nc.semaphore("s_" + e))

    def slot(self, name):
        k = "d_" + name
        self.sems[k] = self.es.enter_context(self.nc.semaphore(k))
        self.dcnt[k] = 0
        return k

    @staticmethod
    def _split(reads, writes):
        r2 = [b for b in reads if not _b(b).excl]
        w2 = list(writes) + [b for b in reads if _b(b).excl]
        return r2, w2

    def _waits(self, eng, reads, writes):
        reads, writes = self._split(reads, writes)
        need = {}
        for b in reads:
            for k, v in _b(b).w.items():
                if need.get(k, 0) < v:
                    need[k] = v
        for b in writes:
            bb = _b(b)
            for dct in (bb.w, bb.r):
                for k, v in dct.items():
                    if need.get(k, 0) < v:
                        need[k] = v
        waits = []
        seen = self.seen[eng]
        for k, v in need.items():
            if k in self.dcnt:
                v = self.dcnt[k]
            if k == eng and eng == "pe":
                continue
            if seen.get(k, 0) >= v:
                continue
            seen[k] = v
            waits.append((k, v))
        return waits

    def _record(self, ev, reads, writes):
        reads, writes = self._split(reads, writes)
        k, v = ev
        for b in reads:
            bb = _b(b)
            if bb.r.get(k, 0) < v:
                bb.r[k] = v
        for b in writes:
            bb = _b(b)
            bb.r = {}
            if bb.w.get(k, 0) < v:
                bb.w[k] = v

    def op(self, eng, fn, reads=(), writes=()):
        waits = self._waits(eng, reads, writes)
        self.cnt[eng] += 1
        ev = (eng, self.cnt[eng])
        self.q[eng].append((waits, _capture(fn), (eng, 1)))
        self._record(ev, reads, writes)

    def pe_group(self, fns, reads=(), writes=()):
        waits = self._waits("pe", reads, writes)
        self.cnt["pe"] += 1
        ev = ("pe", self.cnt["pe"])
        n = len(fns)
        for i, fn in enumerate(fns):
            self.q["pe"].append((waits if i == 0 else [], _capture(fn), ("pe", 1) if i == n - 1 else None))
        self._record(ev, reads, writes)

    def dma(self, q, slot, fn, reads=(), writes=()):
        waits = self._waits(q, reads, writes)
        self.dcnt[slot] += 16
        ev = (slot, self.dcnt[slot])
        self.q[q].append((waits, _capture(fn), (slot, 16)))
        self._record(ev, reads, writes)

    def barrier(self):
        for e in self.ENG:
            waits = []
            for k in list(self.ENG) + list(self.dcnt.keys()):
                if k == e:
                    continue
                v = self.cnt[k] if k in self.cnt else self.dcnt[k]
                if v > 0 and self.seen[e].get(k, 0) < v:
                    self.seen[e][k] = v
                    waits.append((k, v))
            if waits:
                self.q[e].append((waits, None, None))

    def emit(self):
        nc = self.nc

        def replay(name, eng):
            for waits, fn, inc in self.q[name]:
                for k, v in waits:
                    eng.wait_ge(self.sems[k], v)
                if fn is None:
                    continue
                ins = fn(eng)
                try:
                    self.names[ins.ins.name] = (name, fn.line)
                except Exception:
                    pass
                if inc is not None:
                    ins.then_inc(self.sems[inc[0]], inc[1])

        with nc.Block() as block:
            @block.sync
            def _(e):
                replay("sp", e)

            @block.tensor
            def _(e):
                replay("pe", e)

            @block.scalar
            def _(e):
                replay("act", e)

            @block.vector
            def _(e):
                replay("dve", e)

            @block.gpsimd
            def _(e):
                replay("pool", e)


class Arena:
    def __init__(self, ap, size):
        self.ap = ap
        self.size = size
        self.off = 0

    def mark(self):
        return self.off

    def reset(self, m):
        self.off = m

    def alloc(self, shape, dtype=F32):
        n = int(np.prod(shape))
        n32 = n if dtype == F32 else (n + 1) // 2
        assert self.off + n32 <= self.size, ("SBUF arena overflow", self.off, n32, self.size)
        v = self.ap[:, self.off:self.off + n32]
        self.off += n32
        if dtype == BF16:
            v = v.bitcast(BF16)
        if len(shape) == 2:
            v = v.rearrange("p (a b) -> p a b", a=shape[0])
        elif len(shape) == 3:
            v = v.rearrange("p (a b c) -> p a b c", a=shape[0], b=shape[1])
        return T(v)


class _Stop(Exception):
    pass


STOP_AT = [None]
STQ = "pool"
DEBUG = [False]


class Prog:
    def dbg(self, name, t):
        if not DEBUG[0]:
            return
        nm = "dbg_%s_%d" % (name, len(self.dbg_names))
        self.dbg_names.append(nm)
        shp = list(t.ap.shape)
        d = self.nc.dram_tensor(nm, shp, t.ap.dtype, kind="ExternalOutput").ap()
        sl = self.sc.slot(nm)
        self.sc.dma("sp", sl, lambda e: e.dma_start(out=d, in_=t.ap), reads=[t])

    def ck(self, n):
        if STOP_AT[0] is not None and n >= STOP_AT[0]:
            raise _Stop()

    def __init__(self, S, layer_ids):
        self.S = S
        self.layer_ids = list(layer_ids)
        self.nST = S // ST
        self.in_names = []
        self.dbg_names = []
        nc = self.nc = bass.Bass("TRN2", target_bir_lowering=False)
        self.dram = {}
        self._din("x", [S, D])
        self._din("ccol", [128, 8])
        self._din("ident", [128, 128])
        self._din("tri", [128, 128])
        self._din("uneg", [128, 128])
        self._din("negmask", [128, 128])
        self._din("blockones", [128, 128])
        self._din("sel", [65, 64])
        for L in self.layer_ids:
            kind = L % 3
            p = "L%d_" % L
            self._din(p + "wmod", [D, 3 * D])
            self._din(p + "bmodc", [128, 24])
            self._din(p + "bmodg", [128, D])
            self._din(p + "gprec", [128, 8])
            self._din(p + "gpost", [128, D])
            self._din(p + "win", [D, N_IN[kind]])
            self._din(p + "wout", [D, D])
            if kind == 0:
                self._din(p + "wa2", [17, 512])
                self._din(p + "ghc", [128, 8])
            if kind == 2:
                self._din(p + "gqk", [128, 2])
                self._din(p + "bf", [16, 1])
            if kind == 1:
                self._din(p + "lng", [128, D])
                self._din(p + "lnb", [128, D])
                self._din(p + "ws", [128, 4, 128])
                self._din(p + "bs", [128, 4, 128])
        self.out = nc.dram_tensor("out", [S, D], F32, kind="ExternalOutput").ap()
        self.xs = [nc.dram_tensor("xs%d" % i, [S, D], F32, kind="Internal").ap() for i in range(2)]
        if any(L % 3 == 2 for L in self.layer_ids):
            self.QA = nc.dram_tensor("fox_qa", [16, 70, S], BF16, kind="Internal").ap()
            self.KA = nc.dram_tensor("fox_ka", [16, 70, S], BF16, kind="Internal").ap()
            self.VS = nc.dram_tensor("fox_v", [S, D], BF16, kind="Internal").ap()
            self.SZ = nc.dram_tensor("fox_sz", [D, S], BF16, kind="Internal").ap()
            self.OG = nc.dram_tensor("fox_og", [D, S], BF16, kind="Internal").ap()

    def _din(self, name, shape, dtype=F32):
        self.in_names.append(name)
        self.dram[name] = self.nc.dram_tensor(name, shape, dtype, kind="ExternalInput").ap()

    def rstd_from_ss(self, ss, out, n_inv, tmp):
        sc = self.sc
        sc.op("act", lambda e: e.activation(out=tmp.ap, in_=ss.ap, func=AF.Ln, scale=n_inv, bias=EPS),
              reads=[ss], writes=[tmp])
        sc.op("act", lambda e: e.activation(out=out.ap, in_=tmp.ap, func=AF.Exp, scale=-0.5),
              reads=[tmp], writes=[out])

    def build(self):
        nc = self.nc
        with ExitStack() as es:
            arena_t = es.enter_context(nc.sbuf_tensor("arena", [128, 53000], F32))
            ps_t = es.enter_context(nc.psum_tensor("ps", [128, 8, 512], F32))
            self.sc = sc = Sched(nc, es)
            self.A = A = Arena(arena_t[:, :], 53000)
            self.ps = ps_t
            self.psbuf = [Buf(excl=True) for _ in range(8)]
            self.slots = {}
            for nm in ["c0", "stg0", "stg1", "x0", "x1", "x2", "x3", "xo0", "xo1", "xo2", "xo3", "xs0", "xs1", "xs2", "xs3", "sm0", "sm1", "sm2", "sm3",
                       "fq0", "fq1", "fv0", "fv1", "fz0", "ff0", "fk0", "fk1", "fvh0", "fvh1", "fqq0", "fqq1", "fqq2",
                       "fsz0", "fsz1", "fog0", "fog1", "fo0", "fo1"]:
                self.slots[nm] = sc.slot(nm)
            self.global_consts()
            x_in = self.dram["x"]
            try:
                for li, L in enumerate(self.layer_ids):
                    x_out = self.out if li == len(self.layer_ids) - 1 else self.xs[li % 2]
                    m = A.mark()
                    self.layer(L, x_in, x_out)
                    sc.barrier()
                    A.reset(m)
                    x_in = x_out
            except _Stop:
                pass
            sc.barrier()
            sc.emit()
        return nc

    def pst(self, bank, n=1):
        if n == 1:
            return self.ps[:, bank, :]
        return self.ps[:, bank:bank + n, :].rearrange("p a b -> p (a b)")

    def global_consts(self):
        sc, A, dr = self.sc, self.A, self.dram
        sl = self.slots["c0"]
        self.identf = A.alloc([128])
        self.identb = A.alloc([128], BF16)
        self.trif = A.alloc([128])
        self.unegf = A.alloc([128])
        self.onesb = A.alloc([128], BF16)
        self.ones = A.alloc([128])
        self.cond = A.alloc([8])
        self.cond_rep = A.alloc([8, 128])
        cc = A.alloc([8])
        sc.dma("sp", sl, lambda e: e.dma_start(out=self.identf.ap, in_=dr["ident"]), writes=[self.identf])
        sc.dma("sp", sl, lambda e: e.dma_start(out=self.trif.ap, in_=dr["tri"]), writes=[self.trif])
        sc.dma("sp", sl, lambda e: e.dma_start(out=self.unegf.ap, in_=dr["uneg"]), writes=[self.unegf])
        sc.dma("sp", sl, lambda e: e.dma_start(out=cc.ap, in_=dr["ccol"]), writes=[cc])
        sc.barrier()
        sc.op("dve", lambda e: e.tensor_copy(out=self.identb.ap, in_=self.identf.ap), reads=[self.identf], writes=[self.identb])
        sc.op("dve", lambda e: e.memset(self.ones.ap, 1.0), writes=[self.ones])
        sc.op("dve", lambda e: e.memset(self.onesb.ap, 1.0), writes=[self.onesb])
        sc.op("act", lambda e: e.activation(out=self.cond.ap, in_=cc.ap, func=AF.Silu), reads=[cc], writes=[self.cond])
        for kc in range(KC):
            sc.op("dve", lambda e, kc=kc: e.tensor_scalar(out=self.cond_rep[:, kc, :], in0=self.ones.ap,
                                                         scalar1=self.cond[:, kc:kc + 1], scalar2=None, op0=ALU.mult),
                  reads=[self.ones, self.cond], writes=[self.cond_rep])

    def prep(self, L, tokmajor_ranges, nocol_ranges=None, wout_rowscale=None):
        sc, A, dr = self.sc, self.A, self.dram
        kind = L % 3
        p = "L%d_" % L
        nin = N_IN[kind]
        BW = 256
        if nocol_ranges is None:
            nocol_ranges = tokmajor_ranges
        self.Win = A.alloc([KC, nin], BF16)
        self.Wout = A.alloc([KC, D], BF16)
        self.Gbc = A.alloc([D])
        nch = (nin + 127) // 128
        self.biascol = A.alloc([nch])
        self.biasrow = {r: A.alloc([r[1] - r[0]]) for r in tokmajor_ranges}
        tmp_mark = A.mark()
        stg = [A.alloc([KC, BW]) for _ in range(2)]
        stg_slot = [self.slots["stg0"], self.slots["stg1"]]
        modc = A.alloc([16])
        acol = A.alloc([8])
        shift_rep = A.alloc([KC, 128])
        small = A.alloc([24 + 8])
        bmodc, gprec = T(small[:, 0:24], small.buf), T(small[:, 24:32], small.buf)
        gtmp = A.alloc([D])
        gpost = A.alloc([D])
        sl = self.slots["c0"]
        sc.dma("sp", sl, lambda e: e.dma_start(out=bmodc.ap, in_=dr[p + "bmodc"]), writes=[small])
        sc.dma("sp", sl, lambda e: e.dma_start(out=gprec.ap, in_=dr[p + "gprec"]), writes=[small])
        sc.dma("sp", sl, lambda e: e.dma_start(out=gtmp.ap, in_=dr[p + "bmodg"]), writes=[gtmp])
        sc.dma("sp", sl, lambda e: e.dma_start(out=gpost.ap, in_=dr[p + "gpost"]), writes=[gpost])
        sc.barrier()
        blk = [0]

        def load_block(src, c0, w):
            i = blk[0] % 2
            blk[0] += 1
            s = stg[i]
            sc.dma("sp", stg_slot[i],
                   lambda e: e.dma_start(out=s[:, :, 0:w], in_=src[:, c0:c0 + w].rearrange("(kc p) n -> p kc n", p=128)),
                   writes=[s])
            return s

        PB_MOD, PB_G, PB_BC, PB_BR = 0, 1, 3, 4
        psmod = T(self.pst(PB_MOD), self.psbuf[PB_MOD])
        psG = [T(self.pst(PB_G + i), self.psbuf[PB_G + i]) for i in range(2)]
        wmod = dr[p + "wmod"]
        for j in range(12):
            s = load_block(wmod, j * BW, BW)
            if j < 8:
                fns = []
                for h in range(2):
                    ch = j * 2 + h
                    for kc in range(KC):
                        fns.append(lambda e, ch=ch, h=h, kc=kc, s=s: e.matmul(
                            psmod[:, ch:ch + 1], lhsT=s[:, kc, h * 128:(h + 1) * 128], rhs=self.cond[:, kc:kc + 1],
                            start=(kc == 0), stop=(kc == KC - 1)))
                sc.pe_group(fns, reads=[s, self.cond], writes=[psmod])
            else:
                g = j - 8
                fns = [lambda e, kc=kc, s=s, g=g: e.matmul(
                    psG[g // 2][:, (g % 2) * BW:(g % 2 + 1) * BW], lhsT=self.cond_rep[:, kc, :], rhs=s[:, kc, :],
                    start=(kc == 0), stop=(kc == KC - 1)) for kc in range(KC)]
                sc.pe_group(fns, reads=[s, self.cond_rep], writes=[psG[g // 2]])
        self.ck(1)
        sc.op("dve", lambda e: e.tensor_tensor(out=modc.ap, in0=psmod[:, 0:16], in1=bmodc[:, 0:16], op=ALU.add),
              reads=[psmod, small], writes=[modc])
        sc.op("dve", lambda e: e.scalar_tensor_tensor(out=acol.ap, in0=modc[:, 8:16], scalar=1.0, in1=gprec.ap,
                                                      op0=ALU.add, op1=ALU.mult),
              reads=[modc, small], writes=[acol])
        for kc in range(KC):
            sc.op("dve", lambda e, kc=kc: e.tensor_scalar(out=shift_rep[:, kc, :], in0=self.ones.ap,
                                                         scalar1=modc[:, kc:kc + 1], scalar2=None, op0=ALU.mult),
                  reads=[self.ones, modc], writes=[shift_rep])
        for i in range(2):
            sc.op("dve", lambda e, i=i: e.tensor_tensor(out=gtmp[:, i * 512:(i + 1) * 512], in0=psG[i].ap,
                                                       in1=gtmp[:, i * 512:(i + 1) * 512], op=ALU.add),
                  reads=[psG[i], gtmp], writes=[gtmp])
        sc.op("dve", lambda e: e.tensor_tensor(out=self.Gbc.ap, in0=gtmp.ap, in1=gpost.ap, op=ALU.mult),
              reads=[gtmp, gpost], writes=[self.Gbc])
        self.dbg("modc", modc); self.dbg("acol", acol); self.dbg("Gbc", self.Gbc)
        self.ck(2)
        win = dr[p + "win"]
        psbc = T(self.pst(PB_BC), self.psbuf[PB_BC])
        psbr = [T(self.pst(PB_BR + i), self.psbuf[PB_BR + i]) for i in range(2)]
        written = []
        c0 = 0
        tog = 0
        while c0 < nin:
            w = min(BW, nin - c0)
            s = load_block(win, c0, w)
            rng = None
            for r in tokmajor_ranges:
                if r[0] <= c0 < r[1]:
                    rng = r
            if rng is not None:
                pb = psbr[tog % 2]
                tog += 1
                fns = [lambda e, kc=kc, s=s, pb=pb, w=w: e.matmul(pb[:, 0:w], lhsT=shift_rep[:, kc, :], rhs=s[:, kc, 0:w],
                                                                   start=(kc == 0), stop=(kc == KC - 1)) for kc in range(KC)]
                sc.pe_group(fns, reads=[s, shift_rep], writes=[pb])
                br = self.biasrow[rng]
                o = c0 - rng[0]
                sc.op("act", lambda e, br=br, o=o, w=w, pb=pb: e.copy(out=br[:, o:o + w], in_=pb[:, 0:w]),
                      reads=[pb], writes=[br])
            if not any(r[0] <= c0 < r[1] for r in nocol_ranges):
                fns = []
                for h in range((w + 127) // 128):
                    ch = c0 // 128 + h
                    hw = min(128, w - h * 128)
                    written.append((ch, hw))
                    for kc in range(KC):
                        fns.append(lambda e, ch=ch, h=h, hw=hw, kc=kc, s=s: e.matmul(
                            psbc[0:hw, ch:ch + 1], lhsT=s[:, kc, h * 128:h * 128 + hw], rhs=modc[:, kc:kc + 1],
                            start=(kc == 0), stop=(kc == KC - 1)))
                sc.pe_group(fns, reads=[s, modc], writes=[psbc])
            for kc in range(KC):
                eng = "dve" if kc % 2 == 0 else "pool"
                if eng == "dve":
                    sc.op(eng, lambda e, kc=kc, s=s, c0=c0, w=w: e.tensor_scalar(
                        out=self.Win[:, kc, c0:c0 + w], in0=s[:, kc, 0:w], scalar1=acol[:, kc:kc + 1], scalar2=None,
                        op0=ALU.mult), reads=[s, acol], writes=[self.Win])
                else:
                    sc.op(eng, lambda e, kc=kc, s=s, c0=c0, w=w: e.tensor_scalar(
                        out=self.Win[:, kc, c0:c0 + w], in0=s[:, kc, 0:w], scalar1=acol[:, kc:kc + 1], scalar2=1.0,
                        op0=ALU.mult, op1=ALU.mult), reads=[s, acol], writes=[self.Win])
            c0 += w
        for ch, hw in written:
            sc.op("dve", lambda e, ch=ch, hw=hw: e.tensor_copy(out=self.biascol[0:hw, ch:ch + 1], in_=psbc[0:hw, ch:ch + 1]),
                  reads=[psbc], writes=[self.biascol])
        self.dbg("Win", self.Win); self.dbg("biascol", T(self.biascol[:, 0:8], self.biascol.buf))
        for r_, t_ in self.biasrow.items():
            self.dbg("biasrow", t_)
        self.ck(3)
        wout = dr[p + "wout"]
        for j in range(D // BW):
            s = load_block(wout, j * BW, BW)
            if wout_rowscale is None:
                sc.op("dve", lambda e, s=s, j=j: e.tensor_copy(out=self.Wout[:, 0:4, j * BW:(j + 1) * BW], in_=s[:, 0:4, :]),
                      reads=[s], writes=[self.Wout])
                sc.op("act", lambda e, s=s, j=j: e.copy(out=self.Wout[:, 4:8, j * BW:(j + 1) * BW], in_=s[:, 4:8, :]),
                      reads=[s], writes=[self.Wout])
            else:
                for kc in range(KC):
                    sc.op("dve", lambda e, s=s, j=j, kc=kc: e.tensor_scalar(
                        out=self.Wout[:, kc, j * BW:(j + 1) * BW], in0=s[:, kc, :], scalar1=wout_rowscale[:, kc:kc + 1],
                        scalar2=None, op0=ALU.mult), reads=[s, wout_rowscale], writes=[self.Wout])
        sc.barrier()
        A.reset(tmp_mark)

    def alloc_stageA(self, nxt=4, nxnT=2):
        A = self.A
        self.xt = [A.alloc([D]) for _ in range(nxt)]
        self.xn = [A.alloc([D], BF16) for _ in range(2)]
        self.xnT = [A.alloc([KC, ST], BF16) for _ in range(nxnT)]
        self.junk = A.alloc([D], BF16)
        self.ssA = [A.alloc([4]) for _ in range(2)]
        self.lnA = [A.alloc([4]) for _ in range(2)]
        self.rsA = [A.alloc([4]) for _ in range(2)]
        self.psT = T(self.ps[:, 7, :].bitcast(BF16).rearrange("p (a b) -> p a b", a=8), self.psbuf[7])

    def stageA(self, j, x_in):
        sc = self.sc
        ss, ln, rs, xnT = self.ssA[j % 2], self.lnA[j % 2], self.rsA[j % 2], self.xnT[j % len(self.xnT)]
        nxt = len(self.xt)
        for tt in range(4):
            tok = j * ST + tt * 128
            xt = self.xt[tt % nxt]
            xn = self.xn[tt % 2]
            sc.dma("sp", self.slots["x%d" % (tt % nxt)], lambda e, xt=xt, tok=tok: e.dma_start(out=xt.ap, in_=x_in[tok:tok + 128, :]),
                   writes=[xt])
            sc.op("act", lambda e, xt=xt, tt=tt, ss=ss: e.activation(out=self.junk.ap, in_=xt.ap, func=AF.Square,
                                                                     accum_out=ss[:, tt:tt + 1]),
                  reads=[xt], writes=[self.junk, ss])
            sc.op("act", lambda e, tt=tt: e.activation(out=ln[:, tt:tt + 1], in_=ss[:, tt:tt + 1], func=AF.Ln, scale=1.0 / D, bias=EPS),
                  reads=[ss], writes=[ln])
            sc.op("act", lambda e, tt=tt: e.activation(out=rs[:, tt:tt + 1], in_=ln[:, tt:tt + 1], func=AF.Exp, scale=-0.5),
                  reads=[ln], writes=[rs])
            sc.op("act", lambda e, xt=xt, xn=xn, tt=tt, rs=rs: e.activation(out=xn.ap, in_=xt.ap, func=AF.Copy,
                                                                            scale=rs[:, tt:tt + 1]),
                  reads=[xt, rs], writes=[xn])
            fns = [lambda e, c=c, xn=xn: e.transpose(out=self.psT[:, c, :], in_=xn[:, c * 128:(c + 1) * 128],
                                                     identity=self.identb.ap) for c in range(KC)]
            sc.pe_group(fns, reads=[xn, self.identb], writes=[self.psT])
            sc.op("dve", lambda e, tt=tt, xnT=xnT: e.tensor_copy(out=xnT[:, :, tt * 128:(tt + 1) * 128], in_=self.psT.ap),
                  reads=[self.psT], writes=[xnT])
        return xnT

    def alloc_stageO(self, nxo=4, nt1=2):
        A = self.A
        self.xo = [A.alloc([D]) for _ in range(nxo)]
        self.t1 = [A.alloc([D]) for _ in range(nt1)]
        self.ssO = [A.alloc([4]) for _ in range(2)]
        self.lnO = [A.alloc([4]) for _ in range(2)]
        self.rsO = [A.alloc([4]) for _ in range(2)]

    def stageO(self, j, ogT, x_in, x_out, psY):
        sc = self.sc
        ss, ln, rs = self.ssO[j % 2], self.lnO[j % 2], self.rsO[j % 2]
        for tt in range(4):
            tok = j * ST + tt * 128
            py = psY[tt % 2]
            xo = self.xo[tt % 2]
            t1 = self.t1[tt % 2]
            sc.dma("sp", self.slots["xo%d" % (tt % 2)], lambda e, xo=xo, tok=tok: e.dma_start(out=xo.ap, in_=x_in[tok:tok + 128, :]),
                   writes=[xo])
            fns = []
            for nb in range(2):
                for c in range(KC):
                    fns.append(lambda e, nb=nb, c=c, py=py, tt=tt: e.matmul(
                        py[:, nb * 512:(nb + 1) * 512], lhsT=ogT[:, c, tt * 128:(tt + 1) * 128],
                        rhs=self.Wout[:, c, nb * 512:(nb + 1) * 512], start=(c == 0), stop=(c == KC - 1)))
            sc.pe_group(fns, reads=[ogT, self.Wout], writes=[py])
            sc.op("act", lambda e, py=py, tt=tt, ss=ss: e.activation(out=self.junk.ap, in_=py.ap, func=AF.Square,
                                                                     accum_out=ss[:, tt:tt + 1]),
                  reads=[py], writes=[self.junk, ss])
            sc.op("dve", lambda e, py=py, t1=t1: e.tensor_tensor(out=t1.ap, in0=py.ap, in1=self.Gbc.ap, op=ALU.mult),
                  reads=[py, self.Gbc], writes=[t1])
        self.rstd_from_ss(ss, rs, 1.0 / D, ln)
        return ss, rs

    def stageO_full(self, j, ogT, x_in, x_out, psY):
        for tt in range(4):
            self.stageO_tile(j, tt, ogT, x_in, x_out, psY)

    def stageO_tile(self, j, tt, ogT, x_in, x_out, psY, og_dep=None):
        sc = self.sc
        og_dep = ogT if og_dep is None else og_dep
        if True:
            tok = j * ST + tt * 128
            k = (j * 4 + tt) % 2
            py = psY[k]
            xi = tt % len(self.xo)
            xo = self.xo[xi]
            t1 = self.t1[k % len(self.t1)]
            ss, ln, rs = self.ssO[k], self.lnO[k], self.rsO[k]
            sc.dma("sp", self.slots["xo%d" % xi], lambda e, xo=xo, tok=tok: e.dma_start(out=xo.ap, in_=x_in[tok:tok + 128, :]),
                   writes=[xo])
            fns = []
            for nb in range(2):
                for c in range(KC):
                    fns.append(lambda e, nb=nb, c=c, py=py, tt=tt: e.matmul(
                        py[:, nb * 512:(nb + 1) * 512], lhsT=ogT[:, c, tt * 128:(tt + 1) * 128],
                        rhs=self.Wout[:, c, nb * 512:(nb + 1) * 512], start=(c == 0), stop=(c == KC - 1)))
            self.ck(12)
            sc.pe_group(fns, reads=[og_dep, self.Wout], writes=[py])
            self.ck(13)
            for nb in range(2):
                sc.op("act", lambda e, py=py, ss=ss, nb=nb: e.activation(out=self.junk[:, nb * 512:(nb + 1) * 512], in_=py[:, nb * 512:(nb + 1) * 512],
                                                                         func=AF.Square, accum_out=ss[:, 1 + nb:2 + nb]),
                      reads=[py], writes=[self.junk, ss])
                sc.op("dve", lambda e, py=py, t1=t1, nb=nb: e.tensor_tensor(out=t1[:, nb * 512:(nb + 1) * 512], in0=py[:, nb * 512:(nb + 1) * 512],
                                                                            in1=self.Gbc[:, nb * 512:(nb + 1) * 512], op=ALU.mult),
                      reads=[py, self.Gbc], writes=[t1])
            self.ck(14)
            sc.op("dve", lambda e, ss=ss: e.tensor_tensor(out=ss[:, 0:1], in0=ss[:, 1:2], in1=ss[:, 2:3], op=ALU.add),
                  reads=[ss], writes=[ss])
            sc.op("act", lambda e, ss=ss, ln=ln: e.activation(out=ln[:, 0:1], in_=ss[:, 0:1], func=AF.Ln, scale=1.0 / D, bias=EPS),
                  reads=[ss], writes=[ln])
            sc.op("act", lambda e, rs=rs, ln=ln: e.activation(out=rs[:, 0:1], in_=ln[:, 0:1], func=AF.Exp, scale=-0.5),
                  reads=[ln], writes=[rs])
            self.ck(15)
            sc.op("dve", lambda e, t1=t1, xo=xo, rs=rs: e.scalar_tensor_tensor(out=xo.ap, in0=t1.ap, scalar=rs[:, 0:1], in1=xo.ap,
                                                                               op0=ALU.mult, op1=ALU.add),
                  reads=[t1, xo, rs], writes=[xo])
            self.ck(16)
            sc.dma(STQ, self.slots["xs%d" % xi], lambda e, xo=xo, tok=tok: e.dma_start(out=x_out[tok:tok + 128, :], in_=xo.ap),
                   reads=[xo])

    def layer(self, L, x_in, x_out):
        kind = L % 3
        if kind == 1:
            self.layer_sgu(L, x_in, x_out)
        elif kind == 0:
            self.layer_gla(L, x_in, x_out)
        else:
            self.layer_fox(L, x_in, x_out)


    def layer_gla(self, L, x_in, x_out):
        sc, A, dr = self.sc, self.A, self.dram
        p = "L%d_" % L
        ghc = A.alloc([8])
        sl = self.slots["c0"]
        sc.dma("sp", sl, lambda e: e.dma_start(out=ghc.ap, in_=dr[p + "ghc"]), writes=[ghc])
        sc.barrier()
        self.prep(L, [(512, 2048)], nocol_ranges=[(1024, 2048)], wout_rowscale=ghc)
        self.alloc_stageA(nxt=2, nxnT=2)
        self.alloc_stageO(nxo=2, nt1=1)
        wa2 = A.alloc([512])
        sc.dma("sp", sl, lambda e: e.dma_start(out=wa2[0:17, :], in_=dr[p + "wa2"]), writes=[wa2])
        sc.barrier()
        alT = A.alloc([ST])
        sc.op("dve", lambda e: e.memset(alT[0:17, :], 1.0), writes=[alT])
        f512 = A.alloc([512])
        spt = [A.alloc([512]) for _ in range(2)]
        enb = [A.alloc([512]) for _ in range(1)]
        ebT = A.alloc([4, ST])
        enbT = A.alloc([4, ST])
        elast = [A.alloc([4]) for _ in range(4)]
        qT = A.alloc([4, ST], BF16)
        kT = A.alloc([4, ST], BF16)
        ktok = [A.alloc([512], BF16) for _ in range(4)]
        vtok = [A.alloc([D], BF16) for _ in range(4)]
        szT = A.alloc([KC, ST], BF16)
        ogT = A.alloc([KC, ST], BF16)
        ogp = [T(ogT.ap, Buf()) for _ in range(4)]
        ATs = [A.alloc([4, 128], BF16) for _ in range(2)]
        osq = A.alloc([8, 128], BF16)
        rstd = A.alloc([4, 128])
        otmp = A.alloc([8, 128], BF16)
        Sst = A.alloc([4, 256])
        Sbf = A.alloc([4, 256], BF16)
        sc.op("dve", lambda e: e.memset(Sst.ap, 0.0), writes=[Sst])
        sc.op("dve", lambda e: e.memset(Sbf.ap, 0.0), writes=[Sbf])
        psW = [T(self.pst(0, 2), self.psbuf[0]), T(self.pst(2, 2), self.psbuf[2])]
        psA = [T(self.pst(4 + i), self.psbuf[4 + i]) for i in range(3)]
        brow = self.biasrow[(512, 2048)]
        LNS = -0.5 * float(np.log(128.0))
        tri4 = self.trif.ap.unsqueeze(1).to_broadcast([128, 4, 128])
        ia = [0]
        iw = [0]

        def nextA():
            t = psA[ia[0] % 3]
            ia[0] += 1
            return t

        def nextW():
            t = psW[iw[0] % 2]
            iw[0] += 1
            return t

        xnT_next = self.stageA(0, x_in)
        for j in range(self.nST):
            xnT = xnT_next
            for c in range(KC):
                pa = nextA()
                fns = [lambda e, kc=kc: e.matmul(pa.ap, lhsT=self.Win[:, kc, 2048 + c * 128:2048 + (c + 1) * 128],
                                                 rhs=xnT[:, kc, :], start=(kc == 0), stop=(kc == KC - 1)) for kc in range(KC)]
                sc.pe_group(fns, reads=[xnT, self.Win], writes=[pa])
                sc.op("act", lambda e: e.activation(out=szT[:, c, :], in_=pa.ap, func=AF.Silu, bias=self.biascol[:, 16 + c:17 + c]),
                      reads=[pa, self.biascol], writes=[szT])
            pa = nextA()
            fns = [lambda e, kc=kc: e.matmul(pa[0:16, :], lhsT=self.Win[:, kc, 3072:3088], rhs=xnT[:, kc, :],
                                             start=(kc == 0), stop=(kc == KC - 1)) for kc in range(KC)]
            sc.pe_group(fns, reads=[xnT, self.Win], writes=[pa])
            sc.op("act", lambda e: e.activation(out=alT[0:16, :], in_=pa[0:16, :], func=AF.Identity, bias=self.biascol[0:16, 24:25]),
                  reads=[pa, self.biascol], writes=[alT])
            def emit_xa(tt):
                ts_ = slice(tt * 128, (tt + 1) * 128)
                pa = nextA()
                sc.pe_group([lambda e: e.matmul(pa.ap, lhsT=alT[0:17, ts_], rhs=wa2[0:17, :], start=True, stop=True)],
                            reads=[alT, wa2], writes=[pa])
                sp_ = spt[tt % 2]
                sc.op("act", lambda e: e.activation(out=f512.ap, in_=pa.ap, func=AF.Exp, scale=-1.0), reads=[pa], writes=[f512])
                sc.op("act", lambda e: e.activation(out=sp_.ap, in_=f512.ap, func=AF.Ln, bias=1.0), reads=[f512], writes=[sp_])

            emit_xa(0)
            emit_xa(1)
            for tt in range(4):
                ts_ = slice(tt * 128, (tt + 1) * 128)
                sp_ = spt[tt % 2]
                pb = nextA()
                sc.pe_group([lambda e: e.matmul(pb.ap, lhsT=self.unegf.ap, rhs=sp_.ap, start=True, stop=True)],
                            reads=[self.unegf, sp_], writes=[pb])
                en_ = enb[0]
                sc.op("act", lambda e: e.activation(out=en_.ap, in_=pb.ap, func=AF.Exp, scale=-1.0), reads=[pb], writes=[en_])
                pc = nextA()
                fns = [lambda e, h=h: e.matmul(pc[:, h * 128:(h + 1) * 128], lhsT=sp_[:, h * 128:(h + 1) * 128],
                                               rhs=self.unegf.ap, start=True, stop=True) for h in range(4)]
                sc.pe_group(fns, reads=[self.unegf, sp_], writes=[pc])
                pc3 = pc.ap.rearrange("p (h t) -> p h t", h=4)
                sc.op("act", lambda e: e.activation(out=ebT[:, :, ts_], in_=pc3, func=AF.Exp, bias=LNS), reads=[pc], writes=[ebT])
                sc.op("act", lambda e: e.activation(out=enbT[:, :, ts_], in_=pc3, func=AF.Exp, scale=-1.0), reads=[pc], writes=[enbT])
                el = elast[tt]
                sc.op("act", lambda e: e.activation(out=el.ap, in_=pc3[:, :, 127], func=AF.Exp), reads=[pc], writes=[el])
                if tt + 2 < 4:
                    emit_xa(tt + 2)
                pk = nextA()
                fns = [lambda e, kc=kc: e.matmul(pk.ap, lhsT=xnT[:, kc, ts_], rhs=self.Win[:, kc, 512:1024],
                                                 start=(kc == 0), stop=(kc == KC - 1)) for kc in range(KC)]
                sc.pe_group(fns, reads=[xnT, self.Win], writes=[pk])
                sc.op("dve", lambda e: e.tensor_tensor(out=f512.ap, in0=pk.ap, in1=brow[:, 0:512], op=ALU.add),
                      reads=[pk, brow], writes=[f512])
                sc.op("pool", lambda e: e.tensor_tensor(out=ktok[tt].ap, in0=f512.ap, in1=en_.ap, op=ALU.mult),
                      reads=[f512, en_], writes=[ktok[tt]])
                pv = nextW()
                fns = []
                for nb in range(2):
                    for kc in range(KC):
                        fns.append(lambda e, nb=nb, kc=kc: e.matmul(
                            pv[:, nb * 512:(nb + 1) * 512], lhsT=xnT[:, kc, ts_],
                            rhs=self.Win[:, kc, 1024 + nb * 512:1024 + (nb + 1) * 512], start=(kc == 0), stop=(kc == KC - 1)))
                sc.pe_group(fns, reads=[xnT, self.Win], writes=[pv])
                sc.op("dve", lambda e: e.tensor_tensor(out=vtok[tt].ap, in0=pv.ap, in1=brow[:, 512:1536], op=ALU.add),
                      reads=[pv, brow], writes=[vtok[tt]])
            for h in range(4):
                pa = nextA()
                fns = [lambda e, kc=kc: e.matmul(pa.ap, lhsT=self.Win[:, kc, h * 128:(h + 1) * 128], rhs=xnT[:, kc, :],
                                                 start=(kc == 0), stop=(kc == KC - 1)) for kc in range(KC)]
                sc.pe_group(fns, reads=[xnT, self.Win], writes=[pa])
                sc.op("dve", lambda e: e.scalar_tensor_tensor(out=qT[:, h, :], in0=pa.ap, scalar=self.biascol[:, h:h + 1],
                                                              in1=ebT[:, h, :], op0=ALU.add, op1=ALU.mult),
                      reads=[pa, self.biascol, ebT], writes=[qT])
                pa2 = nextA()
                fns = [lambda e, kc=kc: e.matmul(pa2.ap, lhsT=self.Win[:, kc, 512 + h * 128:512 + (h + 1) * 128],
                                                 rhs=xnT[:, kc, :], start=(kc == 0), stop=(kc == KC - 1)) for kc in range(KC)]
                sc.pe_group(fns, reads=[xnT, self.Win], writes=[pa2])
                sc.op("dve", lambda e: e.scalar_tensor_tensor(out=kT[:, h, :], in0=pa2.ap, scalar=self.biascol[:, 4 + h:5 + h],
                                                              in1=enbT[:, h, :], op0=ALU.add, op1=ALU.mult),
                      reads=[pa2, self.biascol, enbT], writes=[kT])
            if j + 1 < self.nST:
                xnT_next = self.stageA(j + 1, x_in)

            def emit_AT(tt):
                ts_ = slice(tt * 128, (tt + 1) * 128)
                pa = nextA()
                fns = [lambda e, h=h: e.matmul(pa[:, h * 128:(h + 1) * 128], lhsT=kT[:, h, ts_], rhs=qT[:, h, ts_],
                                               start=True, stop=True) for h in range(4)]
                sc.pe_group(fns, reads=[kT, qT], writes=[pa])
                at = ATs[tt % 2]
                sc.op("dve", lambda e: e.tensor_tensor(out=at.ap, in0=pa.ap.rearrange("p (h t) -> p h t", h=4), in1=tri4, op=ALU.mult),
                      reads=[pa, self.trif], writes=[at])

            emit_AT(0)
            for tt in range(4):
                ts_ = slice(tt * 128, (tt + 1) * 128)
                if tt + 1 < 4:
                    emit_AT(tt + 1)
                at = ATs[tt % 2]
                po = psW[0]
                fns = []
                for h in range(4):
                    for half in range(2):
                        c = 2 * h + half
                        fns.append(lambda e, h=h, c=c: e.matmul(po[:, c * 128:(c + 1) * 128], lhsT=vtok[tt][:, c * 128:(c + 1) * 128],
                                                                rhs=at[:, h, :], start=True, stop=False))
                        fns.append(lambda e, h=h, c=c, half=half: e.matmul(po[:, c * 128:(c + 1) * 128],
                                                                           lhsT=Sbf[:, h, half * 128:(half + 1) * 128],
                                                                           rhs=qT[:, h, ts_], start=False, stop=True))
                sc.pe_group(fns, reads=[vtok[tt], at, Sbf, qT], writes=[po])
                for nb in range(2):
                    sc.op("act", lambda e, nb=nb: e.activation(
                        out=osq[:, nb * 4:(nb + 1) * 4, :], in_=po[:, nb * 512:(nb + 1) * 512].rearrange("p (c t) -> p c t", c=4),
                        func=AF.Square), reads=[po], writes=[osq])
                el = elast[tt]
                for hp in range(2):
                    pP = nextA()
                    fns = [lambda e, hh=hh: e.matmul(pP[:, hh * 256:(hh + 1) * 256],
                                                     lhsT=ktok[tt][:, (2 * hp + hh) * 128:(2 * hp + hh + 1) * 128],
                                                     rhs=vtok[tt][:, (2 * hp + hh) * 256:(2 * hp + hh + 1) * 256], start=True, stop=True)
                           for hh in range(2)]
                    sc.pe_group(fns, reads=[ktok[tt], vtok[tt]], writes=[pP])
                    Sh = Sst[:, 2 * hp:2 * hp + 2, :]
                    sc.op("dve", lambda e: e.tensor_tensor(out=Sh, in0=pP.ap.rearrange("p (h v) -> p h v", h=2), in1=Sh, op=ALU.add),
                          reads=[pP, Sst], writes=[Sst])
                elb = el.ap.unsqueeze(2).to_broadcast([128, 4, 256])
                sc.op("pool", lambda e: e.tensor_tensor(out=Sst.ap, in0=Sst.ap, in1=elb, op=ALU.mult), reads=[Sst, el], writes=[Sst])
                sc.op("pool", lambda e: e.tensor_copy(out=Sbf.ap, in_=Sst.ap), reads=[Sst], writes=[Sbf])
                if tt > 0:
                    self.stageO_tile(j, tt - 1, ogT, x_in, x_out, [psW[1], psW[1]], og_dep=ogp[tt - 1])
                ps_ = nextA()
                fns = []
                for h in range(4):
                    for half in range(2):
                        fns.append(lambda e, h=h, half=half: e.matmul(ps_[:, h * 128:(h + 1) * 128], lhsT=self.onesb.ap,
                                                                      rhs=osq[:, 2 * h + half, :], start=(half == 0), stop=(half == 1)))
                sc.pe_group(fns, reads=[self.onesb, osq], writes=[ps_])
                sc.op("act", lambda e: e.activation(out=rstd.ap, in_=ps_.ap.rearrange("p (h t) -> p h t", h=4),
                                                    func=AF.Ln, scale=1.0 / 256, bias=EPS), reads=[ps_], writes=[rstd])
                sc.op("act", lambda e: e.activation(out=rstd.ap, in_=rstd.ap, func=AF.Exp, scale=-0.5), reads=[rstd], writes=[rstd])
                for nb in range(2):
                    sc.op("dve", lambda e, nb=nb: e.tensor_tensor(
                        out=otmp[:, nb * 4:(nb + 1) * 4, :].rearrange("p (h f) t -> p h f t", h=2),
                        in0=po[:, nb * 512:(nb + 1) * 512].rearrange("p (h f t) -> p h f t", h=2, f=2),
                        in1=rstd[:, 2 * nb:2 * nb + 2, :].unsqueeze(2).to_broadcast([128, 2, 2, 128]), op=ALU.mult),
                        reads=[po, rstd], writes=[otmp])
                sc.op("pool", lambda e: e.tensor_tensor(out=ogT[:, :, ts_], in0=otmp.ap, in1=szT[:, :, ts_], op=ALU.mult),
                      reads=[otmp, szT], writes=[ogp[tt]])
            self.stageO_tile(j, 3, ogT, x_in, x_out, [psW[1], psW[1]], og_dep=ogp[3])
        print("GLA arena words used:", A.off)

    def layer_fox(self, L, x_in, x_out):
        sc, A, dr, S = self.sc, self.A, self.dram, self.S
        p = "L%d_" % L
        QA, KA, VS, SZ, OG = self.QA, self.KA, self.VS, self.SZ, self.OG
        self.prep(L, [(2048, 3072)])
        sl = self.slots["c0"]
        gqk = A.alloc([2])
        bfc = A.alloc([1])
        negm_f = A.alloc([128])
        negm = A.alloc([128], BF16)
        bo_f = A.alloc([128])
        bones = A.alloc([128], BF16)
        self_sel = A.alloc([64])
        ones3 = A.alloc([3, ST], BF16)
        sc.dma("sp", sl, lambda e: e.dma_start(out=gqk.ap, in_=dr[p + "gqk"]), writes=[gqk])
        sc.dma("sp", sl, lambda e: e.dma_start(out=bfc[0:16, :], in_=dr[p + "bf"]), writes=[bfc])
        sc.dma("sp", sl, lambda e: e.dma_start(out=negm_f.ap, in_=dr["negmask"]), writes=[negm_f])
        sc.dma("sp", sl, lambda e: e.dma_start(out=bo_f.ap, in_=dr["blockones"]), writes=[bo_f])
        sc.dma("sp", sl, lambda e: e.dma_start(out=self_sel[0:65, :], in_=dr["sel"]), writes=[self_sel])
        sc.barrier()
        sc.op("dve", lambda e: e.tensor_copy(out=negm.ap, in_=negm_f.ap), reads=[negm_f], writes=[negm])
        sc.op("dve", lambda e: e.tensor_copy(out=bones.ap, in_=bo_f.ap), reads=[bo_f], writes=[bones])
        sc.op("dve", lambda e: e.memset(ones3.ap, 1.0), writes=[ones3])
        sc.op("dve", lambda e: e.tensor_scalar(out=gqk[:, 0:1], in0=gqk[:, 0:1], scalar1=0.125, scalar2=None, op0=ALU.mult),
              reads=[gqk], writes=[gqk])
        nfb = A.alloc([1])
        sc.op("dve", lambda e: e.tensor_tensor(out=nfb[0:16, :], in0=self.biascol[0:16, 32:33], in1=bfc[0:16, :], op=ALU.add),
              reads=[self.biascol, bfc], writes=[nfb])
        sc.op("dve", lambda e: e.tensor_scalar(out=nfb[0:16, :], in0=nfb[0:16, :], scalar1=-1.0, scalar2=None, op0=ALU.mult),
              reads=[nfb], writes=[nfb])
        ph_mark = A.mark()
        self.alloc_stageA(nxt=2, nxnT=2)
        sq = [A.alloc([ST], BF16) for _ in range(2)]
        rstd = [A.alloc([ST]) for _ in range(2)]
        tmpf = [A.alloc([ST]) for _ in range(2)]
        qn = [A.alloc([ST], BF16) for _ in range(2)]
        vtok = [A.alloc([D], BF16) for _ in range(2)]
        szT = A.alloc([KC, ST], BF16)
        Ff = [A.alloc([ST]) for _ in range(2)]
        fe = A.alloc([ST])
        fsp = A.alloc([ST])
        fr = A.alloc([ST])
        pcs = [A.alloc([ST], BF16) for _ in range(6)]
        sc.op("dve", lambda e: e.memset(fr.ap, 1.0), writes=[fr])
        psW = [T(self.pst(0, 2), self.psbuf[0]), T(self.pst(0, 2), self.psbuf[0])]
        psA = [T(self.pst(2 + i), self.psbuf[2 + i]) for i in range(5)]
        brow = self.biasrow[(2048, 3072)]
        ia = [0]

        def nextA():
            t = psA[ia[0] % 5]
            ia[0] += 1
            return t

        xnT_next = self.stageA(0, x_in)
        for j in range(self.nST):
            tok0 = j * ST
            xnT = xnT_next
            chunks = [(which, c) for which in range(2) for c in range(KC)]
            pas = {}

            def mmA(i):
                which, c = chunks[i]
                col0 = which * 1024 + c * 128
                pa = nextA()
                pas[i] = pa
                fns = [lambda e, kc=kc: e.matmul(pa.ap, lhsT=self.Win[:, kc, col0:col0 + 128], rhs=xnT[:, kc, :],
                                                 start=(kc == 0), stop=(kc == KC - 1)) for kc in range(KC)]
                sc.pe_group(fns, reads=[xnT, self.Win], writes=[pa])

            mmA(0)
            for i, (which, c) in enumerate(chunks):
                if i + 1 < len(chunks):
                    mmA(i + 1)
                DST = QA if which == 0 else KA
                bcol = self.biascol[:, which * 8 + c:which * 8 + c + 1]
                pa = pas.pop(i)
                sq_, rs_, tf_, q_ = sq[i % 2], rstd[i % 2], tmpf[i % 2], qn[i % 2]
                slq = self.slots["fq%d" % (i % 2)]
                sc.op("act", lambda e: e.activation(out=sq_.ap, in_=pa.ap, func=AF.Square, bias=bcol),
                      reads=[pa, self.biascol], writes=[sq_])
                pb = nextA()
                sc.pe_group([lambda e: e.matmul(pb.ap, lhsT=bones.ap, rhs=sq_.ap, start=True, stop=True)],
                            reads=[bones, sq_], writes=[pb])
                sc.op("act", lambda e: e.activation(out=rs_.ap, in_=pb.ap, func=AF.Ln, scale=1.0 / 64, bias=EPS),
                      reads=[pb], writes=[rs_])
                sc.op("act", lambda e: e.activation(out=rs_.ap, in_=rs_.ap, func=AF.Exp, scale=-0.5),
                      reads=[rs_], writes=[rs_])
                sc.op("dve", lambda e: e.scalar_tensor_tensor(out=tf_.ap, in0=pa.ap, scalar=bcol, in1=rs_.ap,
                                                              op0=ALU.add, op1=ALU.mult),
                      reads=[pa, self.biascol, rs_], writes=[tf_])
                sc.op("pool", lambda e: e.tensor_scalar(out=q_.ap, in0=tf_.ap, scalar1=gqk[:, which:which + 1],
                                                        scalar2=1.0, op0=ALU.mult, op1=ALU.mult),
                      reads=[tf_, gqk], writes=[q_])
                for hh in range(2):
                    sc.dma(STQ, slq, lambda e, hh=hh: e.dma_start(
                        out=DST[2 * c + hh, 0:64, tok0:tok0 + ST], in_=q_[hh * 64:(hh + 1) * 64, :]), reads=[q_])
            if j + 1 < self.nST:
                xnT_next = self.stageA(j + 1, x_in)
            for tt in range(4):
                ts_ = slice(tt * 128, (tt + 1) * 128)
                pv = psW[tt % 2]
                fns = []
                for nb in range(2):
                    for kc in range(KC):
                        fns.append(lambda e, nb=nb, kc=kc, pv=pv, ts_=ts_: e.matmul(
                            pv[:, nb * 512:(nb + 1) * 512], lhsT=xnT[:, kc, ts_],
                            rhs=self.Win[:, kc, 2048 + nb * 512:2048 + (nb + 1) * 512], start=(kc == 0), stop=(kc == KC - 1)))
                sc.pe_group(fns, reads=[xnT, self.Win], writes=[pv])
                v_ = vtok[tt % 2]
                sc.op("dve", lambda e, pv=pv, v_=v_: e.tensor_tensor(out=v_.ap, in0=pv.ap, in1=brow.ap, op=ALU.add),
                      reads=[pv, brow], writes=[v_])
                sc.dma(STQ, self.slots["fv%d" % (tt % 2)], lambda e, v_=v_, tt=tt, tok0=tok0: e.dma_start(
                    out=VS[tok0 + tt * 128:tok0 + (tt + 1) * 128, :], in_=v_.ap), reads=[v_])
            for c in range(KC):
                pa = nextA()
                fns = [lambda e, kc=kc, c=c, pa=pa: e.matmul(pa.ap, lhsT=self.Win[:, kc, 3072 + c * 128:3072 + (c + 1) * 128],
                                                             rhs=xnT[:, kc, :], start=(kc == 0), stop=(kc == KC - 1))
                       for kc in range(KC)]
                sc.pe_group(fns, reads=[xnT, self.Win], writes=[pa])
                sc.op("act", lambda e, c=c, pa=pa: e.activation(out=szT[:, c, :], in_=pa.ap, func=AF.Silu,
                                                                bias=self.biascol[:, 24 + c:25 + c]),
                      reads=[pa, self.biascol], writes=[szT])
            sc.dma(STQ, self.slots["fz0"], lambda e, tok0=tok0: e.dma_start(
                out=SZ[:, tok0:tok0 + ST].rearrange("(c p) t -> p c t", p=128), in_=szT.ap), reads=[szT])
            pa = nextA()
            fns = [lambda e, kc=kc, pa=pa: e.matmul(pa[0:16, :], lhsT=self.Win[:, kc, 4096:4112], rhs=xnT[:, kc, :],
                                                    start=(kc == 0), stop=(kc == KC - 1)) for kc in range(KC)]
            sc.pe_group(fns, reads=[xnT, self.Win], writes=[pa])
            sc.op("act", lambda e, pa=pa: e.activation(out=fe[0:16, :], in_=pa[0:16, :], func=AF.Exp, scale=-1.0, bias=nfb[0:16, :]),
                  reads=[pa, nfb], writes=[fe])
            sc.op("act", lambda e: e.activation(out=fsp[0:16, :], in_=fe[0:16, :], func=AF.Ln, bias=1.0),
                  reads=[fe], writes=[fsp])
            F_ = Ff[j % 2]
            Fp = Ff[(j + 1) % 2]
            init = 0.0 if j == 0 else Fp[0:16, ST - 1:ST]
            sc.op("dve", lambda e, F_=F_, init=init: e.tensor_tensor_scan(out=F_[0:16, :], data0=fr[0:16, :],
                                                                          data1=fsp[0:16, :], initial=init, op0=ALU.mult, op1=ALU.subtract),
                  reads=[fsp, fr] + ([Fp] if j > 0 else []), writes=[F_])
            sc.op("dve", lambda e, F_=F_: e.tensor_copy(out=pcs[0][0:16, :], in_=F_[0:16, :]), reads=[F_], writes=[pcs[0]])
            sc.op("dve", lambda e, F_=F_: e.tensor_tensor(out=fe[0:16, :], in0=F_[0:16, :], in1=pcs[0][0:16, :], op=ALU.subtract),
                  reads=[F_, pcs[0]], writes=[fe])
            sc.op("dve", lambda e: e.tensor_copy(out=pcs[1][0:16, :], in_=fe[0:16, :]), reads=[fe], writes=[pcs[1]])
            sc.op("dve", lambda e: e.tensor_tensor(out=fe[0:16, :], in0=fe[0:16, :], in1=pcs[1][0:16, :], op=ALU.subtract),
                  reads=[fe, pcs[1]], writes=[fe])
            sc.op("dve", lambda e: e.tensor_copy(out=pcs[2][0:16, :], in_=fe[0:16, :]), reads=[fe], writes=[pcs[2]])
            for i in range(3):
                sc.op("pool", lambda e, i=i: e.tensor_scalar(out=pcs[3 + i][0:16, :], in0=pcs[i][0:16, :], scalar1=-1.0, scalar2=1.0,
                                                             op0=ALU.mult, op1=ALU.mult), reads=[pcs[i]], writes=[pcs[3 + i]])
            for i in range(3):
                sc.dma(STQ, self.slots["ff0"], lambda e, i=i, tok0=tok0: e.dma_start(out=QA[:, 64 + i, tok0:tok0 + ST], in_=pcs[i][0:16, :]),
                       reads=[pcs[i]])
                sc.dma(STQ, self.slots["ff0"], lambda e, i=i, tok0=tok0: e.dma_start(out=KA[:, 67 + i, tok0:tok0 + ST], in_=pcs[3 + i][0:16, :]),
                       reads=[pcs[3 + i]])
            sc.dma(STQ, self.slots["ff0"], lambda e, tok0=tok0: e.dma_start(out=QA[:, 67:70, tok0:tok0 + ST], in_=ones3[0:16, :, :]),
                   reads=[ones3])
            sc.dma(STQ, self.slots["ff0"], lambda e, tok0=tok0: e.dma_start(out=KA[:, 64:67, tok0:tok0 + ST], in_=ones3[0:16, :, :]),
                   reads=[ones3])
        sc.barrier()
        self.ck(20)
        A.reset(ph_mark)
        NKT = S // 128
        NQB = S // ST
        KAh = [A.alloc([S], BF16) for _ in range(2)]
        Vh = [A.alloc([NKT, 128], BF16) for _ in range(2)]
        QAq = [A.alloc([ST], BF16) for _ in range(3)]
        szq = [A.alloc([ST], BF16) for _ in range(2)]
        PT = [A.alloc([2, ST], BF16) for _ in range(4)]
        rcs = [A.alloc([ST]) for _ in range(2)]
        tn = A.alloc([ST])
        ogq = [A.alloc([ST], BF16) for _ in range(2)]
        for v_ in Vh:
            sc.op("dve", lambda e, v_=v_: e.memset(v_.ap, 1.0), writes=[v_])
        psS = [T(self.ps[:, 2 * i:2 * i + 2, :], self.psbuf[2 * i]) for i in range(3)]
        psO = [T(self.pst(6), self.psbuf[6]), T(self.pst(7), self.psbuf[7])]
        items = []
        nqb_total = 0
        for h in range(16):
            for qb in range(NQB):
                nk = 4 * qb + 4
                groups = [[kt, kt + 1] for kt in range(0, 4 * qb, 2)] + [[kt] for kt in range(4 * qb, nk)]
                for gi, g in enumerate(groups):
                    items.append(dict(h=h, qb=qb, g=g, nk=nk, first=(gi == 0), last=(gi == len(groups) - 1),
                                      iq=nqb_total, idx=len(items)))
                nqb_total += 1

        def load_head(h):
            ka = KAh[h % 2]
            vh = Vh[h % 2]
            sc.dma("sp", self.slots["fk%d" % (h % 2)], lambda e: e.dma_start(out=ka[0:70, :], in_=KA[h, :, :]), writes=[ka])
            sc.dma("sp", self.slots["fvh%d" % (h % 2)], lambda e: e.dma_start(
                out=vh[:, :, 0:64], in_=VS[:, h * 64:(h + 1) * 64].rearrange("(kt p) d -> p kt d", p=128)), writes=[vh])

        def emit_qk(it):
            h, qb, g, iq = it["h"], it["qb"], it["g"], it["iq"]
            ka = KAh[h % 2]
            qa = QAq[iq % 3]
            if it["first"]:
                q0t = qb * ST
                zq = szq[iq % 2]
                sc.dma("sp", self.slots["fqq%d" % (iq % 3)], lambda e: e.dma_start(out=qa[0:70, :], in_=QA[h, :, q0t:q0t + ST]),
                       writes=[qa])
                sc.dma("sp", self.slots["fsz%d" % (iq % 2)], lambda e: e.dma_start(out=zq[0:64, :], in_=SZ[h * 64:(h + 1) * 64, q0t:q0t + ST]),
                       writes=[zq])
            pS = psS[it["idx"] % 3]
            fns = []
            for i, kt in enumerate(g):
                r = kt - 4 * qb
                q0 = max(r, 0) * 128
                fns.append(lambda e, i=i, kt=kt, q0=q0, r=r: e.matmul(
                    pS[:, i, q0:ST], lhsT=ka[0:70, kt * 128:(kt + 1) * 128], rhs=qa[0:70, q0:ST], start=True, stop=(r < 0)))
                if r >= 0:
                    fns.append(lambda e, i=i, q0=q0: e.matmul(pS[:, i, q0:q0 + 128], lhsT=self.identb.ap, rhs=negm.ap,
                                                              start=False, stop=True))
            sc.pe_group(fns, reads=[ka, qa, self.identb, negm], writes=[pS])

        def emit_act(it):
            g, qb = it["g"], it["qb"]
            pS = psS[it["idx"] % 3]
            pt = PT[it["idx"] % 4]
            if len(g) == 2:
                sc.op("act", lambda e: e.activation(out=pt.ap, in_=pS.ap, func=AF.Exp), reads=[pS], writes=[pt])
            else:
                q0 = max(g[0] - 4 * qb, 0) * 128
                sc.op("act", lambda e: e.activation(out=pt[:, 0, q0:ST], in_=pS[:, 0, q0:ST], func=AF.Exp),
                      reads=[pS], writes=[pt])

        def emit_pv(it):
            h, qb, g, nk, iq = it["h"], it["qb"], it["g"], it["nk"], it["iq"]
            vh = Vh[h % 2]
            pt = PT[it["idx"] % 4]
            po = psO[iq % 2]
            fns = []
            for i, kt in enumerate(g):
                q0 = max(kt - 4 * qb, 0) * 128
                fns.append(lambda e, i=i, kt=kt, q0=q0: e.matmul(
                    po[:, q0:ST], lhsT=vh[:, kt, :], rhs=pt[:, i, q0:ST], start=(kt == 0), stop=(kt == nk - 1)))
            sc.pe_group(fns, reads=[vh, pt], writes=[po])

        def fin_a(it):
            h, qb, iq = it["h"], it["qb"], it["iq"]
            rc, po, og_, zq = rcs[iq % 2], psO[iq % 2], ogq[iq % 2], szq[iq % 2]
            q0t = qb * ST
            sc.op("dve", lambda e: e.reciprocal(out=rc[64:128, :], in_=po[64:128, :]), reads=[po], writes=[rc])
            sc.op("dve", lambda e: e.tensor_tensor(out=tn[0:64, :], in0=po[0:64, :], in1=rc[64:128, :], op=ALU.mult),
                  reads=[po, rc], writes=[tn])
            sc.op("pool", lambda e: e.tensor_tensor(out=og_[0:64, :], in0=tn[0:64, :], in1=zq[0:64, :], op=ALU.mult),
                  reads=[tn, zq], writes=[og_])
            sc.dma(STQ, self.slots["fog%d" % (iq % 2)], lambda e: e.dma_start(
                out=OG[h * 64:(h + 1) * 64, q0t:q0t + ST], in_=og_[0:64, :]), reads=[og_])

        def fin_b(it):
            pass

        load_head(0)
        emit_qk(items[0])
        emit_qk(items[1])
        pending = []
        for i, it in enumerate(items):
            if i + 2 < len(items):
                emit_qk(items[i + 2])
            emit_act(it)
            emit_pv(it)
            if it["first"] and it["qb"] == 0 and it["h"] + 1 < 16:
                load_head(it["h"] + 1)
            for pnd in pending:
                pnd[0] -= 1
            while pending and pending[0][0] <= 0:
                fin_b(pending.pop(0)[1])
            if it["last"]:
                fin_a(it)
                pending.append([2, it])
        for pnd in pending:
            fin_b(pnd[1])
        sc.barrier()
        self.ck(21)
        A.reset(ph_mark)
        self.alloc_stageA(nxt=1, nxnT=1)
        self.alloc_stageO()
        ogT = [A.alloc([KC, ST], BF16) for _ in range(2)]
        psW = [T(self.pst(0, 2), self.psbuf[0]), T(self.pst(2, 2), self.psbuf[2])]
        for j in range(self.nST):
            tok0 = j * ST
            og = ogT[j % 2]
            sc.dma("sp", self.slots["fo%d" % (j % 2)], lambda e, og=og, tok0=tok0: e.dma_start(
                out=og.ap, in_=OG[:, tok0:tok0 + ST].rearrange("(c p) t -> p c t", p=128)), writes=[og])
            self.stageO_full(j, og, x_in, x_out, psW)

    def layer_sgu(self, L, x_in, x_out):
        sc, A, dr = self.sc, self.A, self.dram
        p = "L%d_" % L
        self.prep(L, [(1024, 2048)])
        sc.barrier()
        self.ck(4)
        self.alloc_stageA()
        self.alloc_stageO()
        lng = A.alloc([D])
        lnb = A.alloc([D])
        bsb = A.alloc([4, 128])
        wsf = A.alloc([4, 128])
        WcT = A.alloc([4, 128], BF16)
        sl = self.slots["c0"]
        sc.dma("sp", sl, lambda e: e.dma_start(out=lng.ap, in_=dr[p + "lng"]), writes=[lng])
        sc.dma("sp", sl, lambda e: e.dma_start(out=lnb.ap, in_=dr[p + "lnb"]), writes=[lnb])
        sc.dma("sp", sl, lambda e: e.dma_start(out=bsb.ap, in_=dr[p + "bs"]), writes=[bsb])
        sc.dma("sp", sl, lambda e: e.dma_start(out=wsf.ap, in_=dr[p + "ws"]), writes=[wsf])
        sc.barrier()
        pw = T(self.pst(0), self.psbuf[0])
        fns = [lambda e, g=g: e.transpose(out=pw[:, g * 128:(g + 1) * 128], in_=wsf[:, g, :], identity=self.identf.ap)
               for g in range(4)]
        sc.pe_group(fns, reads=[wsf, self.identf], writes=[pw])
        for g in range(4):
            sc.op("dve", lambda e, g=g: e.tensor_tensor(out=WcT[:, g, :], in0=pw[:, g * 128:(g + 1) * 128], in1=self.trif.ap,
                                                       op=ALU.mult), reads=[pw, self.trif], writes=[WcT])
        self.dbg("Wout", self.Wout); self.dbg("WcT", WcT)
        self.ck(5)
        gl = [A.alloc([D]) for _ in range(4)]
        vln = [A.alloc([D], BF16) for _ in range(4)]
        uT = A.alloc([KC, ST], BF16)
        szT = A.alloc([KC, ST], BF16)
        ogT = A.alloc([KC, ST], BF16)
        mt = [A.alloc([ST]) for _ in range(2)]
        s1 = A.alloc([4])
        nm = A.alloc([4])
        s2 = A.alloc([4])
        lv = A.alloc([4])
        rv = A.alloc([4])
        psW = [T(self.pst(0, 2), self.psbuf[0]), T(self.pst(2, 2), self.psbuf[2])]
        psA = [T(self.pst(4 + i), self.psbuf[4 + i]) for i in range(3)]
        brow = self.biasrow[(1024, 2048)]
        ia = 0
        for j in range(self.nST):
            xnT = self.stageA(j, x_in)
            self.dbg("xnT", xnT); self.dbg("rsA", self.rsA[j % 2])
            self.ck(6)
            for tt in range(4):
                pv = psW[tt % 2]
                fns = []
                for nb in range(2):
                    for kc in range(KC):
                        fns.append(lambda e, nb=nb, kc=kc, pv=pv, tt=tt: e.matmul(
                            pv[:, nb * 512:(nb + 1) * 512], lhsT=xnT[:, kc, tt * 128:(tt + 1) * 128],
                            rhs=self.Win[:, kc, 1024 + nb * 512:1024 + (nb + 1) * 512], start=(kc == 0), stop=(kc == KC - 1)))
                sc.pe_group(fns, reads=[xnT, self.Win], writes=[pv])
                g_ = gl[tt]
                sc.op("dve", lambda e, pv=pv, g_=g_: e.tensor_tensor(out=g_.ap, in0=pv.ap, in1=brow.ap, op=ALU.add),
                      reads=[pv, brow], writes=[g_])
                sc.op("act", lambda e, g_=g_, tt=tt: e.activation(out=g_.ap, in_=g_.ap, func=AF.Gelu_apprx_tanh,
                                                                  accum_out=s1[:, tt:tt + 1]),
                      reads=[g_], writes=[g_, s1])
            self.ck(7)
            for c in range(KC):
                pa = psA[ia % 3]
                ia += 1
                fns = [lambda e, kc=kc, c=c, pa=pa: e.matmul(pa.ap, lhsT=self.Win[:, kc, c * 128:(c + 1) * 128], rhs=xnT[:, kc, :],
                                                             start=(kc == 0), stop=(kc == KC - 1)) for kc in range(KC)]
                sc.pe_group(fns, reads=[xnT, self.Win], writes=[pa])
                sc.op("act", lambda e, c=c, pa=pa: e.activation(out=uT[:, c, :], in_=pa.ap, func=AF.Gelu_apprx_tanh,
                                                                bias=self.biascol[:, c:c + 1]),
                      reads=[pa, self.biascol], writes=[uT])
            self.dbg("gl0", gl[0]); self.dbg("uT", uT); self.dbg("s1", s1)
            self.ck(8)
            sc.op("dve", lambda e: e.tensor_scalar(out=nm.ap, in0=s1.ap, scalar1=-1.0 / D, scalar2=None, op0=ALU.mult),
                  reads=[s1], writes=[nm])
            for tt in range(4):
                g_ = gl[tt]
                sc.op("act", lambda e, g_=g_, tt=tt: e.activation(out=self.junk.ap, in_=g_.ap, func=AF.Square,
                                                                  bias=nm[:, tt:tt + 1], accum_out=s2[:, tt:tt + 1]),
                      reads=[g_, nm], writes=[self.junk, s2])
            self.rstd_from_ss(s2, rv, 1.0 / D, lv)
            for tt in range(4):
                g_ = gl[tt]
                sc.op("dve", lambda e, g_=g_, tt=tt: e.tensor_scalar(out=g_.ap, in0=g_.ap, scalar1=nm[:, tt:tt + 1],
                                                                    scalar2=rv[:, tt:tt + 1], op0=ALU.add, op1=ALU.mult),
                      reads=[g_, nm, rv], writes=[g_])
                sc.op("dve", lambda e, g_=g_: e.tensor_tensor(out=g_.ap, in0=g_.ap, in1=lng.ap, op=ALU.mult),
                      reads=[g_, lng], writes=[g_])
                sc.op("dve", lambda e, g_=g_, tt=tt: e.tensor_tensor(out=vln[tt].ap, in0=g_.ap, in1=lnb.ap, op=ALU.add),
                      reads=[g_, lnb], writes=[vln[tt]])
            self.dbg("vln0", vln[0]); self.dbg("rv", rv)
            self.ck(9)
            for c in range(KC):
                pa = psA[ia % 3]
                ia += 1
                fns = [lambda e, kc=kc, c=c, pa=pa: e.matmul(pa.ap, lhsT=self.Win[:, kc, 2048 + c * 128:2048 + (c + 1) * 128],
                                                             rhs=xnT[:, kc, :], start=(kc == 0), stop=(kc == KC - 1))
                       for kc in range(KC)]
                sc.pe_group(fns, reads=[xnT, self.Win], writes=[pa])
                sc.op("act", lambda e, c=c, pa=pa: e.activation(out=szT[:, c, :], in_=pa.ap, func=AF.Silu,
                                                                bias=self.biascol[:, 16 + c:17 + c]),
                      reads=[pa, self.biascol], writes=[szT])
            self.ck(10)
            for c in range(KC):
                g = c // 2
                pa = psA[ia % 3]
                ia += 1
                fns = [lambda e, c=c, g=g, tt=tt, pa=pa: e.matmul(pa[:, tt * 128:(tt + 1) * 128], lhsT=vln[tt][:, c * 128:(c + 1) * 128],
                                                                  rhs=WcT[:, g, :], start=True, stop=True) for tt in range(4)]
                sc.pe_group(fns, reads=vln + [WcT], writes=[pa])
                m_ = mt[c % 2]
                for tt in range(4):
                    sc.op("dve", lambda e, pa=pa, m_=m_, g=g, tt=tt: e.tensor_tensor(
                        out=m_[:, tt * 128:(tt + 1) * 128], in0=pa[:, tt * 128:(tt + 1) * 128], in1=bsb[:, g, :], op=ALU.add),
                        reads=[pa, bsb], writes=[m_])
                sc.op("dve", lambda e, m_=m_, c=c: e.tensor_tensor(out=m_.ap, in0=m_.ap, in1=uT[:, c, :], op=ALU.mult),
                      reads=[m_, uT], writes=[m_])
                sc.op("pool", lambda e, m_=m_, c=c: e.tensor_tensor(out=ogT[:, c, :], in0=m_.ap, in1=szT[:, c, :], op=ALU.mult),
                      reads=[m_, szT], writes=[ogT])
            self.dbg("szT", szT); self.dbg("ogT", ogT)
            self.ck(11)
            self.stageO_full(j, ogT, x_in, x_out, psW)


def col8(v):
    return np.ascontiguousarray(v.reshape(-1, 128).T)


def rep(v, n=128):
    return np.ascontiguousarray(np.broadcast_to(v.reshape(1, -1), (n, v.size)))


def make_in_maps(inputs, layer_ids, S, n_cores):
    f = lambda a: np.ascontiguousarray(np.asarray(a, dtype=np.float32))
    x = f(inputs["x"])
    c = f(inputs["c"])
    shared = {"ident": np.eye(128, dtype=np.float32),
              "tri": np.triu(np.ones((128, 128), dtype=np.float32)),
              "uneg": np.triu(np.full((128, 128), -1.0 / 16, dtype=np.float32)),
              "negmask": np.tril(np.full((128, 128), -30000.0, dtype=np.float32), -1),
              "blockones": np.kron(np.eye(2, dtype=np.float32), np.ones((64, 64), dtype=np.float32)),
              "sel": np.concatenate([np.zeros((64, 64), np.float32), np.ones((1, 64), np.float32)], 0)}
    for L in layer_ids:
        kind, jj = L % 3, L // 3
        p = "L%d_" % L
        bm = f(inputs["b_mod"][L])
        shared[p + "wmod"] = f(inputs["w_mod"][L])
        shared[p + "bmodc"] = col8(bm)
        shared[p + "bmodg"] = rep(bm[2048:3072])
        shared[p + "gprec"] = col8(f(inputs["norm_pre_g"][L]))
        shared[p + "gpost"] = rep(f(inputs["norm_post_g"][L]))
        if kind == 0:
            shared[p + "win"] = f(inputs["gla_w_in"][jj])
            shared[p + "wout"] = f(inputs["gla_w_out"][jj])
            shared[p + "wa2"] = np.ascontiguousarray(np.concatenate([f(inputs["gla_w_a2"][jj]), f(inputs["gla_b_a"][jj])[None]], 0))
            shared[p + "ghc"] = col8(f(inputs["gla_g_head"][jj]).reshape(-1))
        if kind == 2:
            shared[p + "win"] = f(inputs["fox_w_in"][jj])
            shared[p + "wout"] = f(inputs["fox_w_out"][jj])
            shared[p + "gqk"] = np.ascontiguousarray(np.stack([np.tile(f(inputs["fox_g_q"][jj]), 2), np.tile(f(inputs["fox_g_k"][jj]), 2)], 1))
            shared[p + "bf"] = np.ascontiguousarray(f(inputs["fox_b_f"][jj])[:, None])
        if kind == 1:
            shared[p + "win"] = f(inputs["sgu_w_in"][jj])
            shared[p + "wout"] = f(inputs["sgu_w_out"][jj])
            shared[p + "lng"] = rep(f(inputs["sgu_ln_g"][jj]))
            shared[p + "lnb"] = rep(f(inputs["sgu_ln_b"][jj]))
            shared[p + "ws"] = np.ascontiguousarray(f(inputs["sgu_w_s"][jj]).transpose(1, 0, 2))
            shared[p + "bs"] = np.ascontiguousarray(np.broadcast_to(f(inputs["sgu_b_s"][jj])[None], (128, 4, 128)))
    maps = []
    for b in range(n_cores):
        m = dict(shared)
        m["x"] = np.ascontiguousarray(x[b, :S])
        m["ccol"] = col8(c[b])
        maps.append(m)
    return maps


_PROG_CACHE = {}


def run(inputs, layer_ids=(0, 1, 2, 3), S=8192, n_cores=N_CORES):
    key = (S, tuple(layer_ids))
    if key not in _PROG_CACHE:
        pr = Prog(S, layer_ids)
        pr.build()
        _PROG_CACHE[key] = pr
    pr = _PROG_CACHE[key]
    maps = make_in_maps(inputs, layer_ids, S, n_cores)
    res = run_bass_kernel_spmd(pr.nc, maps, core_ids=list(range(n_cores)))
    return np.stack([np.asarray(r["out"]) for r in res.results], axis=0)


def kernel(**inputs):
    return run(inputs).astype(np.float32)
```
